# Optimizing a Trainium2 kernel written in Bass

```python
import math
import jax, jax.numpy as jnp
from jax import lax
import numpy as np

D_MODEL = 1024
BATCH = 4
SEQ = 4096
DEPTH = 4
DEC_BATCH = 2
DEC_SEQ = 8192
PAST_LEN = 128

GRID_W = 64
MAX_WIN_H = 8
WIN_W = 16
NA_HEADS = 8
NA_HEAD_DIM = 64
NA_WIDTH = NA_HEADS * NA_HEAD_DIM
DIFF_HEADS = 4
DIFF_QK_DIM = 64
DIFF_V_DIM = 2 * DIFF_QK_DIM
DIFF_WIDTH = DIFF_HEADS * DIFF_V_DIM
MIX_WIDTH = NA_WIDTH + DIFF_WIDTH
IN_WIDTH = 3 * NA_WIDTH + DIFF_HEADS * 2 * DIFF_QK_DIM * 2 + DIFF_WIDTH
D_FF = 2816
CONV_W = 3
ROPE_THETA = 10000.0
Q_BLOCK = 128
EPS = 1e-6

kernel_name = "hybrid_natten_diffattn_encoder"


def rms_norm(x, g):
    x32 = x.astype(jnp.float32)
    y = x32 * lax.rsqrt(jnp.mean(x32 * x32, axis=-1, keepdims=True) + EPS)
    return (y * g.astype(jnp.float32)).astype(x.dtype)


def rotary(x, T):
    d = x.shape[-1]
    inv_freq = 1.0 / (ROPE_THETA ** (jnp.arange(0, d, 2, dtype=jnp.float32) / d))
    ang = jnp.arange(T, dtype=jnp.float32)[:, None] * inv_freq[None, :]
    cos = jnp.concatenate([jnp.cos(ang), jnp.cos(ang)], -1).astype(x.dtype)[None, :, None, None, :]
    sin = jnp.concatenate([jnp.sin(ang), jnp.sin(ang)], -1).astype(x.dtype)[None, :, None, None, :]
    x1, x2 = x[..., : d // 2], x[..., d // 2:]
    rot = jnp.concatenate([-x2, x1], -1)
    return x * cos + rot * sin


def neighbourhood_attention(q, k, v, rpb):
    B, T, H, dh = q.shape
    rows = T // GRID_W
    win_h = min(MAX_WIN_H, rows)
    scale = 1.0 / math.sqrt(dh)
    qg = q.reshape(B, rows, GRID_W, H, dh).transpose(1, 0, 3, 2, 4)
    kg = k.reshape(B, rows, GRID_W, H, dh).transpose(0, 3, 1, 2, 4)
    vg = v.reshape(B, rows, GRID_W, H, dh).transpose(0, 3, 1, 2, 4)
    cols = np.arange(GRID_W)
    col_start = np.clip(cols - WIN_W // 2, 0, GRID_W - WIN_W)
    col_idx = col_start[:, None] + np.arange(WIN_W)[None, :]
    dc_idx = (col_idx - cols[:, None]) + (WIN_W - 1)
    rpb32 = rpb.astype(jnp.float32)

    def one_row(args):
        r, q_row = args
        start = jnp.clip(r - win_h // 2, 0, rows - win_h)
        k_band = lax.dynamic_slice_in_dim(kg, start, win_h, axis=2)
        v_band = lax.dynamic_slice_in_dim(vg, start, win_h, axis=2)
        k_win = k_band[:, :, :, col_idx, :]
        v_win = v_band[:, :, :, col_idx, :]
        dr_idx = start + jnp.arange(win_h) - r + (MAX_WIN_H - 1)
        bias = rpb32[:, dr_idx[None, :, None], dc_idx[:, None, :]]
        s = jnp.einsum('bhcd,bhwcjd->bhcwj', q_row, k_win).astype(jnp.float32) * scale + bias[None]
        p = jax.nn.softmax(s.reshape(B, H, GRID_W, win_h * WIN_W), axis=-1)
        p = p.reshape(B, H, GRID_W, win_h, WIN_W).astype(v.dtype)
        return jnp.einsum('bhcwj,bhwcjd->bhcd', p, v_win)

    out = lax.map(one_row, (jnp.arange(rows), qg))
    return out.transpose(1, 0, 3, 2, 4).reshape(B, T, H * dh)


def diff_attention(q, k, v, lam, subln_g, lam_init):
    B, T, H, _, d = q.shape
    dv = v.shape[-1]
    scale = 1.0 / math.sqrt(d)
    nb = T // Q_BLOCK
    k1 = k[..., 0, :].transpose(0, 2, 1, 3)
    k2 = k[..., 1, :].transpose(0, 2, 1, 3)
    vh = v.transpose(0, 2, 1, 3)
    qb = q.transpose(0, 2, 1, 3, 4).reshape(B, H, nb, Q_BLOCK, 2, d).transpose(2, 0, 1, 3, 4, 5)

    def one_block(qblk):
        s1 = jnp.einsum('bhqd,bhkd->bhqk', qblk[..., 0, :], k1).astype(jnp.float32) * scale
        s2 = jnp.einsum('bhqd,bhkd->bhqk', qblk[..., 1, :], k2).astype(jnp.float32) * scale
        w = jax.nn.softmax(s1, axis=-1) - lam * jax.nn.softmax(s2, axis=-1)
        return jnp.einsum('bhqk,bhkv->bhqv', w.astype(v.dtype), vh)

    out = lax.map(one_block, qb)
    out = out.transpose(1, 0, 3, 2, 4).reshape(B, T, H, dv)
    out = rms_norm(out, subln_g) * (1.0 - lam_init)
    return out.reshape(B, T, H * dv)


def dwconv_centred(h, w, b):
    hp = jnp.pad(h, ((0, 0), (1, 1), (0, 0)))
    return hp[:, :-2] * w[0] + hp[:, 1:-1] * w[1] + hp[:, 2:] * w[2] + b


def trunk(x, g_attn, w_in, rpb, lam_q1, lam_k1, lam_q2, lam_k2, subln_g, w_out,
          g_ffn, w_up, conv_w, conv_b, w_down, g_final):
    B, T, _ = x.shape
    o_qa, o_ka, o_va = 0, NA_WIDTH, 2 * NA_WIDTH
    o_qb = 3 * NA_WIDTH
    qk_b = DIFF_HEADS * 2 * DIFF_QK_DIM
    o_kb = o_qb + qk_b
    o_vb = o_kb + qk_b
    for l in range(DEPTH):
        n = rms_norm(x, g_attn[l])
        proj = n @ w_in[l]
        qa = proj[..., o_qa:o_ka].reshape(B, T, NA_HEADS, NA_HEAD_DIM)
        ka = proj[..., o_ka:o_va].reshape(B, T, NA_HEADS, NA_HEAD_DIM)
        va = proj[..., o_va:o_qb].reshape(B, T, NA_HEADS, NA_HEAD_DIM)
        qb = proj[..., o_qb:o_kb].reshape(B, T, DIFF_HEADS, 2, DIFF_QK_DIM)
        kb = proj[..., o_kb:o_vb].reshape(B, T, DIFF_HEADS, 2, DIFF_QK_DIM)
        vb = proj[..., o_vb:].reshape(B, T, DIFF_HEADS, DIFF_V_DIM)
        qb = rotary(qb, T)
        kb = rotary(kb, T)
        lam_init = 0.8 - 0.6 * math.exp(-0.3 * l)
        lam = (jnp.exp(jnp.sum(lam_q1[l].astype(jnp.float32) * lam_k1[l].astype(jnp.float32)))
               - jnp.exp(jnp.sum(lam_q2[l].astype(jnp.float32) * lam_k2[l].astype(jnp.float32)))
               + lam_init)
        ya = neighbourhood_attention(qa, ka, va, rpb[l])
        yb = diff_attention(qb, kb, vb, lam, subln_g[l], lam_init)
        x = x + jnp.concatenate([ya, yb], axis=-1) @ w_out[l]
        n = rms_norm(x, g_ffn[l])
        u = n @ w_up[l]
        gate, val = u[..., :D_FF], u[..., D_FF:]
        gate = dwconv_centred(gate, conv_w[l], conv_b[l])
        x = x + (jax.nn.gelu(gate, approximate=False) * val) @ w_down[l]
    return rms_norm(x, g_final)


def setup_inputs(seed: int = 0) -> dict:
    key = jax.random.key(seed)
    ks = jax.random.split(key, 20)
    f32 = jnp.float32
    nrm = lambda k, s, sc: jax.random.normal(k, s, f32) * sc
    return {
        "x_prompt": nrm(ks[0], (BATCH, SEQ, D_MODEL), 1.0),
        "x_sample": nrm(ks[1], (DEC_BATCH, DEC_SEQ, D_MODEL), 1.0),
        "g_attn": 1.0 + nrm(ks[2], (DEPTH, D_MODEL), 0.02),
        "w_in": nrm(ks[3], (DEPTH, D_MODEL, IN_WIDTH), D_MODEL ** -0.5),
        "rpb": nrm(ks[4], (DEPTH, NA_HEADS, 2 * MAX_WIN_H - 1, 2 * WIN_W - 1), 0.1),
        "lam_q1": nrm(ks[5], (DEPTH, DIFF_QK_DIM), 0.1),
        "lam_k1": nrm(ks[6], (DEPTH, DIFF_QK_DIM), 0.1),
        "lam_q2": nrm(ks[7], (DEPTH, DIFF_QK_DIM), 0.1),
        "lam_k2": nrm(ks[8], (DEPTH, DIFF_QK_DIM), 0.1),
        "subln_g": 1.0 + nrm(ks[9], (DEPTH, DIFF_V_DIM), 0.02),
        "w_out": nrm(ks[10], (DEPTH, MIX_WIDTH, D_MODEL), MIX_WIDTH ** -0.5),
        "g_ffn": 1.0 + nrm(ks[11], (DEPTH, D_MODEL), 0.02),
        "w_up": nrm(ks[12], (DEPTH, D_MODEL, 2 * D_FF), D_MODEL ** -0.5),
        "conv_w": nrm(ks[13], (DEPTH, CONV_W, D_FF), CONV_W ** -0.5),
        "conv_b": nrm(ks[14], (DEPTH, D_FF), 0.01),
        "w_down": nrm(ks[15], (DEPTH, D_FF, D_MODEL), D_FF ** -0.5),
        "g_final": 1.0 + nrm(ks[16], (D_MODEL,), 0.02),
    }


def reference(x_prompt, x_sample, g_attn, w_in, rpb, lam_q1, lam_k1, lam_q2, lam_k2,
              subln_g, w_out, g_ffn, w_up, conv_w, conv_b, w_down, g_final):
    y_prompt = trunk(x_prompt, g_attn, w_in, rpb, lam_q1, lam_k1, lam_q2, lam_k2, subln_g,
                     w_out, g_ffn, w_up, conv_w, conv_b, w_down, g_final)
    y_sample = trunk(x_sample, g_attn, w_in, rpb, lam_q1, lam_k1, lam_q2, lam_k2, subln_g,
                     w_out, g_ffn, w_up, conv_w, conv_b, w_down, g_final)
    return (y_prompt, y_sample)
```

```python
import math
from contextlib import ExitStack

import numpy as np
import ml_dtypes

import concourse.bass as bass
import concourse.mybir as mybir
from concourse.bass_utils import run_bass_kernel_spmd

F32 = mybir.dt.float32
BF16 = mybir.dt.bfloat16
AF = mybir.ActivationFunctionType
ALU = mybir.AluOpType

D = 1024
NAH = 8
DFF = 2816
NFC = DFF // 128
EPS = 1e-6
NEG = -30000.0
W_IN_EXT = 4096
SAME_ENGINE_SYNC = True


class Buf:
    def __init__(self, name, t):
        self.name = name
        self.t = t


class Ring:
    def __init__(self, bufs):
        self.bufs = bufs
        self.i = 0

    def next(self):
        b = self.bufs[self.i % len(self.bufs)]
        self.i += 1
        return b


class _Rec:
    def __getattr__(self, name):
        def f(*a, **k):
            self.call = (name, a, k)
            return self
        return f


class Sched:
    ENGS = ("pe", "act", "dve", "pool", "sp")

    def __init__(self, nc, es):
        self.nc = nc
        self.es = es
        self.ops = []
        self.state = {}
        self.dma_sems = {}
        self.dma_cnt = {}
        self.eng_sems = None
        self.eng_cnt = None

    def new_eng_sems(self, tag):
        self.eng_sems = {e: self.es.enter_context(self.nc.semaphore(f"s_{tag}_{e}")) for e in self.ENGS}
        self.eng_cnt = {e: 0 for e in self.ENGS}

    def I(self, eng, meth, reads=(), writes=(), dma_key=None, **kw):
        return self.op(eng, (meth, (), kw), reads, writes, dma_key)

    def op(self, eng, fn, reads=(), writes=(), dma_key=None):
        if callable(fn):
            rec = _Rec()
            fn(rec)
            fn = rec.call
        oid = len(self.ops)
        deps = set()
        for r in reads:
            st = self.state.setdefault(r.name, {"w": {}, "r": {}})
            deps.update(st["w"].values())
        for w in writes:
            st = self.state.setdefault(w.name, {"w": {}, "r": {}})
            deps.update(st["w"].values())
            deps.update(st["r"].values())
        evkey = ("dma", dma_key) if dma_key is not None else eng
        for r in reads:
            self.state[r.name]["r"][evkey] = oid
        for w in writes:
            st = self.state[w.name]
            st["r"] = {}
            st["w"] = {evkey: oid}
        self.ops.append(dict(eng=eng, fn=fn, deps=deps, dma_key=dma_key, sig=False, val=None))
        return oid

    def dma(self, eng, out, in_, reads=(), writes=(), key=None, **kw):
        assert key is not None
        return self.op(eng, ("dma_start", (), dict(out=out, in_=in_, **kw)), reads, writes, dma_key=key.split("|")[-1])

    def flush(self):
        nc = self.nc
        ops = self.ops
        for o in ops:
            need = set()
            for d in o["deps"]:
                p = ops[d]
                if p["dma_key"] is not None:
                    need.add(d)
                elif p["eng"] != o["eng"]:
                    need.add(d)
                    p["sig"] = True
                elif SAME_ENGINE_SYNC and p["eng"] in ("act", "dve", "pool"):
                    need.add(d)
                    p["sig"] = True
            o["need"] = need
        for o in ops:
            if o["dma_key"] is not None:
                k = o["dma_key"]
                if k not in self.dma_sems:
                    self.dma_sems[k] = self.es.enter_context(nc.semaphore(f"d_{k}"))
                    self.dma_cnt[k] = 0
                self.dma_cnt[k] += 16
                o["val"] = (self.dma_sems[k], self.dma_cnt[k])
            elif o["sig"]:
                self.eng_cnt[o["eng"]] += 1
                o["val"] = (self.eng_sems[o["eng"]], self.eng_cnt[o["eng"]])
        per = {e: [] for e in self.ENGS}
        for o in ops:
            per[o["eng"]].append(o)

        def emit(eng_name, lst):
            def body(e):
                seen = {}
                issued = {}
                for o in lst:
                    waits = {}
                    for d in o["need"]:
                        sem, v = ops[d]["val"]
                        kk = id(sem)
                        if seen.get(kk, 0) >= v:
                            continue
                        if kk not in waits or waits[kk][1] < v:
                            waits[kk] = (sem, v)
                    for kk, (sem, v) in waits.items():
                        e.wait_ge(sem, v)
                        seen[kk] = v
                    meth, ar, kw = o["fn"]
                    ins = getattr(e, meth)(*ar, **kw)
                    if o["dma_key"] is not None:
                        sem, v = o["val"]
                        ins.then_inc(sem, 16)
                        issued[id(sem)] = (sem, v)
                    elif o["sig"]:
                        ins.then_inc(o["val"][0], 1)
                for kk, (sem, v) in issued.items():
                    e.wait_ge(sem, v)
            return body

        with nc.Block() as block:
            if per["sp"]:
                block.sync(emit("sp", per["sp"]))
            if per["pe"]:
                block.tensor(emit("pe", per["pe"]))
            if per["act"]:
                block.scalar(emit("act", per["act"]))
            if per["dve"]:
                block.vector(emit("dve", per["dve"]))
            if per["pool"]:
                block.gpsimd(emit("pool", per["pool"]))
        self.ops = []
        self.state = {}


def build_program(T, NL, n_types_dummy=None):
    nc = bass.Bass("TRN2", target_bir_lowering=False)
    NT = T // 128
    NB = T // 512
    ROWS = T // 64
    NG = NB
    SG = sorted(set([0, NG // 2 - 1, NG // 2, NG - 1]))

    def din(name, shape, dt=F32):
        return nc.dram_tensor(name, list(shape), dt, kind="ExternalInput").ap()

    x_in = din("x", [T, D])
    w_in = din("w_in", [NL, D, W_IN_EXT])
    w_out = din("w_out", [NL, D, D])
    w_up = din("w_up", [NL, D, 2 * DFF])
    w_down = din("w_down", [NL, DFF, D])
    g_attn = din("g_attn", [NL, 128, D])
    g_ffn = din("g_ffn", [NL, 128, D])
    g_final = din("g_final", [128, D])
    subln = din("subln", [NL, 128, 128])
    lamv = din("lamv", [NL, 128, 256])
    convw = din("convw", [NL, 128, 4 * NFC])
    rpbt = din("rpbt", [NL, NAH, 15, 64, 64])
    rot = din("rot", [2, 128, T])
    rowmask = din("rowmask", [len(SG) + 1, 8, 128, 512], BF16)
    dmask = din("dmask", [128, 2 * (T // 128)])
    cmask = din("cmask", [128, 2])
    ident_in = din("ident", [128, 128], BF16)
    y_out = nc.dram_tensor("y_out", [T, D], F32, kind="ExternalOutput").ap()

    def dscr(name, shape, dt):
        return nc.dram_tensor(name, list(shape), dt).ap()

    qaT = dscr("qaT", [512, T], BF16)
    kaT = dscr("kaT", [512, T], BF16)
    va = dscr("va", [T, 8 * 65], BF16)
    qbT = dscr("qbT", [512, T], BF16)
    kbT = dscr("kbT", [512, T], BF16)
    vb = dscr("vb", [T, 4 * 129], BF16)
    ycat = dscr("ycat", [T, D], BF16)
    xmid = dscr("xmid", [T, D], F32)
    xres = dscr("xres", [T, D], F32)
    n2T = dscr("n2T", [D, T + 2], BF16)

    es = ExitStack()
    S = Sched(nc, es)

    cur = {"tag": "pre"}

    def sb(st, name, shape, dt):
        full = f"{cur['tag']}|{name}"
        return Buf(full, st.enter_context(nc.sbuf_tensor(full.replace("|", "_"), list(shape), dt)))

    def ps(st, name, shape, dt):
        full = f"{cur['tag']}|{name}"
        return Buf(full, st.enter_context(nc.psum_tensor(full.replace("|", "_"), list(shape), dt)))

    def ring(st, name, n, shape, dt, psum=False):
        f = ps if psum else sb
        return Ring([f(st, f"{name}{i}", shape, dt) for i in range(n)])

    ident = sb(es, "ident", [128, 128], BF16)
    lam_t = sb(es, "lam_t", [128, 8], F32)
    dmask_t = sb(es, "dmask_t", [128, 2 * NT], F32)
    cmask_t = sb(es, "cmask_t", [128, 2], F32)

    def load_w(st_ring, dstbuf, K, N, src, c0=0, CH=1024):
        for k in range(K):
            for n0 in range(0, N, CH):
                w = min(CH, N - n0)
                stg = st_ring.next()
                S.dma("sp", stg.t[:, 0:w], src[k * 128:(k + 1) * 128, c0 + n0:c0 + n0 + w], writes=[stg], key=stg.name)
                S.op("pool", lambda e, stg=stg, k=k, n0=n0, w=w: e.tensor_copy(out=dstbuf.t[:, k, n0:n0 + w], in_=stg.t[:, 0:w]),
                     reads=[stg], writes=[dstbuf])

    def rmsnorm_to_T(st, xt, g_t, nT_dst, col0, rings):
        junk, ssr, nbr, tpr = rings
        jk = junk.next()
        ss = ssr.next()
        S.op("dve", lambda e: e.scalar_tensor_tensor(out=jk.t[:], in0=xt.t[:], scalar=1.0, in1=xt.t[:], op0=ALU.mult, op1=ALU.mult,
                                                     accum_out=ss.t[:, 0:1]), reads=[xt], writes=[jk, ss])
        S.op("dve", lambda e: e.tensor_scalar(out=ss.t[:, 1:2], in0=ss.t[:, 0:1], scalar1=1.0 / D, scalar2=EPS, op0=ALU.mult, op1=ALU.add),
             reads=[ss], writes=[ss])
        S.op("act", lambda e: e.activation(out=ss.t[:, 3:4], in_=ss.t[:, 1:2], func=AF.Ln), reads=[ss], writes=[ss])
        S.op("act", lambda e: e.activation(out=ss.t[:, 2:3], in_=ss.t[:, 3:4], func=AF.Exp, scale=-0.5), reads=[ss], writes=[ss])
        nb = nbr.next()
        S.op("dve", lambda e: e.scalar_tensor_tensor(out=nb.t[:], in0=xt.t[:], scalar=ss.t[:, 2:3], in1=g_t.t[:], op0=ALU.mult, op1=ALU.mult),
             reads=[xt, ss, g_t], writes=[nb])
        tp = tpr.next()
        for c in range(8):
            S.op("pe", lambda e, c=c: e.transpose(out=tp.t[:, c, :], in_=nb.t[:, c * 128:(c + 1) * 128], identity=ident.t[:]),
                 reads=[nb, ident], writes=[tp])
        S.op("act", lambda e: e.activation(out=nT_dst.t[:, 0:8, col0:col0 + 128], in_=tp.t[:], func=AF.Copy),
             reads=[tp], writes=[nT_dst])

    S.new_eng_sems("pre")
    zt = sb(es, "zt", [128, 8], BF16)
    S.dma("sp", ident.t[:], ident_in[:, :], writes=[ident], key="c_ident")
    S.dma("sp", dmask_t.t[:], dmask[:, :], writes=[dmask_t], key="c_dmask")
    S.dma("sp", cmask_t.t[:], cmask[:, :], writes=[cmask_t], key="c_cmask")
    S.op("dve", lambda e: e.memset(zt.t[:], 0.0), writes=[zt])
    n2T_v = n2T.rearrange("(c p) t -> p c t", p=128)
    S.dma("sp", n2T_v[:, :, 0:1], zt.t[:, 0:8].rearrange("p (c o) -> p c o", o=1), reads=[zt], key="c_z0", allow_slow_non_contiguous=True)
    S.dma("sp", n2T_v[:, :, T + 1:T + 2], zt.t[:, 0:8].rearrange("p (c o) -> p c o", o=1), reads=[zt], key="c_z1", allow_slow_non_contiguous=True)
    S.flush()

    for l in range(NL):
        S.new_eng_sems(f"L{l}")
        cur["tag"] = f"L{l}"
        x_src = x_in if l == 0 else xres
        lam_init = 0.8 - 0.6 * math.exp(-0.3 * l)
        with ExitStack() as st:
            W = sb(st, "A_W", [128, 8, W_IN_EXT], BF16)
            stg_r = ring(st, "A_stg", 2, [128, 1024], F32)
            g_t = sb(st, "A_g", [128, D], F32)
            x_r = ring(st, "A_x", 2, [128, D], F32)
            junk_r = ring(st, "A_jk", 1, [128, D], BF16)
            ss_r = ring(st, "A_ss", 2, [128, 4], F32)
            nb_r = ring(st, "A_nb", 2, [128, D], BF16)
            tp_r = ring(st, "A_tp", 2, [128, 8, 128], BF16, psum=True)
            nT_r = ring(st, "A_nT", 2, [128, 8, 512], BF16)
            pj_r = ring(st, "A_pj", 4, [128, 512], F32, psum=True)
            pv_r = ring(st, "A_pv", 2, [128, 512], F32, psum=True)
            fo_r = ring(st, "A_fo", 3, [128, 512], BF16)
            rot_r = ring(st, "A_rot", 2, [128, 2, 512], F32)
            t1_r = ring(st, "A_t1", 2, [128, 512], F32)
            t2_r = ring(st, "A_t2", 2, [128, 512], F32)
            va_r = ring(st, "A_va", 2, [128, 8, 65], BF16)
            vb_r = ring(st, "A_vb", 2, [128, 4, 129], BF16)
            lv = sb(st, "A_lv", [128, 256], F32)
            lj = sb(st, "A_lj", [128, 64], F32)

            S.dma("sp", g_t.t[:], g_attn[l], writes=[g_t], key="A_g")
            S.dma("sp", lv.t[:], lamv[l], writes=[lv], key="A_lv")
            S.op("dve", lambda e: e.scalar_tensor_tensor(out=lj.t[:], in0=lv.t[:, 0:64], scalar=1.0, in1=lv.t[:, 64:128], op0=ALU.mult, op1=ALU.mult,
                                                         accum_out=lam_t.t[:, 1:2]), reads=[lv], writes=[lj, lam_t])
            S.op("dve", lambda e: e.scalar_tensor_tensor(out=lj.t[:], in0=lv.t[:, 128:192], scalar=1.0, in1=lv.t[:, 192:256], op0=ALU.mult, op1=ALU.mult,
                                                         accum_out=lam_t.t[:, 2:3]), reads=[lv], writes=[lj, lam_t])
            S.op("act", lambda e: e.activation(out=lam_t.t[:, 3:5], in_=lam_t.t[:, 1:3], func=AF.Exp), reads=[lam_t], writes=[lam_t])
            S.op("dve", lambda e: e.tensor_tensor(out=lam_t.t[:, 5:6], in0=lam_t.t[:, 4:5], in1=lam_t.t[:, 3:4], op=ALU.subtract), reads=[lam_t], writes=[lam_t])
            S.op("dve", lambda e: e.tensor_scalar(out=lam_t.t[:, 0:1], in0=lam_t.t[:, 5:6], scalar1=-lam_init, scalar2=None, op0=ALU.add), reads=[lam_t], writes=[lam_t])
            for bi in range(len(va_r.bufs)):
                b = va_r.bufs[bi]
                S.op("pool", lambda e, b=b: e.memset(b.t[:], 1.0), writes=[b])
                b = vb_r.bufs[bi]
                S.op("pool", lambda e, b=b: e.memset(b.t[:], 1.0), writes=[b])
            load_w(stg_r, W, 8, W_IN_EXT, w_in[l])

            for blk in range(NB):
                t0 = blk * 512
                nT = nT_r.next()
                for tt in range(4):
                    xt = x_r.next()
                    r0 = t0 + tt * 128
                    S.dma("sp", xt.t[:], x_src[r0:r0 + 128, :], writes=[xt], key=xt.name)
                    rmsnorm_to_T(st, xt, g_t, nT, tt * 128, (junk_r, ss_r, nb_r, tp_r))
                rt = rot_r.next()
                S.dma("sp", rt.t[:], rot[:, :, t0:t0 + 512].rearrange("c p t -> p c t"), writes=[rt], key=rt.name)

                def proj(ocol):
                    p = pj_r.next()
                    for k in range(8):
                        S.op("pe", lambda e, k=k, p=p: e.matmul(p.t[:], lhsT=W.t[:, k, ocol:ocol + 128], rhs=nT.t[:, k, :], start=(k == 0), stop=(k == 7)),
                             reads=[W, nT], writes=[p])
                    return p
                for which, dst in ((0, qaT), (1, kaT)):
                    for c in range(4):
                        p = proj(which * 512 + c * 128)
                        fo = fo_r.next()
                        S.op("act", lambda e, p=p, fo=fo: e.activation(out=fo.t[:], in_=p.t[:], func=AF.Copy), reads=[p], writes=[fo])
                        S.dma("pool", dst[c * 128:(c + 1) * 128, t0:t0 + 512], fo.t[:], reads=[fo], key=fo.name)
                for which, dst in ((0, qbT), (1, kbT)):
                    for h in range(4):
                        p1 = proj(1536 + which * 512 + h * 128)
                        p2 = proj(3072 + which * 512 + h * 128)
                        t1 = t1_r.next()
                        t2 = t2_r.next()
                        fo = fo_r.next()
                        S.op("dve", lambda e, p1=p1, t1=t1: e.tensor_tensor(out=t1.t[:], in0=p1.t[:], in1=rt.t[:, 0, :], op=ALU.mult), reads=[p1, rt], writes=[t1])
                        S.op("dve", lambda e, p2=p2, t2=t2: e.tensor_tensor(out=t2.t[:], in0=p2.t[:], in1=rt.t[:, 1, :], op=ALU.mult), reads=[p2, rt], writes=[t2])
                        S.op("pool", lambda e, t1=t1, t2=t2, fo=fo: e.tensor_tensor(out=fo.t[:], in0=t1.t[:], in1=t2.t[:], op=ALU.add), reads=[t1, t2], writes=[fo])
                        S.dma("pool", dst[h * 128:(h + 1) * 128, t0:t0 + 512], fo.t[:], reads=[fo], key=fo.name)
                for tt in range(4):
                    r0 = t0 + tt * 128
                    for which in range(2):
                        pv = pv_r.next()
                        oc = 1024 if which == 0 else 2560
                        for k in range(8):
                            S.op("pe", lambda e, k=k, pv=pv, oc=oc: e.matmul(pv.t[:], lhsT=nT.t[:, k, tt * 128:(tt + 1) * 128], rhs=W.t[:, k, oc:oc + 512],
                                                                              start=(k == 0), stop=(k == 7)), reads=[W, nT], writes=[pv])
                        if which == 0:
                            vt = va_r.next()
                            S.op("act", lambda e, pv=pv, vt=vt: e.activation(out=vt.t[:, :, 0:64], in_=pv.t[:].rearrange("p (h d) -> p h d", d=64), func=AF.Copy),
                                 reads=[pv], writes=[vt])
                            S.dma("pool", va[r0:r0 + 128, :], vt.t[:].rearrange("p h d -> p (h d)"), reads=[vt], key=vt.name)
                        else:
                            vt = vb_r.next()
                            S.op("act", lambda e, pv=pv, vt=vt: e.activation(out=vt.t[:, :, 0:128], in_=pv.t[:].rearrange("p (h d) -> p h d", d=128), func=AF.Copy),
                                 reads=[pv], writes=[vt])
                            S.dma("pool", vb[r0:r0 + 128, :], vt.t[:].rearrange("p h d -> p (h d)"), reads=[vt], key=vt.name)
            S.flush()

        with ExitStack() as st:
            Tf = sb(st, "N_Tf", [128, NAH, 24, 64], BF16)
            rstg = ring(st, "N_rs", 2, [128, 15, 64], F32)
            rm_int = sb(st, "N_rmi", [128, 8, 512], BF16)
            rm_r = ring(st, "N_rm", 2, [128, 8, 512], BF16)
            kT_r = ring(st, "N_kT", 2, [128, 4, 1024], BF16)
            qT_r = ring(st, "N_qT", 2, [128, 4, 512], BF16)
            v_r = ring(st, "N_v", 2, [128, 8, 520], BF16)
            s_r = ring(st, "N_s", 4, [128, 512], F32, psum=True)
            e_r = ring(st, "N_e", 3, [128, 1024], BF16)
            p_r = ring(st, "N_p", 3, [128, 1024], BF16)
            TMi = sb(st, "N_TMi", [128, 8, NAH, 512], BF16)
            acc_r = ring(st, "N_acc", 4, [128, 512], F32, psum=True)
            av = lambda b_: b_.t[:, 0:260].rearrange("p (q c) -> p q c", c=65)
            rc_r = ring(st, "N_rc", 2, [128, 4], F32)
            ya_r = ring(st, "N_ya", 2, [128, 4, 512], BF16)

            S.op("dve", lambda e: e.memset(Tf.t[:], 0.0), writes=[Tf])
            for h in range(NAH):
                for a in range(2):
                    rs = rstg.next()
                    S.dma("sp", rs.t[a * 64:(a + 1) * 64], rpbt[l, h].rearrange("r k q -> k r q"), writes=[rs], key=rs.name)
                    S.op("act", lambda e, rs=rs, h=h, a=a: e.activation(out=Tf.t[a * 64:(a + 1) * 64, h, 4 + a:19 + a, :], in_=rs.t[a * 64:(a + 1) * 64],
                                                                         func=AF.Exp), reads=[rs], writes=[Tf])
            S.dma("sp", rm_int.t[:], rowmask[0].rearrange("j p q -> p j q"), writes=[rm_int], key="N_rmi")
            for j in range(8):
                for h in range(NAH):
                    s0 = 15 - 2 * j
                    S.op("dve", lambda e: e.tensor_tensor(out=TMi.t[:, j, h, :].rearrange("p (b q) -> p b q", q=64), in0=Tf.t[:, h, s0:s0 + 8, :],
                                                          in1=rm_int.t[:, j, :].rearrange("p (b q) -> p b q", q=64), op=ALU.mult), reads=[Tf, rm_int], writes=[TMi])

            for g in range(NG):
                jl = [j for j in range(8) if 0 <= 8 * g - 4 + 2 * j and 8 * g - 4 + 2 * j + 1 < ROWS]
                jlo, jhi = jl[0], jl[-1] + 1
                k0 = (8 * g - 4 + 2 * jlo) * 64
                nk = (jhi - jlo) * 128
                kT = kT_r.next()
                qT = qT_r.next()
                vg = v_r.next()
                S.dma("sp", kT.t[:, :, jlo * 128:jhi * 128], kaT.rearrange("(hp p) t -> p hp t", p=128)[:, :, k0:k0 + nk], writes=[kT], key=kT.name)
                S.dma("sp", qT.t[:], qaT.rearrange("(hp p) t -> p hp t", p=128)[:, :, g * 512:(g + 1) * 512], writes=[qT], key=qT.name)
                S.dma("sp", vg.t[:, jlo:jhi, :], va[k0:k0 + nk, :].rearrange("(j p) c -> p j c", p=128), writes=[vg], key=vg.name)
                if g in SG:
                    rm = rm_r.next()
                    S.dma("sp", rm.t[:], rowmask[1 + SG.index(g)].rearrange("j p q -> p j q"), writes=[rm], key=rm.name)
                else:
                    rm = rm_int
                ya = ya_r.next()
                def n_scores(hp, j):
                    sps = []
                    for a2 in range(2):
                        sp_ = s_r.next()
                        lo = a2 * 64
                        S.op("pe", lambda e: e.matmul(sp_.t[:], lhsT=kT.t[lo:lo + 64, hp, j * 128:(j + 1) * 128], rhs=qT.t[lo:lo + 64, hp, :],
                                                      start=True, stop=True), reads=[kT, qT], writes=[sp_])
                        sps.append(sp_)
                    return sps

                units = [(hp, ji, j) for hp in range(4) for ji, j in enumerate(jl)]
                sps_next = n_scores(units[0][0], units[0][2])
                accs = None
                for ui, (hp, ji, j) in enumerate(units):
                    if ji == 0:
                        accs = [acc_r.next(), acc_r.next()]
                    sps = sps_next
                    if ui + 1 < len(units):
                        sps_next = n_scores(units[ui + 1][0], units[ui + 1][2])
                    et = e_r.next()
                    pt = p_r.next()
                    s0 = 15 - 2 * j
                    for a2 in range(2):
                        sp_ = sps[a2]
                        S.op("act", lambda e: e.activation(out=et.t[:, a2 * 512:(a2 + 1) * 512], in_=sp_.t[:], func=AF.Exp, scale=0.125), reads=[sp_], writes=[et])
                    if rm is rm_int:
                        S.op("dve", lambda e: e.tensor_tensor(out=pt.t[:], in0=et.t[:], in1=TMi.t[:, j, 2 * hp:2 * hp + 2, :].rearrange("p h q -> p (h q)"), op=ALU.mult),
                             reads=[et, TMi], writes=[pt])
                    else:
                        for a2 in range(2):
                            hh = 2 * hp + a2
                            ev = et.t[:, a2 * 512:(a2 + 1) * 512]
                            S.op("dve", lambda e: e.tensor_tensor(out=ev.rearrange("p (b q) -> p b q", q=64), in0=ev.rearrange("p (b q) -> p b q", q=64),
                                                                  in1=Tf.t[:, hh, s0:s0 + 8, :], op=ALU.mult), reads=[et, Tf], writes=[et])
                            S.op("dve", lambda e: e.tensor_tensor(out=pt.t[:, a2 * 512:(a2 + 1) * 512], in0=ev, in1=rm.t[:, j, :], op=ALU.mult), reads=[et, rm], writes=[pt])
                    for a2 in range(2):
                        hh = 2 * hp + a2
                        for qt in range(4):
                            S.op("pe", lambda e: e.matmul(av(accs[a2])[:, qt, :], lhsT=pt.t[:, a2 * 512 + qt * 128:a2 * 512 + (qt + 1) * 128],
                                                          rhs=vg.t[:, j, hh * 65:(hh + 1) * 65], start=(ji == 0 and qt == 0), stop=(ji == len(jl) - 1)),
                                 reads=[pt, vg], writes=[accs[a2]])
                    if ji == len(jl) - 1:
                        for a2 in range(2):
                            hh = 2 * hp + a2
                            rc = rc_r.next()
                            S.op("dve", lambda e: e.reciprocal(out=rc.t[:, 0:4].rearrange("p (q o) -> p q o", o=1), in_=av(accs[a2])[:, :, 64:65]), reads=[accs[a2]], writes=[rc])
                            for qt in range(4):
                                S.op("dve", lambda e: e.tensor_scalar(out=ya.t[:, qt, hh * 64:(hh + 1) * 64], in0=av(accs[a2])[:, qt, 0:64], scalar1=rc.t[:, qt:qt + 1],
                                                                      scalar2=None, op0=ALU.mult), reads=[accs[a2], rc], writes=[ya])
                S.dma("pool", ycat[g * 512:(g + 1) * 512, 0:512].rearrange("(qt p) c -> p qt c", p=128), ya.t[:], reads=[ya], key=ya.name)
            S.flush()

        with ExitStack() as st:
            kT_r = ring(st, "F_kT", 2, [128, T], BF16)
            v_r = ring(st, "F_v", 1, [128, NT, 129], BF16)
            vm_r = ring(st, "F_vm", 2, [128, 2, NT, 129], BF16)
            qT_r = ring(st, "F_qT", 2, [128, 512], BF16)
            s_r = ring(st, "F_s", 2, [128, 1024], F32, psum=True)
            p_r = ring(st, "F_p", 3, [128, 1024], BF16)
            acc_r = ring(st, "F_acc", 4, [128, 512], F32, psum=True)
            fv = lambda b_: b_.t[:, 0:258].rearrange("p (a c) -> p a c", c=129)
            rc_r = ring(st, "F_rc", 2, [128, 8], F32)
            t_r = ring(st, "F_t", 2, [128, 128], F32)
            yv_r = ring(st, "F_yv", 2, [128, 128], F32)
            jk_r = ring(st, "F_jk", 1, [128, 128], F32)
            yb_r = ring(st, "F_yb", 2, [128, 4, 128], BF16)
            gs = sb(st, "F_gs", [128, 128], F32)
            S.dma("sp", gs.t[:], subln[l], writes=[gs], key="F_gs")
            S.op("dve", lambda e: e.tensor_scalar(out=gs.t[:], in0=gs.t[:], scalar1=(1.0 - lam_init), scalar2=None, op0=ALU.mult), reads=[gs], writes=[gs])
            for h in range(4):
                kT = kT_r.next()
                vh = v_r.next()
                S.dma("sp", kT.t[:], kbT[h * 128:(h + 1) * 128, :], writes=[kT], key=kT.name)
                S.dma("sp", vh.t[:], vb[:, h * 129:(h + 1) * 129].rearrange("(j p) c -> p j c", p=128), writes=[vh], key=vh.name)
                vm = vm_r.next()
                for q2 in range(2):
                    S.op("dve", lambda e: e.tensor_tensor(out=vm.t[:, q2], in0=vh.t[:], in1=dmask_t.t[:, q2 * NT:(q2 + 1) * NT].unsqueeze(2).to_broadcast([128, NT, 129]),
                                                          op=ALU.mult), reads=[vh, dmask_t], writes=[vm])

                def f_scores(qT, kt):
                    sp_ = s_r.next()
                    for a2 in range(2):
                        lo = a2 * 64
                        S.op("pe", lambda e: e.matmul(sp_.t[:, a2 * 512:(a2 + 1) * 512], lhsT=kT.t[lo:lo + 64, kt * 128:(kt + 1) * 128], rhs=qT.t[lo:lo + 64, :], start=True, stop=True),
                             reads=[kT, qT], writes=[sp_])
                    return sp_

                for qb in range(NB):
                    qT = qT_r.next()
                    S.dma("sp", qT.t[:], qbT[h * 128:(h + 1) * 128, qb * 512:(qb + 1) * 512], writes=[qT], key=qT.name)
                    accs = [acc_r.next() for _ in range(4)]
                    qhalf = 0 if qb < NB // 2 else 1
                    sp_next = f_scores(qT, 0)
                    for kt in range(NT):
                        sp_ = sp_next
                        if kt + 1 < NT:
                            sp_next = f_scores(qT, kt + 1)
                        pt = p_r.next()
                        for a2 in range(2):
                            S.op("act", lambda e: e.activation(out=pt.t[:, a2 * 512:(a2 + 1) * 512], in_=sp_.t[:, a2 * 512:(a2 + 1) * 512], func=AF.Exp, scale=0.125), reads=[sp_], writes=[pt])
                        for a2 in range(2):
                            for qt in range(4):
                                S.op("pe", lambda e: e.matmul(fv(accs[qt])[:, a2, :], lhsT=pt.t[:, a2 * 512 + qt * 128:a2 * 512 + (qt + 1) * 128], rhs=vm.t[:, qhalf, kt, :],
                                                              start=(kt == 0 and a2 == 0), stop=(kt == NT - 1)), reads=[pt, vm], writes=[accs[qt]])
                    yb = yb_r.next()
                    for qt in range(4):
                        ac = accs[qt]
                        rc = rc_r.next()
                        tt_ = t_r.next()
                        yv = yv_r.next()
                        jk = jk_r.next()
                        S.op("dve", lambda e, ac=ac, rc=rc: e.reciprocal(out=rc.t[:, 0:2].rearrange("p (q o) -> p q o", o=1), in_=fv(ac)[:, :, 128:129]), reads=[ac], writes=[rc])
                        S.op("dve", lambda e, rc=rc: e.tensor_tensor(out=rc.t[:, 2:3], in0=rc.t[:, 1:2], in1=lam_t.t[:, 0:1], op=ALU.mult), reads=[rc, lam_t], writes=[rc])
                        S.op("dve", lambda e, ac=ac, rc=rc, tt_=tt_: e.tensor_scalar(out=tt_.t[:], in0=fv(ac)[:, 1, 0:128], scalar1=rc.t[:, 2:3], scalar2=None, op0=ALU.mult),
                             reads=[ac, rc], writes=[tt_])
                        S.op("dve", lambda e, ac=ac, rc=rc, tt_=tt_, yv=yv: e.scalar_tensor_tensor(out=yv.t[:], in0=fv(ac)[:, 0, 0:128], scalar=rc.t[:, 0:1], in1=tt_.t[:],
                                                                                                    op0=ALU.mult, op1=ALU.add), reads=[ac, rc, tt_], writes=[yv])
                        S.op("dve", lambda e, yv=yv, jk=jk, rc=rc: e.scalar_tensor_tensor(out=jk.t[:], in0=yv.t[:], scalar=1.0, in1=yv.t[:], op0=ALU.mult, op1=ALU.mult,
                                                                                           accum_out=rc.t[:, 3:4]), reads=[yv], writes=[jk, rc])
                        S.op("dve", lambda e, rc=rc: e.tensor_scalar(out=rc.t[:, 4:5], in0=rc.t[:, 3:4], scalar1=1.0 / 128, scalar2=EPS, op0=ALU.mult, op1=ALU.add), reads=[rc], writes=[rc])
                        S.op("act", lambda e, rc=rc: e.activation(out=rc.t[:, 6:7], in_=rc.t[:, 4:5], func=AF.Ln), reads=[rc], writes=[rc])
                        S.op("act", lambda e, rc=rc: e.activation(out=rc.t[:, 5:6], in_=rc.t[:, 6:7], func=AF.Exp, scale=-0.5), reads=[rc], writes=[rc])
                        S.op("dve", lambda e, yv=yv, rc=rc, qt=qt: e.scalar_tensor_tensor(out=yb.t[:, qt, :], in0=yv.t[:], scalar=rc.t[:, 5:6], in1=gs.t[:], op0=ALU.mult, op1=ALU.mult),
                             reads=[yv, rc, gs], writes=[yb])
                    S.dma("pool", ycat[qb * 512:(qb + 1) * 512, 512 + h * 128:512 + (h + 1) * 128].rearrange("(qt p) c -> p qt c", p=128), yb.t[:], reads=[yb], key=yb.name)
            S.flush()

        with ExitStack() as st:
            W = sb(st, "D_W", [128, 8, D], BF16)
            stg_r = ring(st, "D_stg", 2, [128, 1024], F32)
            g_t = sb(st, "D_g", [128, D], F32)
            y_r = ring(st, "D_y", 3, [128, D], BF16)
            x_r = ring(st, "D_x", 4, [128, D], F32)
            xm_r = ring(st, "D_xm", 4, [128, D], F32)
            yT_r = ring(st, "D_yT", 3, [128, 8, 128], BF16)
            tpa_r = ring(st, "D_tpa", 2, [128, 8, 128], BF16, psum=True)
            tpb_r = ring(st, "D_tpb", 2, [128, 8, 128], BF16, psum=True)
            po_r = ring(st, "D_po", 4, [128, 512], F32, psum=True)
            junk_r = ring(st, "D_jk", 2, [128, D], BF16)
            ss_r = ring(st, "D_ss", 4, [128, 4], F32)
            nb_r = ring(st, "D_nb", 3, [128, D], BF16)
            nT_r = ring(st, "D_nT", 3, [128, 8, 128], BF16)
            S.dma("sp", g_t.t[:], g_ffn[l], writes=[g_t], key="D_g")
            load_w(stg_r, W, 8, D, w_out[l])
            ctx = {}

            def d0(i):
                c = ctx[i] = {}
                r0 = i * 128
                c["yt"] = yt = y_r.next()
                c["xt"] = xt = x_r.next()
                S.dma("sp", yt.t[:], ycat[r0:r0 + 128, :], writes=[yt], key=yt.name)
                S.dma("sp", xt.t[:], x_src[r0:r0 + 128, :], writes=[xt], key=xt.name)

            def d1(i):
                c = ctx[i]
                yt = c["yt"]
                tp = tpa_r.next()
                for cc in range(8):
                    S.op("pe", lambda e: e.transpose(out=tp.t[:, cc, :], in_=yt.t[:, cc * 128:(cc + 1) * 128], identity=ident.t[:]), reads=[yt, ident], writes=[tp])
                c["yT"] = yT = yT_r.next()
                S.op("act", lambda e: e.activation(out=yT.t[:], in_=tp.t[:], func=AF.Copy), reads=[tp], writes=[yT])

            def d2(i):
                c = ctx[i]
                yT, xt = c["yT"], c["xt"]
                r0 = i * 128
                c["xm"] = xm = xm_r.next()
                for half in range(2):
                    po = po_r.next()
                    for k in range(8):
                        S.op("pe", lambda e: e.matmul(po.t[:], lhsT=yT.t[:, k, :], rhs=W.t[:, k, half * 512:(half + 1) * 512], start=(k == 0), stop=(k == 7)),
                             reads=[yT, W], writes=[po])
                    S.op("dve", lambda e: e.tensor_tensor(out=xm.t[:, half * 512:(half + 1) * 512], in0=po.t[:], in1=xt.t[:, half * 512:(half + 1) * 512], op=ALU.add),
                         reads=[po, xt], writes=[xm])
                S.dma("pool", xmid[r0:r0 + 128, :], xm.t[:], reads=[xm], key=xm.name)
                jk = junk_r.next()
                c["ss"] = ss = ss_r.next()
                S.op("dve", lambda e: e.scalar_tensor_tensor(out=jk.t[:], in0=xm.t[:], scalar=1.0, in1=xm.t[:], op0=ALU.mult, op1=ALU.mult,
                                                             accum_out=ss.t[:, 0:1]), reads=[xm], writes=[jk, ss])
                S.op("dve", lambda e: e.tensor_scalar(out=ss.t[:, 1:2], in0=ss.t[:, 0:1], scalar1=1.0 / D, scalar2=EPS, op0=ALU.mult, op1=ALU.add),
                     reads=[ss], writes=[ss])

            def d3(i):
                ss = ctx[i]["ss"]
                S.op("act", lambda e: e.activation(out=ss.t[:, 3:4], in_=ss.t[:, 1:2], func=AF.Ln), reads=[ss], writes=[ss])
                S.op("act", lambda e: e.activation(out=ss.t[:, 2:3], in_=ss.t[:, 3:4], func=AF.Exp, scale=-0.5), reads=[ss], writes=[ss])

            def d4(i):
                c = ctx[i]
                xm, ss = c["xm"], c["ss"]
                nb = nb_r.next()
                S.op("dve", lambda e: e.scalar_tensor_tensor(out=nb.t[:], in0=xm.t[:], scalar=ss.t[:, 2:3], in1=g_t.t[:], op0=ALU.mult, op1=ALU.mult),
                     reads=[xm, ss, g_t], writes=[nb])
                c["tp2"] = tp = tpb_r.next()
                for cc in range(8):
                    S.op("pe", lambda e: e.transpose(out=tp.t[:, cc, :], in_=nb.t[:, cc * 128:(cc + 1) * 128], identity=ident.t[:]), reads=[nb, ident], writes=[tp])

            def d5(i):
                c = ctx[i]
                tp = c["tp2"]
                r0 = i * 128
                nT = nT_r.next()
                S.op("act", lambda e: e.activation(out=nT.t[:], in_=tp.t[:], func=AF.Copy), reads=[tp], writes=[nT])
                S.dma("pool", n2T_v[:, :, 1 + r0:1 + r0 + 128], nT.t[:], reads=[nT], key=nT.name)
                del ctx[i]

            stages = [d0, d1, d2, d3, d4, d5]
            for tstep in range(NT + len(stages) - 1):
                for s_ in reversed(range(len(stages))):
                    i = tstep - s_
                    if 0 <= i < NT:
                        stages[s_](i)
            S.flush()

        last = (l == NL - 1)
        with ExitStack() as st:
            Wu = sb(st, "E_Wu", [128, 8, 2 * DFF], BF16)
            Wd = sb(st, "E_Wd", [128, NFC, D], BF16)
            stg_r = ring(st, "E_stg", 2, [128, 512], F32)
            cw = sb(st, "E_cw", [128, 4 * NFC], F32)
            nT_r = ring(st, "E_nT", 1, [128, 8, 514], BF16)
            hT = sb(st, "E_hT", [128, NFC, 512], BF16)
            pg_r = ring(st, "E_pg", 2, [128, 512], F32, psum=True)
            pvv_r = ring(st, "E_pvv", 2, [128, 512], F32, psum=True)
            ph_r = ring(st, "E_ph", 2, [128, 2], F32, psum=True)
            pd_r = ring(st, "E_pd", 2, [128, 512], F32, psum=True)
            gx_r = ring(st, "E_gx", 2, [128, 514], F32)
            a_r = ring(st, "E_a", 2, [128, 512], F32)
            gl_r = ring(st, "E_gl", 2, [128, 512], F32)
            xm_r = ring(st, "E_xm", 2, [128, D], F32)
            xo_r = ring(st, "E_xo", 1, [128, D], F32)
            S.dma("sp", cw.t[:], convw[l], writes=[cw], key="E_cw")
            if last:
                g_t = sb(st, "E_g", [128, D], F32)
                jk_r = ring(st, "E_jk", 1, [128, D], BF16)
                ss_r = ring(st, "E_ss", 2, [128, 4], F32)
                yo_r = ring(st, "E_yo", 1, [128, D], F32)
                S.dma("sp", g_t.t[:], g_final[:, :], writes=[g_t], key="E_g")
            load_w(stg_r, Wu, 8, 2 * DFF, w_up[l], CH=512)
            load_w(stg_r, Wd, NFC, D, w_down[l], CH=512)
            for blk in range(NB):
                t0 = blk * 512
                nT = nT_r.next()
                S.dma("sp", nT.t[:], n2T_v[:, :, t0:t0 + 514], writes=[nT], key=nT.name)
                if NB >= 2 and blk == NB // 2:
                    S.op("dve", lambda e, nT=nT: e.tensor_scalar(out=nT.t[:, :, 0:1], in0=nT.t[:, :, 0:1], scalar1=cmask_t.t[:, 0:1], scalar2=None, op0=ALU.mult), reads=[nT, cmask_t], writes=[nT])
                if NB >= 2 and blk == NB // 2 - 1:
                    S.op("dve", lambda e, nT=nT: e.tensor_scalar(out=nT.t[:, :, 513:514], in0=nT.t[:, :, 513:514], scalar1=cmask_t.t[:, 1:2], scalar2=None, op0=ALU.mult), reads=[nT, cmask_t], writes=[nT])
                for fc in range(NFC):
                    pg = pg_r.next()
                    ph = ph_r.next()
                    pvv = pvv_r.next()
                    for k in range(8):
                        S.op("pe", lambda e, k=k, pg=pg, fc=fc, nT=nT: e.matmul(pg.t[:], lhsT=Wu.t[:, k, fc * 128:(fc + 1) * 128], rhs=nT.t[:, k, 1:513], start=(k == 0), stop=(k == 7)),
                             reads=[Wu, nT], writes=[pg])
                    for k in range(8):
                        S.op("pe", lambda e, k=k, ph=ph, fc=fc, nT=nT: e.matmul(ph.t[:], lhsT=Wu.t[:, k, fc * 128:(fc + 1) * 128], rhs=nT.t[:, k, 0:514:513], start=(k == 0), stop=(k == 7)),
                             reads=[Wu, nT], writes=[ph])
                    for k in range(8):
                        S.op("pe", lambda e, k=k, pvv=pvv, fc=fc, nT=nT: e.matmul(pvv.t[:], lhsT=Wu.t[:, k, DFF + fc * 128:DFF + (fc + 1) * 128], rhs=nT.t[:, k, 1:513], start=(k == 0), stop=(k == 7)),
                             reads=[Wu, nT], writes=[pvv])
                    gx = gx_r.next()
                    S.op("act", lambda e, gx=gx, pg=pg: e.activation(out=gx.t[:, 1:513], in_=pg.t[:], func=AF.Copy), reads=[pg], writes=[gx])
                    S.op("act", lambda e, gx=gx, ph=ph: e.activation(out=gx.t[:, 0:514:513], in_=ph.t[:], func=AF.Copy), reads=[ph], writes=[gx])
                    a = a_r.next()
                    S.op("dve", lambda e, a=a, gx=gx, fc=fc: e.tensor_scalar(out=a.t[:], in0=gx.t[:, 1:513], scalar1=cw.t[:, NFC + fc:NFC + fc + 1], scalar2=cw.t[:, 3 * NFC + fc:3 * NFC + fc + 1],
                                                                          op0=ALU.mult, op1=ALU.add), reads=[gx, cw], writes=[a])
                    S.op("dve", lambda e, a=a, gx=gx, fc=fc: e.scalar_tensor_tensor(out=a.t[:], in0=gx.t[:, 0:512], scalar=cw.t[:, fc:fc + 1], in1=a.t[:], op0=ALU.mult, op1=ALU.add),
                         reads=[gx, cw, a], writes=[a])
                    S.op("dve", lambda e, a=a, gx=gx, fc=fc: e.scalar_tensor_tensor(out=a.t[:], in0=gx.t[:, 2:514], scalar=cw.t[:, 2 * NFC + fc:2 * NFC + fc + 1], in1=a.t[:], op0=ALU.mult, op1=ALU.add),
                         reads=[gx, cw, a], writes=[a])
                    gl = gl_r.next()
                    S.op("act", lambda e, a=a, gl=gl: e.activation(out=gl.t[:], in_=a.t[:], func=AF.Gelu), reads=[a], writes=[gl])
                    S.op("dve", lambda e, gl=gl, pvv=pvv, fc=fc: e.tensor_tensor(out=hT.t[:, fc, :], in0=pvv.t[:], in1=gl.t[:], op=ALU.mult), reads=[gl, pvv], writes=[hT])
                for tt in range(4):
                    r0 = t0 + tt * 128
                    xm = xm_r.next()
                    S.dma("sp", xm.t[:], xmid[r0:r0 + 128, :], writes=[xm], key=xm.name)
                    xo = xo_r.next()
                    for half in range(2):
                        pd = pd_r.next()
                        for fc in range(NFC):
                            S.op("pe", lambda e, fc=fc, pd=pd, tt=tt, half=half: e.matmul(pd.t[:], lhsT=hT.t[:, fc, tt * 128:(tt + 1) * 128], rhs=Wd.t[:, fc, half * 512:(half + 1) * 512],
                                                                                       start=(fc == 0), stop=(fc == NFC - 1)), reads=[hT, Wd], writes=[pd])
                        S.op("dve", lambda e, pd=pd, xm=xm, xo=xo, half=half: e.tensor_tensor(out=xo.t[:, half * 512:(half + 1) * 512], in0=pd.t[:], in1=xm.t[:, half * 512:(half + 1) * 512], op=ALU.add),
                             reads=[pd, xm], writes=[xo])
                    if not last:
                        S.dma("pool", xres[r0:r0 + 128, :], xo.t[:], reads=[xo], key=xo.name)
                    else:
                        jk = jk_r.next()
                        ss = ss_r.next()
                        yo = yo_r.next()
                        S.op("dve", lambda e, jk=jk, ss=ss, xo=xo: e.scalar_tensor_tensor(out=jk.t[:], in0=xo.t[:], scalar=1.0, in1=xo.t[:], op0=ALU.mult, op1=ALU.mult, accum_out=ss.t[:, 0:1]),
                             reads=[xo], writes=[jk, ss])
                        S.op("dve", lambda e, ss=ss: e.tensor_scalar(out=ss.t[:, 1:2], in0=ss.t[:, 0:1], scalar1=1.0 / D, scalar2=EPS, op0=ALU.mult, op1=ALU.add), reads=[ss], writes=[ss])
                        S.op("act", lambda e, ss=ss: e.activation(out=ss.t[:, 3:4], in_=ss.t[:, 1:2], func=AF.Ln), reads=[ss], writes=[ss])
                        S.op("act", lambda e, ss=ss: e.activation(out=ss.t[:, 2:3], in_=ss.t[:, 3:4], func=AF.Exp, scale=-0.5), reads=[ss], writes=[ss])
                        S.op("dve", lambda e, ss=ss, xo=xo, yo=yo: e.scalar_tensor_tensor(out=yo.t[:], in0=xo.t[:], scalar=ss.t[:, 2:3], in1=g_t.t[:], op0=ALU.mult, op1=ALU.mult),
                             reads=[xo, ss, g_t], writes=[yo])
                        S.dma("pool", y_out[r0:r0 + 128, :], yo.t[:], reads=[yo], key=yo.name)
            S.flush()
    es.close()
    return nc


def _rowmask_tables(T, seq_len):
    ROWS = T // 64
    NG = T // 512
    SG = sorted(set([0, NG // 2 - 1, NG // 2, NG - 1]))
    R = seq_len // 64

    def table(g):
        m = np.zeros((8, 128, 512), np.float32)
        for j in range(8):
            for a in range(2):
                kr = 8 * g - 4 + 2 * j + a
                if kr < 0 or kr >= ROWS:
                    continue
                for b in range(8):
                    qr = 8 * g + b
                    if kr // R != qr // R:
                        continue
                    qs = qr % R
                    start = min(max(qs - 4, 0), R - 8)
                    ks = kr % R
                    if start <= ks < start + 8:
                        m[j, a * 64:(a + 1) * 64, b * 64:(b + 1) * 64] = 1.0
        return m
    mi = np.zeros((8, 128, 512), np.float32)
    for j in range(8):
        for a in range(2):
            for b in range(8):
                dr = 2 * j + a - b - 4
                if -4 <= dr <= 3:
                    mi[j, a * 64:(a + 1) * 64, b * 64:(b + 1) * 64] = 1.0
    tabs = [mi] + [table(g) for g in SG]
    return np.stack(tabs).astype(ml_dtypes.bfloat16)


def _rot_tables(T, seq_len):
    pos = (np.arange(T) % seq_len).astype(np.float32)
    inv = (1.0 / (10000.0 ** (np.arange(0, 64, 2, dtype=np.float32) / 64.0))).astype(np.float32)
    ang = pos[None, :] * inv[:, None]
    cos = np.cos(ang).astype(np.float32)
    sin = np.sin(ang).astype(np.float32)
    cos64 = np.concatenate([cos, cos], 0)
    sin64 = np.concatenate([-sin, sin], 0)
    return np.stack([np.concatenate([cos64, cos64], 0), np.concatenate([sin64, sin64], 0)]).astype(np.float32)


def _prep_shared(inp, NL):
    w_in = np.asarray(inp["w_in"], np.float32)
    perm = np.arange(1024).reshape(8, 2, 64)
    perm = np.concatenate([perm[..., 32:], perm[..., :32]], -1).reshape(-1)
    qb = w_in[:, :, 1536:2560]
    w_ext = np.concatenate([w_in, qb[:, :, perm]], axis=2)
    rep = lambda a: np.ascontiguousarray(np.broadcast_to(np.asarray(a, np.float32)[:, None, :], (a.shape[0], 128, a.shape[1])))
    lamv = np.concatenate([np.asarray(inp[k], np.float32) for k in ("lam_q1", "lam_k1", "lam_q2", "lam_k2")], axis=1)
    cw = np.concatenate([np.asarray(inp["conv_w"], np.float32), np.asarray(inp["conv_b"], np.float32)[:, None, :]], axis=1)
    cw = cw.reshape(NL, 4, NFC, 128).transpose(0, 3, 1, 2).reshape(NL, 128, 4 * NFC)
    rpb = np.asarray(inp["rpb"], np.float32)
    kc = np.arange(64)[:, None]
    qc = np.arange(64)[None, :]
    dc = kc - qc + 15
    cs = np.clip(qc - 8, 0, 48)
    ok = (kc >= cs) & (kc < cs + 16)
    dcc = np.clip(dc, 0, 30)
    rp = rpb[:, :, ::-1, :][:, :, :, dcc]
    rp = np.where(ok[None, None, None], rp, np.float32(NEG)).astype(np.float32)
    return dict(
        w_in=np.ascontiguousarray(w_ext), w_out=np.asarray(inp["w_out"], np.float32), w_up=np.asarray(inp["w_up"], np.float32),
        w_down=np.asarray(inp["w_down"], np.float32), g_attn=rep(inp["g_attn"]), g_ffn=rep(inp["g_ffn"]),
        g_final=np.ascontiguousarray(np.broadcast_to(np.asarray(inp["g_final"], np.float32)[None, :], (128, D))),
        subln=rep(inp["subln_g"]), lamv=rep(lamv), convw=np.ascontiguousarray(cw), rpbt=np.ascontiguousarray(rp),
        ident=np.eye(128, dtype=np.float32).astype(ml_dtypes.bfloat16),
    )


def _prep_core(T, seq_len):
    NT = T // 128
    dm = np.ones((2, NT), np.float32)
    if seq_len < T:
        dm[0, NT // 2:] = 0.0
        dm[1, :NT // 2] = 0.0
    dmask = np.ascontiguousarray(np.broadcast_to(dm.reshape(1, -1), (128, 2 * NT)))
    cm = np.full((128, 2), 1.0 if seq_len == T else 0.0, np.float32)
    return dict(rot=_rot_tables(T, seq_len), rowmask=_rowmask_tables(T, seq_len), dmask=dmask, cmask=cm)


_PROG_CACHE = {}


def run_cores(inp, xs, seqlens, T, NL, n_cores):
    key = (T, NL)
    if key not in _PROG_CACHE:
        _PROG_CACHE[key] = build_program(T, NL)
    nc = _PROG_CACHE[key]
    shared = _prep_shared(inp, NL)
    per_type = {}
    in_maps = []
    for x, sl in zip(xs, seqlens):
        if sl not in per_type:
            per_type[sl] = _prep_core(T, sl)
        m = dict(shared)
        m.update(per_type[sl])
        m["x"] = np.ascontiguousarray(x, dtype=np.float32)
        in_maps.append(m)
    res = run_bass_kernel_spmd(nc, in_maps, core_ids=list(range(n_cores)))
    return [r["y_out"] for r in res.results]


def kernel(x_prompt, x_sample, g_attn, w_in, rpb, lam_q1, lam_k1, lam_q2, lam_k2, subln_g, w_out,
           g_ffn, w_up, conv_w, conv_b, w_down, g_final):
    inp = dict(g_attn=g_attn, w_in=w_in, rpb=rpb, lam_q1=lam_q1, lam_k1=lam_k1, lam_q2=lam_q2, lam_k2=lam_k2,
               subln_g=subln_g, w_out=w_out, g_ffn=g_ffn, w_up=w_up, conv_w=conv_w, conv_b=conv_b, w_down=w_down, g_final=g_final)
    inp = {k: np.asarray(v) for k, v in inp.items()}
    xp = np.asarray(x_prompt, np.float32)
    xs_ = np.asarray(x_sample, np.float32)
    T = 8192
    xs = [xs_[0], xs_[1], xp[0:2].reshape(T, D), xp[2:4].reshape(T, D)]
    sl = [8192, 8192, 4096, 4096]
    outs = run_cores(inp, xs + xs, sl + sl, T, 4, 8)
    y_sample = np.stack([outs[0], outs[1]]).astype(np.float32)
    y_prompt = np.concatenate([outs[2].reshape(2, 4096, D), outs[3].reshape(2, 4096, D)], 0).astype(np.float32)
    return (y_prompt, y_sample)
```

```python
import math
from contextlib import ExitStack

import numpy as np
import ml_dtypes

import concourse.bass as bass
import concourse.mybir as mybir
from concourse.bass_utils import run_bass_kernel_spmd

F32 = mybir.dt.float32
BF16 = mybir.dt.bfloat16
AF = mybir.ActivationFunctionType
ALU = mybir.AluOpType

D = 1024
NAH = 8
DFF = 2816
NFC = DFF // 128
EPS = 1e-6
NEG = -30000.0
W_IN_EXT = 4096
SAME_ENGINE_SYNC = True


class Buf:
    def __init__(self, name, t):
        self.name = name
        self.t = t


class Ring:
    def __init__(self, bufs):
        self.bufs = bufs
        self.i = 0

    def next(self):
        b = self.bufs[self.i % len(self.bufs)]
        self.i += 1
        return b


class _Rec:
    def __getattr__(self, name):
        def f(*a, **k):
            self.call = (name, a, k)
            return self
        return f


class Sched:
    ENGS = ("pe", "act", "dve", "pool", "sp")

    def __init__(self, nc, es):
        self.nc = nc
        self.es = es
        self.ops = []
        self.state = {}
        self.dma_sems = {}
        self.dma_cnt = {}
        self.eng_sems = None
        self.eng_cnt = None

    def new_eng_sems(self, tag):
        self.eng_sems = {e: self.es.enter_context(self.nc.semaphore(f"s_{tag}_{e}")) for e in self.ENGS}
        self.eng_cnt = {e: 0 for e in self.ENGS}

    def I(self, eng, meth, reads=(), writes=(), dma_key=None, **kw):
        return self.op(eng, (meth, (), kw), reads, writes, dma_key)

    def op(self, eng, fn, reads=(), writes=(), dma_key=None):
        if callable(fn):
            rec = _Rec()
            fn(rec)
            fn = rec.call
        oid = len(self.ops)
        deps = set()
        for r in reads:
            st = self.state.setdefault(r.name, {"w": {}, "r": {}})
            deps.update(st["w"].values())
        for w in writes:
            st = self.state.setdefault(w.name, {"w": {}, "r": {}})
            deps.update(st["w"].values())
            deps.update(st["r"].values())
        evkey = ("dma", dma_key) if dma_key is not None else eng
        for r in reads:
            self.state[r.name]["r"][evkey] = oid
        for w in writes:
            st = self.state[w.name]
            st["r"] = {}
            st["w"] = {evkey: oid}
        self.ops.append(dict(eng=eng, fn=fn, deps=deps, dma_key=dma_key, sig=False, val=None))
        return oid

    def dma(self, eng, out, in_, reads=(), writes=(), key=None, **kw):
        assert key is not None
        return self.op(eng, ("dma_start", (), dict(out=out, in_=in_, **kw)), reads, writes, dma_key=key.split("|")[-1])

    def flush(self):
        nc = self.nc
        ops = self.ops
        for o in ops:
            need = set()
            for d in o["deps"]:
                p = ops[d]
                if p["dma_key"] is not None:
                    need.add(d)
                elif p["eng"] != o["eng"]:
                    need.add(d)
                    p["sig"] = True
                elif SAME_ENGINE_SYNC and p["eng"] in ("act", "dve", "pool"):
                    need.add(d)
                    p["sig"] = True
            o["need"] = need
        for o in ops:
            if o["dma_key"] is not None:
                k = o["dma_key"]
                if k not in self.dma_sems:
                    self.dma_sems[k] = self.es.enter_context(nc.semaphore(f"d_{k}"))
                    self.dma_cnt[k] = 0
                self.dma_cnt[k] += 16
                o["val"] = (self.dma_sems[k], self.dma_cnt[k])
            elif o["sig"]:
                self.eng_cnt[o["eng"]] += 1
                o["val"] = (self.eng_sems[o["eng"]], self.eng_cnt[o["eng"]])
        per = {e: [] for e in self.ENGS}
        for o in ops:
            per[o["eng"]].append(o)

        def emit(eng_name, lst):
            def body(e):
                seen = {}
                issued = {}
                for o in lst:
                    waits = {}
                    for d in o["need"]:
                        sem, v = ops[d]["val"]
                        kk = id(sem)
                        if seen.get(kk, 0) >= v:
                            continue
                        if kk not in waits or waits[kk][1] < v:
                            waits[kk] = (sem, v)
                    for kk, (sem, v) in waits.items():
                        e.wait_ge(sem, v)
                        seen[kk] = v
                    meth, ar, kw = o["fn"]
                    ins = getattr(e, meth)(*ar, **kw)
                    if o["dma_key"] is not None:
                        sem, v = o["val"]
                        ins.then_inc(sem, 16)
                        issued[id(sem)] = (sem, v)
                    elif o["sig"]:
                        ins.then_inc(o["val"][0], 1)
                for kk, (sem, v) in issued.items():
                    e.wait_ge(sem, v)
            return body

        with nc.Block() as block:
            if per["sp"]:
                block.sync(emit("sp", per["sp"]))
            if per["pe"]:
                block.tensor(emit("pe", per["pe"]))
            if per["act"]:
                block.scalar(emit("act", per["act"]))
            if per["dve"]:
                block.vector(emit("dve", per["dve"]))
            if per["pool"]:
                block.gpsimd(emit("pool", per["pool"]))
        self.ops = []
        self.state = {}


def build_program(T, NL, n_types_dummy=None):
    nc = bass.Bass("TRN2", target_bir_lowering=False)
    NT = T // 128
    NB = T // 512
    ROWS = T // 64
    NG = NB
    SG = sorted(set([0, NG // 2 - 1, NG // 2, NG - 1]))

    def din(name, shape, dt=F32):
        return nc.dram_tensor(name, list(shape), dt, kind="ExternalInput").ap()

    x_in = din("x", [T, D])
    w_in = din("w_in", [NL, D, W_IN_EXT])
    w_out = din("w_out", [NL, D, D])
    w_up = din("w_up", [NL, D, 2 * DFF])
    w_down = din("w_down", [NL, DFF, D])
    g_attn = din("g_attn", [NL, 128, D])
    g_ffn = din("g_ffn", [NL, 128, D])
    g_final = din("g_final", [128, D])
    subln = din("subln", [NL, 128, 128])
    lamv = din("lamv", [NL, 128, 256])
    convw = din("convw", [NL, 128, 4 * NFC])
    rpbt = din("rpbt", [NL, NAH, 15, 64, 64])
    rot = din("rot", [2, 128, T])
    rowmask = din("rowmask", [len(SG) + 1, 8, 128, 512], BF16)
    dmask = din("dmask", [128, 2 * (T // 128)])
    cmask = din("cmask", [128, 2])
    ident_in = din("ident", [128, 128], BF16)
    y_out = nc.dram_tensor("y_out", [T, D], F32, kind="ExternalOutput").ap()

    def dscr(name, shape, dt):
        return nc.dram_tensor(name, list(shape), dt).ap()

    qaT = dscr("qaT", [512, T], BF16)
    kaT = dscr("kaT", [512, T], BF16)
    va = dscr("va", [T, 8 * 65], BF16)
    qbT = dscr("qbT", [512, T], BF16)
    kbT = dscr("kbT", [512, T], BF16)
    vb = dscr("vb", [T, 4 * 129], BF16)
    ycat = dscr("ycat", [T, D], BF16)
    xmid = dscr("xmid", [T, D], F32)
    xres = dscr("xres", [T, D], F32)
    n2T = dscr("n2T", [D, T + 2], BF16)

    es = ExitStack()
    S = Sched(nc, es)

    cur = {"tag": "pre"}

    def sb(st, name, shape, dt):
        full = f"{cur['tag']}|{name}"
        return Buf(full, st.enter_context(nc.sbuf_tensor(full.replace("|", "_"), list(shape), dt)))

    def ps(st, name, shape, dt):
        full = f"{cur['tag']}|{name}"
        return Buf(full, st.enter_context(nc.psum_tensor(full.replace("|", "_"), list(shape), dt)))

    def ring(st, name, n, shape, dt, psum=False):
        f = ps if psum else sb
        return Ring([f(st, f"{name}{i}", shape, dt) for i in range(n)])

    ident = sb(es, "ident", [128, 128], BF16)
    lam_t = sb(es, "lam_t", [128, 8], F32)
    dmask_t = sb(es, "dmask_t", [128, 2 * NT], F32)
    cmask_t = sb(es, "cmask_t", [128, 2], F32)

    def load_w(st_ring, dstbuf, K, N, src, c0=0, CH=1024):
        for k in range(K):
            for n0 in range(0, N, CH):
                w = min(CH, N - n0)
                stg = st_ring.next()
                S.dma("sp", stg.t[:, 0:w], src[k * 128:(k + 1) * 128, c0 + n0:c0 + n0 + w], writes=[stg], key=stg.name)
                S.op("pool", lambda e, stg=stg, k=k, n0=n0, w=w: e.tensor_copy(out=dstbuf.t[:, k, n0:n0 + w], in_=stg.t[:, 0:w]),
                     reads=[stg], writes=[dstbuf])

    def rmsnorm_to_T(st, xt, g_t, nT_dst, col0, rings):
        junk, ssr, nbr, tpr = rings
        jk = junk.next()
        ss = ssr.next()
        S.op("dve", lambda e: e.scalar_tensor_tensor(out=jk.t[:], in0=xt.t[:], scalar=1.0, in1=xt.t[:], op0=ALU.mult, op1=ALU.mult,
                                                     accum_out=ss.t[:, 0:1]), reads=[xt], writes=[jk, ss])
        S.op("dve", lambda e: e.tensor_scalar(out=ss.t[:, 1:2], in0=ss.t[:, 0:1], scalar1=1.0 / D, scalar2=EPS, op0=ALU.mult, op1=ALU.add),
             reads=[ss], writes=[ss])
        S.op("act", lambda e: e.activation(out=ss.t[:, 3:4], in_=ss.t[:, 1:2], func=AF.Ln), reads=[ss], writes=[ss])
        S.op("act", lambda e: e.activation(out=ss.t[:, 2:3], in_=ss.t[:, 3:4], func=AF.Exp, scale=-0.5), reads=[ss], writes=[ss])
        nb = nbr.next()
        S.op("dve", lambda e: e.scalar_tensor_tensor(out=nb.t[:], in0=xt.t[:], scalar=ss.t[:, 2:3], in1=g_t.t[:], op0=ALU.mult, op1=ALU.mult),
             reads=[xt, ss, g_t], writes=[nb])
        tp = tpr.next()
        for c in range(8):
            S.op("pe", lambda e, c=c: e.transpose(out=tp.t[:, c, :], in_=nb.t[:, c * 128:(c + 1) * 128], identity=ident.t[:]),
                 reads=[nb, ident], writes=[tp])
        S.op("act", lambda e: e.activation(out=nT_dst.t[:, 0:8, col0:col0 + 128], in_=tp.t[:], func=AF.Copy),
             reads=[tp], writes=[nT_dst])

    S.new_eng_sems("pre")
    zt = sb(es, "zt", [128, 8], BF16)
    S.dma("sp", ident.t[:], ident_in[:, :], writes=[ident], key="c_ident")
    S.dma("sp", dmask_t.t[:], dmask[:, :], writes=[dmask_t], key="c_dmask")
    S.dma("sp", cmask_t.t[:], cmask[:, :], writes=[cmask_t], key="c_cmask")
    S.op("dve", lambda e: e.memset(zt.t[:], 0.0), writes=[zt])
    n2T_v = n2T.rearrange("(c p) t -> p c t", p=128)
    S.dma("sp", n2T_v[:, :, 0:1], zt.t[:, 0:8].rearrange("p (c o) -> p c o", o=1), reads=[zt], key="c_z0", allow_slow_non_contiguous=True)
    S.dma("sp", n2T_v[:, :, T + 1:T + 2], zt.t[:, 0:8].rearrange("p (c o) -> p c o", o=1), reads=[zt], key="c_z1", allow_slow_non_contiguous=True)
    S.flush()

    for l in range(NL):
        S.new_eng_sems(f"L{l}")
        cur["tag"] = f"L{l}"
        x_src = x_in if l == 0 else xres
        lam_init = 0.8 - 0.6 * math.exp(-0.3 * l)
        with ExitStack() as st:
            W = sb(st, "A_W", [128, 8, W_IN_EXT], BF16)
            stg_r = ring(st, "A_stg", 2, [128, 1024], F32)
            g_t = sb(st, "A_g", [128, D], F32)
            x_r = ring(st, "A_x", 2, [128, D], F32)
            junk_r = ring(st, "A_jk", 1, [128, D], BF16)
            ss_r = ring(st, "A_ss", 2, [128, 4], F32)
            nb_r = ring(st, "A_nb", 2, [128, D], BF16)
            tp_r = ring(st, "A_tp", 2, [128, 8, 128], BF16, psum=True)
            nT_r = ring(st, "A_nT", 2, [128, 8, 512], BF16)
            pj_r = ring(st, "A_pj", 4, [128, 512], F32, psum=True)
            pv_r = ring(st, "A_pv", 2, [128, 512], F32, psum=True)
            fo_r = ring(st, "A_fo", 3, [128, 512], BF16)
            rot_r = ring(st, "A_rot", 2, [128, 2, 512], F32)
            t1_r = ring(st, "A_t1", 2, [128, 512], F32)
            t2_r = ring(st, "A_t2", 2, [128, 512], F32)
            va_r = ring(st, "A_va", 2, [128, 8, 65], BF16)
            vb_r = ring(st, "A_vb", 2, [128, 4, 129], BF16)
            lv = sb(st, "A_lv", [128, 256], F32)
            lj = sb(st, "A_lj", [128, 64], F32)

            S.dma("sp", g_t.t[:], g_attn[l], writes=[g_t], key="A_g")
            S.dma("sp", lv.t[:], lamv[l], writes=[lv], key="A_lv")
            S.op("dve", lambda e: e.scalar_tensor_tensor(out=lj.t[:], in0=lv.t[:, 0:64], scalar=1.0, in1=lv.t[:, 64:128], op0=ALU.mult, op1=ALU.mult,
                                                         accum_out=lam_t.t[:, 1:2]), reads=[lv], writes=[lj, lam_t])
            S.op("dve", lambda e: e.scalar_tensor_tensor(out=lj.t[:], in0=lv.t[:, 128:192], scalar=1.0, in1=lv.t[:, 192:256], op0=ALU.mult, op1=ALU.mult,
                                                         accum_out=lam_t.t[:, 2:3]), reads=[lv], writes=[lj, lam_t])
            S.op("act", lambda e: e.activation(out=lam_t.t[:, 3:5], in_=lam_t.t[:, 1:3], func=AF.Exp), reads=[lam_t], writes=[lam_t])
            S.op("dve", lambda e: e.tensor_tensor(out=lam_t.t[:, 5:6], in0=lam_t.t[:, 4:5], in1=lam_t.t[:, 3:4], op=ALU.subtract), reads=[lam_t], writes=[lam_t])
            S.op("dve", lambda e: e.tensor_scalar(out=lam_t.t[:, 0:1], in0=lam_t.t[:, 5:6], scalar1=-lam_init, scalar2=None, op0=ALU.add), reads=[lam_t], writes=[lam_t])
            for bi in range(len(va_r.bufs)):
                b = va_r.bufs[bi]
                S.op("pool", lambda e, b=b: e.memset(b.t[:], 1.0), writes=[b])
                b = vb_r.bufs[bi]
                S.op("pool", lambda e, b=b: e.memset(b.t[:], 1.0), writes=[b])
            load_w(stg_r, W, 8, W_IN_EXT, w_in[l])

            for blk in range(NB):
                t0 = blk * 512
                nT = nT_r.next()
                for tt in range(4):
                    xt = x_r.next()
                    r0 = t0 + tt * 128
                    S.dma("sp", xt.t[:], x_src[r0:r0 + 128, :], writes=[xt], key=xt.name)
                    rmsnorm_to_T(st, xt, g_t, nT, tt * 128, (junk_r, ss_r, nb_r, tp_r))
                rt = rot_r.next()
                S.dma("sp", rt.t[:], rot[:, :, t0:t0 + 512].rearrange("c p t -> p c t"), writes=[rt], key=rt.name)

                def proj(ocol):
                    p = pj_r.next()
                    for k in range(8):
                        S.op("pe", lambda e, k=k, p=p: e.matmul(p.t[:], lhsT=W.t[:, k, ocol:ocol + 128], rhs=nT.t[:, k, :], start=(k == 0), stop=(k == 7)),
                             reads=[W, nT], writes=[p])
                    return p
                for which, dst in ((0, qaT), (1, kaT)):
                    for c in range(4):
                        p = proj(which * 512 + c * 128)
                        fo = fo_r.next()
                        S.op("act", lambda e, p=p, fo=fo: e.activation(out=fo.t[:], in_=p.t[:], func=AF.Copy), reads=[p], writes=[fo])
                        S.dma("pool", dst[c * 128:(c + 1) * 128, t0:t0 + 512], fo.t[:], reads=[fo], key=fo.name)
                for which, dst in ((0, qbT), (1, kbT)):
                    for h in range(4):
                        p1 = proj(1536 + which * 512 + h * 128)
                        p2 = proj(3072 + which * 512 + h * 128)
                        t1 = t1_r.next()
                        t2 = t2_r.next()
                        fo = fo_r.next()
                        S.op("dve", lambda e, p1=p1, t1=t1: e.tensor_tensor(out=t1.t[:], in0=p1.t[:], in1=rt.t[:, 0, :], op=ALU.mult), reads=[p1, rt], writes=[t1])
                        S.op("dve", lambda e, p2=p2, t2=t2: e.tensor_tensor(out=t2.t[:], in0=p2.t[:], in1=rt.t[:, 1, :], op=ALU.mult), reads=[p2, rt], writes=[t2])
                        S.op("pool", lambda e, t1=t1, t2=t2, fo=fo: e.tensor_tensor(out=fo.t[:], in0=t1.t[:], in1=t2.t[:], op=ALU.add), reads=[t1, t2], writes=[fo])
                        S.dma("pool", dst[h * 128:(h + 1) * 128, t0:t0 + 512], fo.t[:], reads=[fo], key=fo.name)
                for tt in range(4):
                    r0 = t0 + tt * 128
                    for which in range(2):
                        pv = pv_r.next()
                        oc = 1024 if which == 0 else 2560
                        for k in range(8):
                            S.op("pe", lambda e, k=k, pv=pv, oc=oc: e.matmul(pv.t[:], lhsT=nT.t[:, k, tt * 128:(tt + 1) * 128], rhs=W.t[:, k, oc:oc + 512],
                                                                              start=(k == 0), stop=(k == 7)), reads=[W, nT], writes=[pv])
                        if which == 0:
                            vt = va_r.next()
                            S.op("act", lambda e, pv=pv, vt=vt: e.activation(out=vt.t[:, :, 0:64], in_=pv.t[:].rearrange("p (h d) -> p h d", d=64), func=AF.Copy),
                                 reads=[pv], writes=[vt])
                            S.dma("pool", va[r0:r0 + 128, :], vt.t[:].rearrange("p h d -> p (h d)"), reads=[vt], key=vt.name)
                        else:
                            vt = vb_r.next()
                            S.op("act", lambda e, pv=pv, vt=vt: e.activation(out=vt.t[:, :, 0:128], in_=pv.t[:].rearrange("p (h d) -> p h d", d=128), func=AF.Copy),
                                 reads=[pv], writes=[vt])
                            S.dma("pool", vb[r0:r0 + 128, :], vt.t[:].rearrange("p h d -> p (h d)"), reads=[vt], key=vt.name)
            S.flush()

        with ExitStack() as st:
            Tf = sb(st, "N_Tf", [128, NAH, 24, 64], BF16)
            rstg = ring(st, "N_rs", 2, [128, 15, 64], F32)
            rm_int = sb(st, "N_rmi", [128, 8, 512], BF16)
            rm_r = ring(st, "N_rm", 2, [128, 8, 512], BF16)
            kT_r = ring(st, "N_kT", 2, [128, 4, 1024], BF16)
            qT_r = ring(st, "N_qT", 2, [128, 4, 512], BF16)
            v_r = ring(st, "N_v", 2, [128, 8, 520], BF16)
            s_r = ring(st, "N_s", 4, [128, 512], F32, psum=True)
            e_r = ring(st, "N_e", 3, [128, 1024], BF16)
            p_r = ring(st, "N_p", 3, [128, 1024], BF16)
            TMi = sb(st, "N_TMi", [128, 8, NAH, 512], BF16)
            acc_r = ring(st, "N_acc", 4, [128, 512], F32, psum=True)
            av = lambda b_: b_.t[:, 0:260].rearrange("p (q c) -> p q c", c=65)
            rc_r = ring(st, "N_rc", 2, [128, 4], F32)
            ya_r = ring(st, "N_ya", 2, [128, 4, 512], BF16)

            S.op("dve", lambda e: e.memset(Tf.t[:], 0.0), writes=[Tf])
            for h in range(NAH):
                for a in range(2):
                    rs = rstg.next()
                    S.dma("sp", rs.t[a * 64:(a + 1) * 64], rpbt[l, h].rearrange("r k q -> k r q"), writes=[rs], key=rs.name)
                    S.op("act", lambda e, rs=rs, h=h, a=a: e.activation(out=Tf.t[a * 64:(a + 1) * 64, h, 4 + a:19 + a, :], in_=rs.t[a * 64:(a + 1) * 64],
                                                                         func=AF.Exp), reads=[rs], writes=[Tf])
            S.dma("sp", rm_int.t[:], rowmask[0].rearrange("j p q -> p j q"), writes=[rm_int], key="N_rmi")
            for j in range(8):
                for h in range(NAH):
                    s0 = 15 - 2 * j
                    S.op("dve", lambda e: e.tensor_tensor(out=TMi.t[:, j, h, :].rearrange("p (b q) -> p b q", q=64), in0=Tf.t[:, h, s0:s0 + 8, :],
                                                          in1=rm_int.t[:, j, :].rearrange("p (b q) -> p b q", q=64), op=ALU.mult), reads=[Tf, rm_int], writes=[TMi])

            for g in range(NG):
                jl = [j for j in range(8) if 0 <= 8 * g - 4 + 2 * j and 8 * g - 4 + 2 * j + 1 < ROWS]
                jlo, jhi = jl[0], jl[-1] + 1
                k0 = (8 * g - 4 + 2 * jlo) * 64
                nk = (jhi - jlo) * 128
                kT = kT_r.next()
                qT = qT_r.next()
                vg = v_r.next()
                S.dma("sp", kT.t[:, :, jlo * 128:jhi * 128], kaT.rearrange("(hp p) t -> p hp t", p=128)[:, :, k0:k0 + nk], writes=[kT], key=kT.name)
                S.dma("sp", qT.t[:], qaT.rearrange("(hp p) t -> p hp t", p=128)[:, :, g * 512:(g + 1) * 512], writes=[qT], key=qT.name)
                S.dma("sp", vg.t[:, jlo:jhi, :], va[k0:k0 + nk, :].rearrange("(j p) c -> p j c", p=128), writes=[vg], key=vg.name)
                if g in SG:
                    rm = rm_r.next()
                    S.dma("sp", rm.t[:], rowmask[1 + SG.index(g)].rearrange("j p q -> p j q"), writes=[rm], key=rm.name)
                else:
                    rm = rm_int
                ya = ya_r.next()
                def n_scores(hp, j):
                    sps = []
                    for a2 in range(2):
                        sp_ = s_r.next()
                        lo = a2 * 64
                        S.op("pe", lambda e: e.matmul(sp_.t[:], lhsT=kT.t[lo:lo + 64, hp, j * 128:(j + 1) * 128], rhs=qT.t[lo:lo + 64, hp, :],
                                                      start=True, stop=True), reads=[kT, qT], writes=[sp_])
                        sps.append(sp_)
                    return sps

                units = [(hp, ji, j) for hp in range(4) for ji, j in enumerate(jl)]
                sps_next = n_scores(units[0][0], units[0][2])
                accs = None
                for ui, (hp, ji, j) in enumerate(units):
                    if ji == 0:
                        accs = [acc_r.next(), acc_r.next()]
                    sps = sps_next
                    if ui + 1 < len(units):
                        sps_next = n_scores(units[ui + 1][0], units[ui + 1][2])
                    et = e_r.next()
                    pt = p_r.next()
                    s0 = 15 - 2 * j
                    for a2 in range(2):
                        sp_ = sps[a2]
                        S.op("act", lambda e: e.activation(out=et.t[:, a2 * 512:(a2 + 1) * 512], in_=sp_.t[:], func=AF.Exp, scale=0.125), reads=[sp_], writes=[et])
                    if rm is rm_int:
                        S.op("dve", lambda e: e.tensor_tensor(out=pt.t[:], in0=et.t[:], in1=TMi.t[:, j, 2 * hp:2 * hp + 2, :].rearrange("p h q -> p (h q)"), op=ALU.mult),
                             reads=[et, TMi], writes=[pt])
                    else:
                        for a2 in range(2):
                            hh = 2 * hp + a2
                            ev = et.t[:, a2 * 512:(a2 + 1) * 512]
                            S.op("dve", lambda e: e.tensor_tensor(out=ev.rearrange("p (b q) -> p b q", q=64), in0=ev.rearrange("p (b q) -> p b q", q=64),
                                                                  in1=Tf.t[:, hh, s0:s0 + 8, :], op=ALU.mult), reads=[et, Tf], writes=[et])
                            S.op("dve", lambda e: e.tensor_tensor(out=pt.t[:, a2 * 512:(a2 + 1) * 512], in0=ev, in1=rm.t[:, j, :], op=ALU.mult), reads=[et, rm], writes=[pt])
                    for a2 in range(2):
                        hh = 2 * hp + a2
                        for qt in range(4):
                            S.op("pe", lambda e: e.matmul(av(accs[a2])[:, qt, :], lhsT=pt.t[:, a2 * 512 + qt * 128:a2 * 512 + (qt + 1) * 128],
                                                          rhs=vg.t[:, j, hh * 65:(hh + 1) * 65], start=(ji == 0 and qt == 0), stop=(ji == len(jl) - 1)),
                                 reads=[pt, vg], writes=[accs[a2]])
                    if ji == len(jl) - 1:
                        for a2 in range(2):
                            hh = 2 * hp + a2
                            rc = rc_r.next()
                            S.op("dve", lambda e: e.reciprocal(out=rc.t[:, 0:4].rearrange("p (q o) -> p q o", o=1), in_=av(accs[a2])[:, :, 64:65]), reads=[accs[a2]], writes=[rc])
                            for qt in range(4):
                                S.op("dve", lambda e: e.tensor_scalar(out=ya.t[:, qt, hh * 64:(hh + 1) * 64], in0=av(accs[a2])[:, qt, 0:64], scalar1=rc.t[:, qt:qt + 1],
                                                                      scalar2=None, op0=ALU.mult), reads=[accs[a2], rc], writes=[ya])
                S.dma("pool", ycat[g * 512:(g + 1) * 512, 0:512].rearrange("(qt p) c -> p qt c", p=128), ya.t[:], reads=[ya], key=ya.name)
            S.flush()

        with ExitStack() as st:
            kT_r = ring(st, "F_kT", 2, [128, T], BF16)
            v_r = ring(st, "F_v", 1, [128, NT, 129], BF16)
            vm_r = ring(st, "F_vm", 2, [128, 2, NT, 129], BF16)
            qT_r = ring(st, "F_qT", 2, [128, 512], BF16)
            s_r = ring(st, "F_s", 4, [128, 512], F32, psum=True)
            p_r = ring(st, "F_p", 6, [128, 512], BF16)
            acc_r = ring(st, "F_acc", 4, [128, 512], F32, psum=True)
            fv = lambda b_: b_.t[:, 0:258].rearrange("p (a c) -> p a c", c=129)
            rc_r = ring(st, "F_rc", 2, [128, 8], F32)
            t_r = ring(st, "F_t", 2, [128, 128], F32)
            yv_r = ring(st, "F_yv", 2, [128, 128], F32)
            jk_r = ring(st, "F_jk", 1, [128, 128], F32)
            yb_r = ring(st, "F_yb", 2, [128, 4, 128], BF16)
            gs = sb(st, "F_gs", [128, 128], F32)
            S.dma("sp", gs.t[:], subln[l], writes=[gs], key="F_gs")
            S.op("dve", lambda e: e.tensor_scalar(out=gs.t[:], in0=gs.t[:], scalar1=(1.0 - lam_init), scalar2=None, op0=ALU.mult), reads=[gs], writes=[gs])
            for h in range(4):
                kT = kT_r.next()
                vh = v_r.next()
                S.dma("sp", kT.t[:], kbT[h * 128:(h + 1) * 128, :], writes=[kT], key=kT.name)
                S.dma("sp", vh.t[:], vb[:, h * 129:(h + 1) * 129].rearrange("(j p) c -> p j c", p=128), writes=[vh], key=vh.name)
                vm = vm_r.next()
                for q2 in range(2):
                    S.op("dve", lambda e: e.tensor_tensor(out=vm.t[:, q2], in0=vh.t[:], in1=dmask_t.t[:, q2 * NT:(q2 + 1) * NT].unsqueeze(2).to_broadcast([128, NT, 129]),
                                                          op=ALU.mult), reads=[vh, dmask_t], writes=[vm])

                def f_scores(qT, kt):
                    sps = []
                    for a2 in range(2):
                        lo = a2 * 64
                        sp_ = s_r.next()
                        S.op("pe", lambda e: e.matmul(sp_.t[:], lhsT=kT.t[lo:lo + 64, kt * 128:(kt + 1) * 128], rhs=qT.t[lo:lo + 64, :], start=True, stop=True),
                             reads=[kT, qT], writes=[sp_])
                        sps.append(sp_)
                    return sps

                for qb in range(NB):
                    qT = qT_r.next()
                    S.dma("sp", qT.t[:], qbT[h * 128:(h + 1) * 128, qb * 512:(qb + 1) * 512], writes=[qT], key=qT.name)
                    accs = [acc_r.next() for _ in range(4)]
                    qhalf = 0 if qb < NB // 2 else 1
                    sps_next = f_scores(qT, 0)
                    for kt in range(NT):
                        sps = sps_next
                        if kt + 1 < NT:
                            sps_next = f_scores(qT, kt + 1)
                        pts = []
                        for a2 in range(2):
                            sp_ = sps[a2]
                            pt = p_r.next()
                            S.op("act", lambda e: e.activation(out=pt.t[:], in_=sp_.t[:], func=AF.Exp, scale=0.125), reads=sps, writes=[pt])
                            pts.append(pt)
                        for a2 in range(2):
                            pt = pts[a2]
                            for qt in range(4):
                                S.op("pe", lambda e: e.matmul(fv(accs[qt])[:, a2, :], lhsT=pt.t[:, qt * 128:(qt + 1) * 128], rhs=vm.t[:, qhalf, kt, :],
                                                              start=(kt == 0 and a2 == 0), stop=(kt == NT - 1)), reads=[pt, vm], writes=[accs[qt]])
                    yb = yb_r.next()
                    for qt in range(4):
                        ac = accs[qt]
                        rc = rc_r.next()
                        tt_ = t_r.next()
                        yv = yv_r.next()
                        jk = jk_r.next()
                        S.op("dve", lambda e, ac=ac, rc=rc: e.reciprocal(out=rc.t[:, 0:2].rearrange("p (q o) -> p q o", o=1), in_=fv(ac)[:, :, 128:129]), reads=[ac], writes=[rc])
                        S.op("dve", lambda e, rc=rc: e.tensor_tensor(out=rc.t[:, 2:3], in0=rc.t[:, 1:2], in1=lam_t.t[:, 0:1], op=ALU.mult), reads=[rc, lam_t], writes=[rc])
                        S.op("dve", lambda e, ac=ac, rc=rc, tt_=tt_: e.tensor_scalar(out=tt_.t[:], in0=fv(ac)[:, 1, 0:128], scalar1=rc.t[:, 2:3], scalar2=None, op0=ALU.mult),
                             reads=[ac, rc], writes=[tt_])
                        S.op("dve", lambda e, ac=ac, rc=rc, tt_=tt_, yv=yv: e.scalar_tensor_tensor(out=yv.t[:], in0=fv(ac)[:, 0, 0:128], scalar=rc.t[:, 0:1], in1=tt_.t[:],
                                                                                                    op0=ALU.mult, op1=ALU.add), reads=[ac, rc, tt_], writes=[yv])
                        S.op("dve", lambda e, yv=yv, jk=jk, rc=rc: e.scalar_tensor_tensor(out=jk.t[:], in0=yv.t[:], scalar=1.0, in1=yv.t[:], op0=ALU.mult, op1=ALU.mult,
                                                                                           accum_out=rc.t[:, 3:4]), reads=[yv], writes=[jk, rc])
                        S.op("dve", lambda e, rc=rc: e.tensor_scalar(out=rc.t[:, 4:5], in0=rc.t[:, 3:4], scalar1=1.0 / 128, scalar2=EPS, op0=ALU.mult, op1=ALU.add), reads=[rc], writes=[rc])
                        S.op("act", lambda e, rc=rc: e.activation(out=rc.t[:, 6:7], in_=rc.t[:, 4:5], func=AF.Ln), reads=[rc], writes=[rc])
                        S.op("act", lambda e, rc=rc: e.activation(out=rc.t[:, 5:6], in_=rc.t[:, 6:7], func=AF.Exp, scale=-0.5), reads=[rc], writes=[rc])
                        S.op("dve", lambda e, yv=yv, rc=rc, qt=qt: e.scalar_tensor_tensor(out=yb.t[:, qt, :], in0=yv.t[:], scalar=rc.t[:, 5:6], in1=gs.t[:], op0=ALU.mult, op1=ALU.mult),
                             reads=[yv, rc, gs], writes=[yb])
                    S.dma("pool", ycat[qb * 512:(qb + 1) * 512, 512 + h * 128:512 + (h + 1) * 128].rearrange("(qt p) c -> p qt c", p=128), yb.t[:], reads=[yb], key=yb.name)
            S.flush()

        with ExitStack() as st:
            W = sb(st, "D_W", [128, 8, D], BF16)
            stg_r = ring(st, "D_stg", 2, [128, 1024], F32)
            g_t = sb(st, "D_g", [128, D], F32)
            y_r = ring(st, "D_y", 3, [128, D], BF16)
            x_r = ring(st, "D_x", 4, [128, D], F32)
            xm_r = ring(st, "D_xm", 4, [128, D], F32)
            yT_r = ring(st, "D_yT", 3, [128, 8, 128], BF16)
            tpa_r = ring(st, "D_tpa", 2, [128, 8, 128], BF16, psum=True)
            tpb_r = ring(st, "D_tpb", 2, [128, 8, 128], BF16, psum=True)
            po_r = ring(st, "D_po", 4, [128, 512], F32, psum=True)
            junk_r = ring(st, "D_jk", 2, [128, D], BF16)
            ss_r = ring(st, "D_ss", 4, [128, 4], F32)
            nb_r = ring(st, "D_nb", 3, [128, D], BF16)
            nT_r = ring(st, "D_nT", 3, [128, 8, 128], BF16)
            S.dma("sp", g_t.t[:], g_ffn[l], writes=[g_t], key="D_g")
            load_w(stg_r, W, 8, D, w_out[l])
            ctx = {}

            def d0(i):
                c = ctx[i] = {}
                r0 = i * 128
                c["yt"] = yt = y_r.next()
                c["xt"] = xt = x_r.next()
                S.dma("sp", yt.t[:], ycat[r0:r0 + 128, :], writes=[yt], key=yt.name)
                S.dma("sp", xt.t[:], x_src[r0:r0 + 128, :], writes=[xt], key=xt.name)

            def d1(i):
                c = ctx[i]
                yt = c["yt"]
                tp = tpa_r.next()
                for cc in range(8):
                    S.op("pe", lambda e: e.transpose(out=tp.t[:, cc, :], in_=yt.t[:, cc * 128:(cc + 1) * 128], identity=ident.t[:]), reads=[yt, ident], writes=[tp])
                c["yT"] = yT = yT_r.next()
                S.op("act", lambda e: e.activation(out=yT.t[:], in_=tp.t[:], func=AF.Copy), reads=[tp], writes=[yT])

            def d2(i):
                c = ctx[i]
                yT, xt = c["yT"], c["xt"]
                r0 = i * 128
                c["xm"] = xm = xm_r.next()
                for half in range(2):
                    po = po_r.next()
                    for k in range(8):
                        S.op("pe", lambda e: e.matmul(po.t[:], lhsT=yT.t[:, k, :], rhs=W.t[:, k, half * 512:(half + 1) * 512], start=(k == 0), stop=(k == 7)),
                             reads=[yT, W], writes=[po])
                    S.op("dve", lambda e: e.tensor_tensor(out=xm.t[:, half * 512:(half + 1) * 512], in0=po.t[:], in1=xt.t[:, half * 512:(half + 1) * 512], op=ALU.add),
                         reads=[po, xt], writes=[xm])
                S.dma("pool", xmid[r0:r0 + 128, :], xm.t[:], reads=[xm], key=xm.name)
                jk = junk_r.next()
                c["ss"] = ss = ss_r.next()
                S.op("dve", lambda e: e.scalar_tensor_tensor(out=jk.t[:], in0=xm.t[:], scalar=1.0, in1=xm.t[:], op0=ALU.mult, op1=ALU.mult,
                                                             accum_out=ss.t[:, 0:1]), reads=[xm], writes=[jk, ss])
                S.op("dve", lambda e: e.tensor_scalar(out=ss.t[:, 1:2], in0=ss.t[:, 0:1], scalar1=1.0 / D, scalar2=EPS, op0=ALU.mult, op1=ALU.add),
                     reads=[ss], writes=[ss])

            def d3(i):
                ss = ctx[i]["ss"]
                S.op("act", lambda e: e.activation(out=ss.t[:, 3:4], in_=ss.t[:, 1:2], func=AF.Ln), reads=[ss], writes=[ss])
                S.op("act", lambda e: e.activation(out=ss.t[:, 2:3], in_=ss.t[:, 3:4], func=AF.Exp, scale=-0.5), reads=[ss], writes=[ss])

            def d4a(i):
                c = ctx[i]
                xm, ss = c["xm"], c["ss"]
                c["nb"] = nb = nb_r.next()
                S.op("dve", lambda e: e.scalar_tensor_tensor(out=nb.t[:], in0=xm.t[:], scalar=ss.t[:, 2:3], in1=g_t.t[:], op0=ALU.mult, op1=ALU.mult),
                     reads=[xm, ss, g_t], writes=[nb])

            def d4b(i):
                c = ctx[i]
                nb = c["nb"]
                c["tp2"] = tp = tpb_r.next()
                for cc in range(8):
                    S.op("pe", lambda e: e.transpose(out=tp.t[:, cc, :], in_=nb.t[:, cc * 128:(cc + 1) * 128], identity=ident.t[:]), reads=[nb, ident], writes=[tp])

            def d5(i):
                c = ctx[i]
                tp = c["tp2"]
                r0 = i * 128
                nT = nT_r.next()
                S.op("act", lambda e: e.activation(out=nT.t[:], in_=tp.t[:], func=AF.Copy), reads=[tp], writes=[nT])
                S.dma("pool", n2T_v[:, :, 1 + r0:1 + r0 + 128], nT.t[:], reads=[nT], key=nT.name)
                del ctx[i]

            stages = [d0, d1, d2, d3, d4a, d4b, d5]
            lag = [0, 1, 2, 3, 4, 4, 5]
            order = [4, 6, 3, 2, 1, 5, 0]
            for tstep in range(NT + 6):
                for s_ in order:
                    i = tstep - lag[s_]
                    if 0 <= i < NT:
                        stages[s_](i)
            S.flush()

        last = (l == NL - 1)
        with ExitStack() as st:
            Wu = sb(st, "E_Wu", [128, 8, 2 * DFF], BF16)
            Wd = sb(st, "E_Wd", [128, NFC, D], BF16)
            stg_r = ring(st, "E_stg", 2, [128, 512], F32)
            cw = sb(st, "E_cw", [128, 4 * NFC], F32)
            nT_r = ring(st, "E_nT", 1, [128, 8, 514], BF16)
            hT = sb(st, "E_hT", [128, NFC, 512], BF16)
            pg_r = ring(st, "E_pg", 2, [128, 512], F32, psum=True)
            pvv_r = ring(st, "E_pvv", 2, [128, 512], F32, psum=True)
            ph_r = ring(st, "E_ph", 2, [128, 2], F32, psum=True)
            pd_r = ring(st, "E_pd", 2, [128, 512], F32, psum=True)
            gx_r = ring(st, "E_gx", 2, [128, 514], F32)
            a_r = ring(st, "E_a", 2, [128, 512], F32)
            gl_r = ring(st, "E_gl", 2, [128, 512], F32)
            xm_r = ring(st, "E_xm", 2, [128, D], F32)
            xo_r = ring(st, "E_xo", 1, [128, D], F32)
            S.dma("sp", cw.t[:], convw[l], writes=[cw], key="E_cw")
            if last:
                g_t = sb(st, "E_g", [128, D], F32)
                jk_r = ring(st, "E_jk", 1, [128, D], BF16)
                ss_r = ring(st, "E_ss", 2, [128, 4], F32)
                yo_r = ring(st, "E_yo", 1, [128, D], F32)
                S.dma("sp", g_t.t[:], g_final[:, :], writes=[g_t], key="E_g")
            load_w(stg_r, Wu, 8, 2 * DFF, w_up[l], CH=512)
            load_w(stg_r, Wd, NFC, D, w_down[l], CH=512)
            for blk in range(NB):
                t0 = blk * 512
                nT = nT_r.next()
                S.dma("sp", nT.t[:], n2T_v[:, :, t0:t0 + 514], writes=[nT], key=nT.name)
                if NB >= 2 and blk == NB // 2:
                    S.op("dve", lambda e, nT=nT: e.tensor_scalar(out=nT.t[:, :, 0:1], in0=nT.t[:, :, 0:1], scalar1=cmask_t.t[:, 0:1], scalar2=None, op0=ALU.mult), reads=[nT, cmask_t], writes=[nT])
                if NB >= 2 and blk == NB // 2 - 1:
                    S.op("dve", lambda e, nT=nT: e.tensor_scalar(out=nT.t[:, :, 513:514], in0=nT.t[:, :, 513:514], scalar1=cmask_t.t[:, 1:2], scalar2=None, op0=ALU.mult), reads=[nT, cmask_t], writes=[nT])
                for fc in range(NFC):
                    pg = pg_r.next()
                    ph = ph_r.next()
                    pvv = pvv_r.next()
                    for k in range(8):
                        S.op("pe", lambda e, k=k, pg=pg, fc=fc, nT=nT: e.matmul(pg.t[:], lhsT=Wu.t[:, k, fc * 128:(fc + 1) * 128], rhs=nT.t[:, k, 1:513], start=(k == 0), stop=(k == 7)),
                             reads=[Wu, nT], writes=[pg])
                    for k in range(8):
                        S.op("pe", lambda e, k=k, ph=ph, fc=fc, nT=nT: e.matmul(ph.t[:], lhsT=Wu.t[:, k, fc * 128:(fc + 1) * 128], rhs=nT.t[:, k, 0:514:513], start=(k == 0), stop=(k == 7)),
                             reads=[Wu, nT], writes=[ph])
                    for k in range(8):
                        S.op("pe", lambda e, k=k, pvv=pvv, fc=fc, nT=nT: e.matmul(pvv.t[:], lhsT=Wu.t[:, k, DFF + fc * 128:DFF + (fc + 1) * 128], rhs=nT.t[:, k, 1:513], start=(k == 0), stop=(k == 7)),
                             reads=[Wu, nT], writes=[pvv])
                    gx = gx_r.next()
                    S.op("act", lambda e, gx=gx, pg=pg: e.activation(out=gx.t[:, 1:513], in_=pg.t[:], func=AF.Copy), reads=[pg], writes=[gx])
                    S.op("act", lambda e, gx=gx, ph=ph: e.activation(out=gx.t[:, 0:514:513], in_=ph.t[:], func=AF.Copy), reads=[ph], writes=[gx])
                    a = a_r.next()
                    S.op("dve", lambda e, a=a, gx=gx, fc=fc: e.tensor_scalar(out=a.t[:], in0=gx.t[:, 1:513], scalar1=cw.t[:, NFC + fc:NFC + fc + 1], scalar2=cw.t[:, 3 * NFC + fc:3 * NFC + fc + 1],
                                                                          op0=ALU.mult, op1=ALU.add), reads=[gx, cw], writes=[a])
                    S.op("dve", lambda e, a=a, gx=gx, fc=fc: e.scalar_tensor_tensor(out=a.t[:], in0=gx.t[:, 0:512], scalar=cw.t[:, fc:fc + 1], in1=a.t[:], op0=ALU.mult, op1=ALU.add),
                         reads=[gx, cw, a], writes=[a])
                    S.op("dve", lambda e, a=a, gx=gx, fc=fc: e.scalar_tensor_tensor(out=a.t[:], in0=gx.t[:, 2:514], scalar=cw.t[:, 2 * NFC + fc:2 * NFC + fc + 1], in1=a.t[:], op0=ALU.mult, op1=ALU.add),
                         reads=[gx, cw, a], writes=[a])
                    gl = gl_r.next()
                    S.op("act", lambda e, a=a, gl=gl: e.activation(out=gl.t[:], in_=a.t[:], func=AF.Gelu), reads=[a], writes=[gl])
                    S.op("dve", lambda e, gl=gl, pvv=pvv, fc=fc: e.tensor_tensor(out=hT.t[:, fc, :], in0=pvv.t[:], in1=gl.t[:], op=ALU.mult), reads=[gl, pvv], writes=[hT])
                for tt in range(4):
                    r0 = t0 + tt * 128
                    xm = xm_r.next()
                    S.dma("sp", xm.t[:], xmid[r0:r0 + 128, :], writes=[xm], key=xm.name)
                    xo = xo_r.next()
                    for half in range(2):
                        pd = pd_r.next()
                        for fc in range(NFC):
                            S.op("pe", lambda e, fc=fc, pd=pd, tt=tt, half=half: e.matmul(pd.t[:], lhsT=hT.t[:, fc, tt * 128:(tt + 1) * 128], rhs=Wd.t[:, fc, half * 512:(half + 1) * 512],
                                                                                       start=(fc == 0), stop=(fc == NFC - 1)), reads=[hT, Wd], writes=[pd])
                        S.op("dve", lambda e, pd=pd, xm=xm, xo=xo, half=half: e.tensor_tensor(out=xo.t[:, half * 512:(half + 1) * 512], in0=pd.t[:], in1=xm.t[:, half * 512:(half + 1) * 512], op=ALU.add),
                             reads=[pd, xm], writes=[xo])
                    if not last:
                        S.dma("pool", xres[r0:r0 + 128, :], xo.t[:], reads=[xo], key=xo.name)
                    else:
                        jk = jk_r.next()
                        ss = ss_r.next()
                        yo = yo_r.next()
                        S.op("dve", lambda e, jk=jk, ss=ss, xo=xo: e.scalar_tensor_tensor(out=jk.t[:], in0=xo.t[:], scalar=1.0, in1=xo.t[:], op0=ALU.mult, op1=ALU.mult, accum_out=ss.t[:, 0:1]),
                             reads=[xo], writes=[jk, ss])
                        S.op("dve", lambda e, ss=ss: e.tensor_scalar(out=ss.t[:, 1:2], in0=ss.t[:, 0:1], scalar1=1.0 / D, scalar2=EPS, op0=ALU.mult, op1=ALU.add), reads=[ss], writes=[ss])
                        S.op("act", lambda e, ss=ss: e.activation(out=ss.t[:, 3:4], in_=ss.t[:, 1:2], func=AF.Ln), reads=[ss], writes=[ss])
                        S.op("act", lambda e, ss=ss: e.activation(out=ss.t[:, 2:3], in_=ss.t[:, 3:4], func=AF.Exp, scale=-0.5), reads=[ss], writes=[ss])
                        S.op("dve", lambda e, ss=ss, xo=xo, yo=yo: e.scalar_tensor_tensor(out=yo.t[:], in0=xo.t[:], scalar=ss.t[:, 2:3], in1=g_t.t[:], op0=ALU.mult, op1=ALU.mult),
                             reads=[xo, ss, g_t], writes=[yo])
                        S.dma("pool", y_out[r0:r0 + 128, :], yo.t[:], reads=[yo], key=yo.name)
            S.flush()
    es.close()
    return nc


def _rowmask_tables(T, seq_len):
    ROWS = T // 64
    NG = T // 512
    SG = sorted(set([0, NG // 2 - 1, NG // 2, NG - 1]))
    R = seq_len // 64

    def table(g):
        m = np.zeros((8, 128, 512), np.float32)
        for j in range(8):
            for a in range(2):
                kr = 8 * g - 4 + 2 * j + a
                if kr < 0 or kr >= ROWS:
                    continue
                for b in range(8):
                    qr = 8 * g + b
                    if kr // R != qr // R:
                        continue
                    qs = qr % R
                    start = min(max(qs - 4, 0), R - 8)
                    ks = kr % R
                    if start <= ks < start + 8:
                        m[j, a * 64:(a + 1) * 64, b * 64:(b + 1) * 64] = 1.0
        return m
    mi = np.zeros((8, 128, 512), np.float32)
    for j in range(8):
        for a in range(2):
            for b in range(8):
                dr = 2 * j + a - b - 4
                if -4 <= dr <= 3:
                    mi[j, a * 64:(a + 1) * 64, b * 64:(b + 1) * 64] = 1.0
    tabs = [mi] + [table(g) for g in SG]
    return np.stack(tabs).astype(ml_dtypes.bfloat16)


def _rot_tables(T, seq_len):
    pos = (np.arange(T) % seq_len).astype(np.float32)
    inv = (1.0 / (10000.0 ** (np.arange(0, 64, 2, dtype=np.float32) / 64.0))).astype(np.float32)
    ang = pos[None, :] * inv[:, None]
    cos = np.cos(ang).astype(np.float32)
    sin = np.sin(ang).astype(np.float32)
    cos64 = np.concatenate([cos, cos], 0)
    sin64 = np.concatenate([-sin, sin], 0)
    return np.stack([np.concatenate([cos64, cos64], 0), np.concatenate([sin64, sin64], 0)]).astype(np.float32)


def _prep_shared(inp, NL):
    w_in = np.asarray(inp["w_in"], np.float32)
    perm = np.arange(1024).reshape(8, 2, 64)
    perm = np.concatenate([perm[..., 32:], perm[..., :32]], -1).reshape(-1)
    qb = w_in[:, :, 1536:2560]
    w_ext = np.concatenate([w_in, qb[:, :, perm]], axis=2)
    rep = lambda a: np.ascontiguousarray(np.broadcast_to(np.asarray(a, np.float32)[:, None, :], (a.shape[0], 128, a.shape[1])))
    lamv = np.concatenate([np.asarray(inp[k], np.float32) for k in ("lam_q1", "lam_k1", "lam_q2", "lam_k2")], axis=1)
    cw = np.concatenate([np.asarray(inp["conv_w"], np.float32), np.asarray(inp["conv_b"], np.float32)[:, None, :]], axis=1)
    cw = cw.reshape(NL, 4, NFC, 128).transpose(0, 3, 1, 2).reshape(NL, 128, 4 * NFC)
    rpb = np.asarray(inp["rpb"], np.float32)
    kc = np.arange(64)[:, None]
    qc = np.arange(64)[None, :]
    dc = kc - qc + 15
    cs = np.clip(qc - 8, 0, 48)
    ok = (kc >= cs) & (kc < cs + 16)
    dcc = np.clip(dc, 0, 30)
    rp = rpb[:, :, ::-1, :][:, :, :, dcc]
    rp = np.where(ok[None, None, None], rp, np.float32(NEG)).astype(np.float32)
    return dict(
        w_in=np.ascontiguousarray(w_ext), w_out=np.asarray(inp["w_out"], np.float32), w_up=np.asarray(inp["w_up"], np.float32),
        w_down=np.asarray(inp["w_down"], np.float32), g_attn=rep(inp["g_attn"]), g_ffn=rep(inp["g_ffn"]),
        g_final=np.ascontiguousarray(np.broadcast_to(np.asarray(inp["g_final"], np.float32)[None, :], (128, D))),
        subln=rep(inp["subln_g"]), lamv=rep(lamv), convw=np.ascontiguousarray(cw), rpbt=np.ascontiguousarray(rp),
        ident=np.eye(128, dtype=np.float32).astype(ml_dtypes.bfloat16),
    )


def _prep_core(T, seq_len):
    NT = T // 128
    dm = np.ones((2, NT), np.float32)
    if seq_len < T:
        dm[0, NT // 2:] = 0.0
        dm[1, :NT // 2] = 0.0
    dmask = np.ascontiguousarray(np.broadcast_to(dm.reshape(1, -1), (128, 2 * NT)))
    cm = np.full((128, 2), 1.0 if seq_len == T else 0.0, np.float32)
    return dict(rot=_rot_tables(T, seq_len), rowmask=_rowmask_tables(T, seq_len), dmask=dmask, cmask=cm)


_PROG_CACHE = {}


def run_cores(inp, xs, seqlens, T, NL, n_cores):
    key = (T, NL)
    if key not in _PROG_CACHE:
        _PROG_CACHE[key] = build_program(T, NL)
    nc = _PROG_CACHE[key]
    shared = _prep_shared(inp, NL)
    per_type = {}
    in_maps = []
    for x, sl in zip(xs, seqlens):
        if sl not in per_type:
            per_type[sl] = _prep_core(T, sl)
        m = dict(shared)
        m.update(per_type[sl])
        m["x"] = np.ascontiguousarray(x, dtype=np.float32)
        in_maps.append(m)
    res = run_bass_kernel_spmd(nc, in_maps, core_ids=list(range(n_cores)))
    return [r["y_out"] for r in res.results]


def kernel(x_prompt, x_sample, g_attn, w_in, rpb, lam_q1, lam_k1, lam_q2, lam_k2, subln_g, w_out,
           g_ffn, w_up, conv_w, conv_b, w_down, g_final):
    inp = dict(g_attn=g_attn, w_in=w_in, rpb=rpb, lam_q1=lam_q1, lam_k1=lam_k1, lam_q2=lam_q2, lam_k2=lam_k2,
               subln_g=subln_g, w_out=w_out, g_ffn=g_ffn, w_up=w_up, conv_w=conv_w, conv_b=conv_b, w_down=w_down, g_final=g_final)
    inp = {k: np.asarray(v) for k, v in inp.items()}
    xp = np.asarray(x_prompt, np.float32)
    xs_ = np.asarray(x_sample, np.float32)
    T = 8192
    xs = [xs_[0], xs_[1], xp[0:2].reshape(T, D), xp[2:4].reshape(T, D)]
    sl = [8192, 8192, 4096, 4096]
    outs = run_cores(inp, xs + xs, sl + sl, T, 4, 8)
    y_sample = np.stack([outs[0], outs[1]]).astype(np.float32)
    y_prompt = np.concatenate([outs[2].reshape(2, 4096, D), outs[3].reshape(2, 4096, D)], 0).astype(np.float32)
    return (y_prompt, y_sample)
```

```python
import math
from contextlib import ExitStack

import numpy as np
import ml_dtypes

import concourse.bass as bass
import concourse.mybir as mybir
from concourse.bass_utils import run_bass_kernel_spmd

F32 = mybir.dt.float32
BF16 = mybir.dt.bfloat16
AF = mybir.ActivationFunctionType
ALU = mybir.AluOpType

D = 1024
NAH = 8
DFF = 2816
NFC = DFF // 128
EPS = 1e-6
NEG = -30000.0
W_IN_EXT = 4096
SAME_ENGINE_SYNC = True


class Buf:
    def __init__(self, name, t):
        self.name = name
        self.t = t


class Ring:
    def __init__(self, bufs):
        self.bufs = bufs
        self.i = 0

    def next(self):
        b = self.bufs[self.i % len(self.bufs)]
        self.i += 1
        return b


class _Rec:
    def __getattr__(self, name):
        def f(*a, **k):
            self.call = (name, a, k)
            return self
        return f


class Sched:
    ENGS = ("pe", "act", "dve", "pool", "sp")

    def __init__(self, nc, es):
        self.nc = nc
        self.es = es
        self.ops = []
        self.state = {}
        self.dma_sems = {}
        self.dma_cnt = {}
        self.eng_sems = None
        self.eng_cnt = None

    def new_eng_sems(self, tag):
        self.eng_sems = {e: self.es.enter_context(self.nc.semaphore(f"s_{tag}_{e}")) for e in self.ENGS}
        self.eng_cnt = {e: 0 for e in self.ENGS}

    def I(self, eng, meth, reads=(), writes=(), dma_key=None, **kw):
        return self.op(eng, (meth, (), kw), reads, writes, dma_key)

    def op(self, eng, fn, reads=(), writes=(), dma_key=None):
        if callable(fn):
            rec = _Rec()
            fn(rec)
            fn = rec.call
        oid = len(self.ops)
        deps = set()
        for r in reads:
            st = self.state.setdefault(r.name, {"w": {}, "r": {}})
            deps.update(st["w"].values())
        for w in writes:
            st = self.state.setdefault(w.name, {"w": {}, "r": {}})
            deps.update(st["w"].values())
            deps.update(st["r"].values())
        evkey = ("dma", dma_key) if dma_key is not None else eng
        for r in reads:
            self.state[r.name]["r"][evkey] = oid
        for w in writes:
            st = self.state[w.name]
            st["r"] = {}
            st["w"] = {evkey: oid}
        self.ops.append(dict(eng=eng, fn=fn, deps=deps, dma_key=dma_key, sig=False, val=None))
        return oid

    def dma(self, eng, out, in_, reads=(), writes=(), key=None, **kw):
        assert key is not None
        return self.op(eng, ("dma_start", (), dict(out=out, in_=in_, **kw)), reads, writes, dma_key=key.split("|")[-1])

    def flush(self):
        nc = self.nc
        ops = self.ops
        for o in ops:
            need = set()
            for d in o["deps"]:
                p = ops[d]
                if p["dma_key"] is not None:
                    need.add(d)
                elif p["eng"] != o["eng"]:
                    need.add(d)
                    p["sig"] = True
                elif SAME_ENGINE_SYNC and p["eng"] in ("act", "dve", "pool"):
                    need.add(d)
                    p["sig"] = True
            o["need"] = need
        for o in ops:
            if o["dma_key"] is not None:
                k = o["dma_key"]
                if k not in self.dma_sems:
                    self.dma_sems[k] = self.es.enter_context(nc.semaphore(f"d_{k}"))
                    self.dma_cnt[k] = 0
                self.dma_cnt[k] += 16
                o["val"] = (self.dma_sems[k], self.dma_cnt[k])
            elif o["sig"]:
                self.eng_cnt[o["eng"]] += 1
                o["val"] = (self.eng_sems[o["eng"]], self.eng_cnt[o["eng"]])
        per = {e: [] for e in self.ENGS}
        for o in ops:
            per[o["eng"]].append(o)

        def emit(eng_name, lst):
            def body(e):
                seen = {}
                issued = {}
                for o in lst:
                    waits = {}
                    for d in o["need"]:
                        sem, v = ops[d]["val"]
                        kk = id(sem)
                        if seen.get(kk, 0) >= v:
                            continue
                        if kk not in waits or waits[kk][1] < v:
                            waits[kk] = (sem, v)
                    for kk, (sem, v) in waits.items():
                        e.wait_ge(sem, v)
                        seen[kk] = v
                    meth, ar, kw = o["fn"]
                    ins = getattr(e, meth)(*ar, **kw)
                    if o["dma_key"] is not None:
                        sem, v = o["val"]
                        ins.then_inc(sem, 16)
                        issued[id(sem)] = (sem, v)
                    elif o["sig"]:
                        ins.then_inc(o["val"][0], 1)
                for kk, (sem, v) in issued.items():
                    e.wait_ge(sem, v)
            return body

        with nc.Block() as block:
            if per["sp"]:
                block.sync(emit("sp", per["sp"]))
            if per["pe"]:
                block.tensor(emit("pe", per["pe"]))
            if per["act"]:
                block.scalar(emit("act", per["act"]))
            if per["dve"]:
                block.vector(emit("dve", per["dve"]))
            if per["pool"]:
                block.gpsimd(emit("pool", per["pool"]))
        self.ops = []
        self.state = {}


def build_program(T, NL, n_types_dummy=None):
    nc = bass.Bass("TRN2", target_bir_lowering=False)
    NT = T // 128
    NB = T // 512
    ROWS = T // 64
    NG = NB
    SG = sorted(set([0, NG // 2 - 1, NG // 2, NG - 1]))

    def din(name, shape, dt=F32):
        return nc.dram_tensor(name, list(shape), dt, kind="ExternalInput").ap()

    x_in = din("x", [T, D])
    w_in = din("w_in", [NL, D, W_IN_EXT])
    w_out = din("w_out", [NL, D, D])
    w_up = din("w_up", [NL, D, 2 * DFF])
    w_down = din("w_down", [NL, DFF, D])
    g_attn = din("g_attn", [NL, 128, D])
    g_ffn = din("g_ffn", [NL, 128, D])
    g_final = din("g_final", [128, D])
    subln = din("subln", [NL, 128, 128])
    lamv = din("lamv", [NL, 128, 256])
    convw = din("convw", [NL, 128, 4 * NFC])
    rpbt = din("rpbt", [NL, NAH, 15, 64, 64])
    rot = din("rot", [2, 128, T])
    rowmask = din("rowmask", [len(SG) + 1, 8, 128, 512], BF16)
    dmask = din("dmask", [128, 2 * (T // 128)])
    cmask = din("cmask", [128, 2])
    ident_in = din("ident", [128, 128], BF16)
    y_out = nc.dram_tensor("y_out", [T, D], F32, kind="ExternalOutput").ap()

    def dscr(name, shape, dt):
        return nc.dram_tensor(name, list(shape), dt).ap()

    qaT = dscr("qaT", [512, T], BF16)
    kaT = dscr("kaT", [512, T], BF16)
    va = dscr("va", [T, 8 * 65], BF16)
    qbT = dscr("qbT", [512, T], BF16)
    kbT = dscr("kbT", [512, T], BF16)
    vb = dscr("vb", [T, 4 * 129], BF16)
    ycat = dscr("ycat", [T, D], BF16)
    xmid = dscr("xmid", [T, D], F32)
    xres = dscr("xres", [T, D], F32)
    n2T = dscr("n2T", [D, T + 2], BF16)

    es = ExitStack()
    S = Sched(nc, es)

    cur = {"tag": "pre"}

    def sb(st, name, shape, dt):
        full = f"{cur['tag']}|{name}"
        return Buf(full, st.enter_context(nc.sbuf_tensor(full.replace("|", "_"), list(shape), dt)))

    def ps(st, name, shape, dt):
        full = f"{cur['tag']}|{name}"
        return Buf(full, st.enter_context(nc.psum_tensor(full.replace("|", "_"), list(shape), dt)))

    def ring(st, name, n, shape, dt, psum=False):
        f = ps if psum else sb
        return Ring([f(st, f"{name}{i}", shape, dt) for i in range(n)])

    ident = sb(es, "ident", [128, 128], BF16)
    lam_t = sb(es, "lam_t", [128, 8], F32)
    dmask_t = sb(es, "dmask_t", [128, 2 * NT], F32)
    cmask_t = sb(es, "cmask_t", [128, 2], F32)

    def load_w(st_ring, dstbuf, K, N, src, c0=0, CH=1024, order=None):
        parts = {}
        for n0 in (order if order is not None else range(0, N, CH)):
            w = min(CH, N - n0)
            part = Buf(f"{dstbuf.name}#c{n0}", dstbuf.t)
            parts[n0] = part
            for k in range(K):
                stg = st_ring.next()
                S.dma("sp", stg.t[:, 0:w], src[k * 128:(k + 1) * 128, c0 + n0:c0 + n0 + w], writes=[stg], key=stg.name)
                S.op("pool", lambda e: e.tensor_copy(out=dstbuf.t[:, k, n0:n0 + w], in_=stg.t[:, 0:w]), reads=[stg], writes=[part])
        return parts

    def rmsnorm_to_T(st, xt, g_t, nT_dst, col0, rings):
        junk, ssr, nbr, tpr = rings
        jk = junk.next()
        ss = ssr.next()
        S.op("dve", lambda e: e.scalar_tensor_tensor(out=jk.t[:], in0=xt.t[:], scalar=1.0, in1=xt.t[:], op0=ALU.mult, op1=ALU.mult,
                                                     accum_out=ss.t[:, 0:1]), reads=[xt], writes=[jk, ss])
        S.op("dve", lambda e: e.tensor_scalar(out=ss.t[:, 1:2], in0=ss.t[:, 0:1], scalar1=1.0 / D, scalar2=EPS, op0=ALU.mult, op1=ALU.add),
             reads=[ss], writes=[ss])
        S.op("act", lambda e: e.activation(out=ss.t[:, 3:4], in_=ss.t[:, 1:2], func=AF.Ln), reads=[ss], writes=[ss])
        S.op("act", lambda e: e.activation(out=ss.t[:, 2:3], in_=ss.t[:, 3:4], func=AF.Exp, scale=-0.5), reads=[ss], writes=[ss])
        nb = nbr.next()
        S.op("dve", lambda e: e.scalar_tensor_tensor(out=nb.t[:], in0=xt.t[:], scalar=ss.t[:, 2:3], in1=g_t.t[:], op0=ALU.mult, op1=ALU.mult),
             reads=[xt, ss, g_t], writes=[nb])
        tp = tpr.next()
        for c in range(8):
            S.op("pe", lambda e, c=c: e.transpose(out=tp.t[:, c, :], in_=nb.t[:, c * 128:(c + 1) * 128], identity=ident.t[:]),
                 reads=[nb, ident], writes=[tp])
        S.op("act", lambda e: e.activation(out=nT_dst.t[:, 0:8, col0:col0 + 128], in_=tp.t[:], func=AF.Copy),
             reads=[tp], writes=[nT_dst])

    S.new_eng_sems("pre")
    zt = sb(es, "zt", [128, 8], BF16)
    S.dma("sp", ident.t[:], ident_in[:, :], writes=[ident], key="c_ident")
    S.dma("sp", dmask_t.t[:], dmask[:, :], writes=[dmask_t], key="c_dmask")
    S.dma("sp", cmask_t.t[:], cmask[:, :], writes=[cmask_t], key="c_cmask")
    S.op("dve", lambda e: e.memset(zt.t[:], 0.0), writes=[zt])
    n2T_v = n2T.rearrange("(c p) t -> p c t", p=128)
    S.dma("sp", n2T_v[:, :, 0:1], zt.t[:, 0:8].rearrange("p (c o) -> p c o", o=1), reads=[zt], key="c_z0", allow_slow_non_contiguous=True)
    S.dma("sp", n2T_v[:, :, T + 1:T + 2], zt.t[:, 0:8].rearrange("p (c o) -> p c o", o=1), reads=[zt], key="c_z1", allow_slow_non_contiguous=True)
    S.flush()

    for l in range(NL):
        S.new_eng_sems(f"L{l}")
        cur["tag"] = f"L{l}"
        x_src = x_in if l == 0 else xres
        lam_init = 0.8 - 0.6 * math.exp(-0.3 * l)
        with ExitStack() as st:
            W = sb(st, "A_W", [128, 8, W_IN_EXT], BF16)
            stg_r = ring(st, "A_stg", 2, [128, 1024], F32)
            g_t = sb(st, "A_g", [128, D], F32)
            x_r = ring(st, "A_x", 2, [128, D], F32)
            junk_r = ring(st, "A_jk", 1, [128, D], BF16)
            ss_r = ring(st, "A_ss", 2, [128, 4], F32)
            nb_r = ring(st, "A_nb", 2, [128, D], BF16)
            tp_r = ring(st, "A_tp", 2, [128, 8, 128], BF16, psum=True)
            nT_r = ring(st, "A_nT", 2, [128, 8, 512], BF16)
            pj_r = ring(st, "A_pj", 4, [128, 512], F32, psum=True)
            pv_r = ring(st, "A_pv", 2, [128, 512], F32, psum=True)
            fo_r = ring(st, "A_fo", 3, [128, 512], BF16)
            rot_r = ring(st, "A_rot", 2, [128, 2, 512], F32)
            t1_r = ring(st, "A_t1", 2, [128, 512], F32)
            t2_r = ring(st, "A_t2", 2, [128, 512], F32)
            va_r = ring(st, "A_va", 2, [128, 8, 65], BF16)
            vb_r = ring(st, "A_vb", 2, [128, 4, 129], BF16)
            lv = sb(st, "A_lv", [128, 256], F32)
            lj = sb(st, "A_lj", [128, 64], F32)

            S.dma("sp", g_t.t[:], g_attn[l], writes=[g_t], key="A_g")
            S.dma("sp", lv.t[:], lamv[l], writes=[lv], key="A_lv")
            S.op("dve", lambda e: e.scalar_tensor_tensor(out=lj.t[:], in0=lv.t[:, 0:64], scalar=1.0, in1=lv.t[:, 64:128], op0=ALU.mult, op1=ALU.mult,
                                                         accum_out=lam_t.t[:, 1:2]), reads=[lv], writes=[lj, lam_t])
            S.op("dve", lambda e: e.scalar_tensor_tensor(out=lj.t[:], in0=lv.t[:, 128:192], scalar=1.0, in1=lv.t[:, 192:256], op0=ALU.mult, op1=ALU.mult,
                                                         accum_out=lam_t.t[:, 2:3]), reads=[lv], writes=[lj, lam_t])
            S.op("act", lambda e: e.activation(out=lam_t.t[:, 3:5], in_=lam_t.t[:, 1:3], func=AF.Exp), reads=[lam_t], writes=[lam_t])
            S.op("dve", lambda e: e.tensor_tensor(out=lam_t.t[:, 5:6], in0=lam_t.t[:, 4:5], in1=lam_t.t[:, 3:4], op=ALU.subtract), reads=[lam_t], writes=[lam_t])
            S.op("dve", lambda e: e.tensor_scalar(out=lam_t.t[:, 0:1], in0=lam_t.t[:, 5:6], scalar1=-lam_init, scalar2=None, op0=ALU.add), reads=[lam_t], writes=[lam_t])
            for bi in range(len(va_r.bufs)):
                b = va_r.bufs[bi]
                S.op("pool", lambda e, b=b: e.memset(b.t[:], 1.0), writes=[b])
                b = vb_r.bufs[bi]
                S.op("pool", lambda e, b=b: e.memset(b.t[:], 1.0), writes=[b])
            Wp = load_w(stg_r, W, 8, W_IN_EXT, w_in[l], CH=512, order=[0, 512, 1536, 3072, 2048, 3584, 1024, 2560])

            for blk in range(NB):
                t0 = blk * 512
                nT = nT_r.next()
                for tt in range(4):
                    xt = x_r.next()
                    r0 = t0 + tt * 128
                    S.dma("sp", xt.t[:], x_src[r0:r0 + 128, :], writes=[xt], key=xt.name)
                    rmsnorm_to_T(st, xt, g_t, nT, tt * 128, (junk_r, ss_r, nb_r, tp_r))
                rt = rot_r.next()
                S.dma("sp", rt.t[:], rot[:, :, t0:t0 + 512].rearrange("c p t -> p c t"), writes=[rt], key=rt.name)

                def proj(ocol):
                    p = pj_r.next()
                    for k in range(8):
                        S.op("pe", lambda e, k=k, p=p: e.matmul(p.t[:], lhsT=W.t[:, k, ocol:ocol + 128], rhs=nT.t[:, k, :], start=(k == 0), stop=(k == 7)),
                             reads=[Wp[(ocol // 512) * 512], nT], writes=[p])
                    return p
                for which, dst in ((0, qaT), (1, kaT)):
                    for c in range(4):
                        p = proj(which * 512 + c * 128)
                        fo = fo_r.next()
                        S.op("act", lambda e, p=p, fo=fo: e.activation(out=fo.t[:], in_=p.t[:], func=AF.Copy), reads=[p], writes=[fo])
                        S.dma("pool", dst[c * 128:(c + 1) * 128, t0:t0 + 512], fo.t[:], reads=[fo], key=fo.name)
                for which, dst in ((0, qbT), (1, kbT)):
                    for h in range(4):
                        p1 = proj(1536 + which * 512 + h * 128)
                        p2 = proj(3072 + which * 512 + h * 128)
                        t1 = t1_r.next()
                        t2 = t2_r.next()
                        fo = fo_r.next()
                        S.op("dve", lambda e, p1=p1, t1=t1: e.tensor_tensor(out=t1.t[:], in0=p1.t[:], in1=rt.t[:, 0, :], op=ALU.mult), reads=[p1, rt], writes=[t1])
                        S.op("dve", lambda e, p2=p2, t2=t2: e.tensor_tensor(out=t2.t[:], in0=p2.t[:], in1=rt.t[:, 1, :], op=ALU.mult), reads=[p2, rt], writes=[t2])
                        S.op("pool", lambda e, t1=t1, t2=t2, fo=fo: e.tensor_tensor(out=fo.t[:], in0=t1.t[:], in1=t2.t[:], op=ALU.add), reads=[t1, t2], writes=[fo])
                        S.dma("pool", dst[h * 128:(h + 1) * 128, t0:t0 + 512], fo.t[:], reads=[fo], key=fo.name)
                for tt in range(4):
                    r0 = t0 + tt * 128
                    for which in range(2):
                        pv = pv_r.next()
                        oc = 1024 if which == 0 else 2560
                        for k in range(8):
                            S.op("pe", lambda e, k=k, pv=pv, oc=oc: e.matmul(pv.t[:], lhsT=nT.t[:, k, tt * 128:(tt + 1) * 128], rhs=W.t[:, k, oc:oc + 512],
                                                                              start=(k == 0), stop=(k == 7)), reads=[Wp[oc], nT], writes=[pv])
                        if which == 0:
                            vt = va_r.next()
                            S.op("act", lambda e, pv=pv, vt=vt: e.activation(out=vt.t[:, :, 0:64], in_=pv.t[:].rearrange("p (h d) -> p h d", d=64), func=AF.Copy),
                                 reads=[pv], writes=[vt])
                            S.dma("pool", va[r0:r0 + 128, :], vt.t[:].rearrange("p h d -> p (h d)"), reads=[vt], key=vt.name)
                        else:
                            vt = vb_r.next()
                            S.op("act", lambda e, pv=pv, vt=vt: e.activation(out=vt.t[:, :, 0:128], in_=pv.t[:].rearrange("p (h d) -> p h d", d=128), func=AF.Copy),
                                 reads=[pv], writes=[vt])
                            S.dma("pool", vb[r0:r0 + 128, :], vt.t[:].rearrange("p h d -> p (h d)"), reads=[vt], key=vt.name)
            S.flush()

        with ExitStack() as st:
            Tf = sb(st, "N_Tf", [128, NAH, 24, 64], BF16)
            rstg = ring(st, "N_rs", 2, [128, 15, 64], F32)
            rm_int = sb(st, "N_rmi", [128, 8, 512], BF16)
            rm_r = ring(st, "N_rm", 2, [128, 8, 512], BF16)
            kT_r = ring(st, "N_kT", 2, [128, 4, 1024], BF16)
            qT_r = ring(st, "N_qT", 2, [128, 4, 512], BF16)
            v_r = ring(st, "N_v", 2, [128, 8, 520], BF16)
            s_r = ring(st, "N_s", 6, [128, 512], F32, psum=True)
            e_r = ring(st, "N_e", 3, [128, 1024], BF16)
            p_r = ring(st, "N_p", 3, [128, 1024], BF16)
            TMi = sb(st, "N_TMi", [128, 8, NAH, 512], BF16)
            acc_r = ring(st, "N_acc", 2, [128, 512], F32, psum=True)
            av = lambda b_: b_.t[:, 0:260].rearrange("p (q c) -> p q c", c=65)
            rc_r = ring(st, "N_rc", 2, [128, 4], F32)
            ya_r = ring(st, "N_ya", 2, [128, 4, 512], BF16)

            S.op("dve", lambda e: e.memset(Tf.t[:], 0.0), writes=[Tf])
            for h in range(NAH):
                for a in range(2):
                    rs = rstg.next()
                    S.dma("sp", rs.t[a * 64:(a + 1) * 64], rpbt[l, h].rearrange("r k q -> k r q"), writes=[rs], key=rs.name)
                    S.op("act", lambda e, rs=rs, h=h, a=a: e.activation(out=Tf.t[a * 64:(a + 1) * 64, h, 4 + a:19 + a, :], in_=rs.t[a * 64:(a + 1) * 64],
                                                                         func=AF.Exp), reads=[rs], writes=[Tf])
            S.dma("sp", rm_int.t[:], rowmask[0].rearrange("j p q -> p j q"), writes=[rm_int], key="N_rmi")
            for j in range(8):
                for h in range(NAH):
                    s0 = 15 - 2 * j
                    S.op("dve", lambda e: e.tensor_tensor(out=TMi.t[:, j, h, :].rearrange("p (b q) -> p b q", q=64), in0=Tf.t[:, h, s0:s0 + 8, :],
                                                          in1=rm_int.t[:, j, :].rearrange("p (b q) -> p b q", q=64), op=ALU.mult), reads=[Tf, rm_int], writes=[TMi])

            for g in range(NG):
                jl = [j for j in range(8) if 0 <= 8 * g - 4 + 2 * j and 8 * g - 4 + 2 * j + 1 < ROWS]
                jlo, jhi = jl[0], jl[-1] + 1
                k0 = (8 * g - 4 + 2 * jlo) * 64
                nk = (jhi - jlo) * 128
                kT = kT_r.next()
                qT = qT_r.next()
                vg = v_r.next()
                S.dma("sp", kT.t[:, :, jlo * 128:jhi * 128], kaT.rearrange("(hp p) t -> p hp t", p=128)[:, :, k0:k0 + nk], writes=[kT], key=kT.name)
                S.dma("sp", qT.t[:], qaT.rearrange("(hp p) t -> p hp t", p=128)[:, :, g * 512:(g + 1) * 512], writes=[qT], key=qT.name)
                S.dma("sp", vg.t[:, jlo:jhi, :], va[k0:k0 + nk, :].rearrange("(j p) c -> p j c", p=128), writes=[vg], key=vg.name)
                if g in SG:
                    rm = rm_r.next()
                    S.dma("sp", rm.t[:], rowmask[1 + SG.index(g)].rearrange("j p q -> p j q"), writes=[rm], key=rm.name)
                else:
                    rm = rm_int
                ya = ya_r.next()
                def n_scores(hp, j):
                    sps = []
                    for a2 in range(2):
                        sp_ = s_r.next()
                        lo = a2 * 64
                        S.op("pe", lambda e: e.matmul(sp_.t[:], lhsT=kT.t[lo:lo + 64, hp, j * 128:(j + 1) * 128], rhs=qT.t[lo:lo + 64, hp, :],
                                                      start=True, stop=True), reads=[kT, qT], writes=[sp_])
                        sps.append(sp_)
                    return sps

                units = [(hp, ji, j) for hp in range(4) for ji, j in enumerate(jl)]
                spq = [n_scores(units[0][0], units[0][2])]
                if len(units) > 1:
                    spq.append(n_scores(units[1][0], units[1][2]))
                accs = None
                for ui, (hp, ji, j) in enumerate(units):
                    if ji == 0:
                        accs = [acc_r.next(), acc_r.next()]
                    sps = spq.pop(0)
                    if ui + 2 < len(units):
                        spq.append(n_scores(units[ui + 2][0], units[ui + 2][2]))
                    et = e_r.next()
                    pt = p_r.next()
                    s0 = 15 - 2 * j
                    for a2 in range(2):
                        sp_ = sps[a2]
                        S.op("act", lambda e: e.activation(out=et.t[:, a2 * 512:(a2 + 1) * 512], in_=sp_.t[:], func=AF.Exp, scale=0.125), reads=[sp_], writes=[et])
                    if rm is rm_int:
                        S.op("dve", lambda e: e.tensor_tensor(out=pt.t[:], in0=et.t[:], in1=TMi.t[:, j, 2 * hp:2 * hp + 2, :].rearrange("p h q -> p (h q)"), op=ALU.mult),
                             reads=[et, TMi], writes=[pt])
                    else:
                        for a2 in range(2):
                            hh = 2 * hp + a2
                            ev = et.t[:, a2 * 512:(a2 + 1) * 512]
                            S.op("dve", lambda e: e.tensor_tensor(out=ev.rearrange("p (b q) -> p b q", q=64), in0=ev.rearrange("p (b q) -> p b q", q=64),
                                                                  in1=Tf.t[:, hh, s0:s0 + 8, :], op=ALU.mult), reads=[et, Tf], writes=[et])
                            S.op("dve", lambda e: e.tensor_tensor(out=pt.t[:, a2 * 512:(a2 + 1) * 512], in0=ev, in1=rm.t[:, j, :], op=ALU.mult), reads=[et, rm], writes=[pt])
                    for a2 in range(2):
                        hh = 2 * hp + a2
                        for qt in range(4):
                            S.op("pe", lambda e: e.matmul(av(accs[a2])[:, qt, :], lhsT=pt.t[:, a2 * 512 + qt * 128:a2 * 512 + (qt + 1) * 128],
                                                          rhs=vg.t[:, j, hh * 65:(hh + 1) * 65], start=(ji == 0 and qt == 0), stop=(ji == len(jl) - 1)),
                                 reads=[pt, vg], writes=[accs[a2]])
                    if ji == len(jl) - 1:
                        for a2 in range(2):
                            hh = 2 * hp + a2
                            rc = rc_r.next()
                            S.op("dve", lambda e: e.reciprocal(out=rc.t[:, 0:4].rearrange("p (q o) -> p q o", o=1), in_=av(accs[a2])[:, :, 64:65]), reads=[accs[a2]], writes=[rc])
                            for qt in range(4):
                                S.op("dve", lambda e: e.tensor_scalar(out=ya.t[:, qt, hh * 64:(hh + 1) * 64], in0=av(accs[a2])[:, qt, 0:64], scalar1=rc.t[:, qt:qt + 1],
                                                                      scalar2=None, op0=ALU.mult), reads=[accs[a2], rc], writes=[ya])
                S.dma("pool", ycat[g * 512:(g + 1) * 512, 0:512].rearrange("(qt p) c -> p qt c", p=128), ya.t[:], reads=[ya], key=ya.name)
            S.flush()

        with ExitStack() as st:
            kT_r = ring(st, "F_kT", 2, [128, T], BF16)
            v_r = ring(st, "F_v", 1, [128, NT, 129], BF16)
            vm_r = ring(st, "F_vm", 2, [128, 2, NT, 129], BF16)
            qT_r = ring(st, "F_qT", 2, [128, 512], BF16)
            s_r = ring(st, "F_s", 4, [128, 512], F32, psum=True)
            p_r = ring(st, "F_p", 6, [128, 512], BF16)
            acc_r = ring(st, "F_acc", 4, [128, 512], F32, psum=True)
            fv = lambda b_: b_.t[:, 0:258].rearrange("p (a c) -> p a c", c=129)
            rc_r = ring(st, "F_rc", 2, [128, 8], F32)
            t_r = ring(st, "F_t", 2, [128, 128], F32)
            yv_r = ring(st, "F_yv", 2, [128, 128], F32)
            jk_r = ring(st, "F_jk", 1, [128, 128], F32)
            yb_r = ring(st, "F_yb", 2, [128, 4, 128], BF16)
            gs = sb(st, "F_gs", [128, 128], F32)
            S.dma("sp", gs.t[:], subln[l], writes=[gs], key="F_gs")
            S.op("dve", lambda e: e.tensor_scalar(out=gs.t[:], in0=gs.t[:], scalar1=(1.0 - lam_init), scalar2=None, op0=ALU.mult), reads=[gs], writes=[gs])
            for h in range(4):
                kT = kT_r.next()
                vh = v_r.next()
                S.dma("sp", kT.t[:], kbT[h * 128:(h + 1) * 128, :], writes=[kT], key=kT.name)
                S.dma("sp", vh.t[:], vb[:, h * 129:(h + 1) * 129].rearrange("(j p) c -> p j c", p=128), writes=[vh], key=vh.name)
                vm = vm_r.next()
                for q2 in range(2):
                    S.op("dve", lambda e: e.tensor_tensor(out=vm.t[:, q2], in0=vh.t[:], in1=dmask_t.t[:, q2 * NT:(q2 + 1) * NT].unsqueeze(2).to_broadcast([128, NT, 129]),
                                                          op=ALU.mult), reads=[vh, dmask_t], writes=[vm])

                def f_scores(qT, kt):
                    sps = []
                    for a2 in range(2):
                        lo = a2 * 64
                        sp_ = s_r.next()
                        S.op("pe", lambda e: e.matmul(sp_.t[:], lhsT=kT.t[lo:lo + 64, kt * 128:(kt + 1) * 128], rhs=qT.t[lo:lo + 64, :], start=True, stop=True),
                             reads=[kT, qT], writes=[sp_])
                        sps.append(sp_)
                    return sps

                for qb in range(NB):
                    qT = qT_r.next()
                    S.dma("sp", qT.t[:], qbT[h * 128:(h + 1) * 128, qb * 512:(qb + 1) * 512], writes=[qT], key=qT.name)
                    accs = [acc_r.next() for _ in range(4)]
                    qhalf = 0 if qb < NB // 2 else 1
                    sps_next = f_scores(qT, 0)
                    for kt in range(NT):
                        sps = sps_next
                        if kt + 1 < NT:
                            sps_next = f_scores(qT, kt + 1)
                        pts = []
                        for a2 in range(2):
                            sp_ = sps[a2]
                            pt = p_r.next()
                            S.op("act", lambda e: e.activation(out=pt.t[:], in_=sp_.t[:], func=AF.Exp, scale=0.125), reads=sps, writes=[pt])
                            pts.append(pt)
                        for a2 in range(2):
                            pt = pts[a2]
                            for qt in range(4):
                                S.op("pe", lambda e: e.matmul(fv(accs[qt])[:, a2, :], lhsT=pt.t[:, qt * 128:(qt + 1) * 128], rhs=vm.t[:, qhalf, kt, :],
                                                              start=(kt == 0 and a2 == 0), stop=(kt == NT - 1)), reads=[pt, vm], writes=[accs[qt]])
                    yb = yb_r.next()
                    for qt in range(4):
                        ac = accs[qt]
                        rc = rc_r.next()
                        tt_ = t_r.next()
                        yv = yv_r.next()
                        jk = jk_r.next()
                        S.op("dve", lambda e, ac=ac, rc=rc: e.reciprocal(out=rc.t[:, 0:2].rearrange("p (q o) -> p q o", o=1), in_=fv(ac)[:, :, 128:129]), reads=[ac], writes=[rc])
                        S.op("dve", lambda e, rc=rc: e.tensor_tensor(out=rc.t[:, 2:3], in0=rc.t[:, 1:2], in1=lam_t.t[:, 0:1], op=ALU.mult), reads=[rc, lam_t], writes=[rc])
                        S.op("dve", lambda e, ac=ac, rc=rc, tt_=tt_: e.tensor_scalar(out=tt_.t[:], in0=fv(ac)[:, 1, 0:128], scalar1=rc.t[:, 2:3], scalar2=None, op0=ALU.mult),
                             reads=[ac, rc], writes=[tt_])
                        S.op("dve", lambda e, ac=ac, rc=rc, tt_=tt_, yv=yv: e.scalar_tensor_tensor(out=yv.t[:], in0=fv(ac)[:, 0, 0:128], scalar=rc.t[:, 0:1], in1=tt_.t[:],
                                                                                                    op0=ALU.mult, op1=ALU.add), reads=[ac, rc, tt_], writes=[yv])
                        S.op("dve", lambda e, yv=yv, jk=jk, rc=rc: e.scalar_tensor_tensor(out=jk.t[:], in0=yv.t[:], scalar=1.0, in1=yv.t[:], op0=ALU.mult, op1=ALU.mult,
                                                                                           accum_out=rc.t[:, 3:4]), reads=[yv], writes=[jk, rc])
                        S.op("dve", lambda e, rc=rc: e.tensor_scalar(out=rc.t[:, 4:5], in0=rc.t[:, 3:4], scalar1=1.0 / 128, scalar2=EPS, op0=ALU.mult, op1=ALU.add), reads=[rc], writes=[rc])
                        S.op("act", lambda e, rc=rc: e.activation(out=rc.t[:, 6:7], in_=rc.t[:, 4:5], func=AF.Ln), reads=[rc], writes=[rc])
                        S.op("act", lambda e, rc=rc: e.activation(out=rc.t[:, 5:6], in_=rc.t[:, 6:7], func=AF.Exp, scale=-0.5), reads=[rc], writes=[rc])
                        S.op("dve", lambda e, yv=yv, rc=rc, qt=qt: e.scalar_tensor_tensor(out=yb.t[:, qt, :], in0=yv.t[:], scalar=rc.t[:, 5:6], in1=gs.t[:], op0=ALU.mult, op1=ALU.mult),
                             reads=[yv, rc, gs], writes=[yb])
                    S.dma("pool", ycat[qb * 512:(qb + 1) * 512, 512 + h * 128:512 + (h + 1) * 128].rearrange("(qt p) c -> p qt c", p=128), yb.t[:], reads=[yb], key=yb.name)
            S.flush()

        with ExitStack() as st:
            W = sb(st, "D_W", [128, 8, D], BF16)
            stg_r = ring(st, "D_stg", 2, [128, 1024], F32)
            g_t = sb(st, "D_g", [128, D], F32)
            y_r = ring(st, "D_y", 3, [128, D], BF16)
            x_r = ring(st, "D_x", 4, [128, D], F32)
            xm_r = ring(st, "D_xm", 4, [128, D], F32)
            yT_r = ring(st, "D_yT", 3, [128, 8, 128], BF16)
            tpa_r = ring(st, "D_tpa", 2, [128, 8, 128], BF16, psum=True)
            tpb_r = ring(st, "D_tpb", 2, [128, 8, 128], BF16, psum=True)
            po_r = ring(st, "D_po", 4, [128, 512], F32, psum=True)
            junk_r = ring(st, "D_jk", 2, [128, D], BF16)
            ss_r = ring(st, "D_ss", 4, [128, 4], F32)
            nb_r = ring(st, "D_nb", 3, [128, D], BF16)
            nT_r = ring(st, "D_nT", 3, [128, 8, 128], BF16)
            S.dma("sp", g_t.t[:], g_ffn[l], writes=[g_t], key="D_g")
            load_w(stg_r, W, 8, D, w_out[l])
            ctx = {}

            def d0(i):
                c = ctx[i] = {}
                r0 = i * 128
                c["yt"] = yt = y_r.next()
                c["xt"] = xt = x_r.next()
                S.dma("sp", yt.t[:], ycat[r0:r0 + 128, :], writes=[yt], key=yt.name)
                S.dma("sp", xt.t[:], x_src[r0:r0 + 128, :], writes=[xt], key=xt.name)

            def d1(i):
                c = ctx[i]
                yt = c["yt"]
                tp = tpa_r.next()
                for cc in range(8):
                    S.op("pe", lambda e: e.transpose(out=tp.t[:, cc, :], in_=yt.t[:, cc * 128:(cc + 1) * 128], identity=ident.t[:]), reads=[yt, ident], writes=[tp])
                c["yT"] = yT = yT_r.next()
                S.op("act", lambda e: e.activation(out=yT.t[:], in_=tp.t[:], func=AF.Copy), reads=[tp], writes=[yT])

            def d2(i):
                c = ctx[i]
                yT, xt = c["yT"], c["xt"]
                r0 = i * 128
                c["xm"] = xm = xm_r.next()
                for half in range(2):
                    po = po_r.next()
                    for k in range(8):
                        S.op("pe", lambda e: e.matmul(po.t[:], lhsT=yT.t[:, k, :], rhs=W.t[:, k, half * 512:(half + 1) * 512], start=(k == 0), stop=(k == 7)),
                             reads=[yT, W], writes=[po])
                    S.op("dve", lambda e: e.tensor_tensor(out=xm.t[:, half * 512:(half + 1) * 512], in0=po.t[:], in1=xt.t[:, half * 512:(half + 1) * 512], op=ALU.add),
                         reads=[po, xt], writes=[xm])
                S.dma("pool", xmid[r0:r0 + 128, :], xm.t[:], reads=[xm], key=xm.name)
                jk = junk_r.next()
                c["ss"] = ss = ss_r.next()
                S.op("dve", lambda e: e.scalar_tensor_tensor(out=jk.t[:], in0=xm.t[:], scalar=1.0, in1=xm.t[:], op0=ALU.mult, op1=ALU.mult,
                                                             accum_out=ss.t[:, 0:1]), reads=[xm], writes=[jk, ss])
                S.op("dve", lambda e: e.tensor_scalar(out=ss.t[:, 1:2], in0=ss.t[:, 0:1], scalar1=1.0 / D, scalar2=EPS, op0=ALU.mult, op1=ALU.add),
                     reads=[ss], writes=[ss])

            def d3(i):
                ss = ctx[i]["ss"]
                S.op("act", lambda e: e.activation(out=ss.t[:, 3:4], in_=ss.t[:, 1:2], func=AF.Ln), reads=[ss], writes=[ss])
                S.op("act", lambda e: e.activation(out=ss.t[:, 2:3], in_=ss.t[:, 3:4], func=AF.Exp, scale=-0.5), reads=[ss], writes=[ss])

            def d4a(i):
                c = ctx[i]
                xm, ss = c["xm"], c["ss"]
                c["nb"] = nb = nb_r.next()
                S.op("dve", lambda e: e.scalar_tensor_tensor(out=nb.t[:], in0=xm.t[:], scalar=ss.t[:, 2:3], in1=g_t.t[:], op0=ALU.mult, op1=ALU.mult),
                     reads=[xm, ss, g_t], writes=[nb])

            def d4b(i):
                c = ctx[i]
                nb = c["nb"]
                c["tp2"] = tp = tpb_r.next()
                for cc in range(8):
                    S.op("pe", lambda e: e.transpose(out=tp.t[:, cc, :], in_=nb.t[:, cc * 128:(cc + 1) * 128], identity=ident.t[:]), reads=[nb, ident], writes=[tp])

            def d5(i):
                c = ctx[i]
                tp = c["tp2"]
                r0 = i * 128
                nT = nT_r.next()
                S.op("act", lambda e: e.activation(out=nT.t[:], in_=tp.t[:], func=AF.Copy), reads=[tp], writes=[nT])
                S.dma("pool", n2T_v[:, :, 1 + r0:1 + r0 + 128], nT.t[:], reads=[nT], key=nT.name)
                del ctx[i]

            stages = [d0, d1, d2, d3, d4a, d4b, d5]
            lag = [0, 1, 2, 3, 4, 4, 5]
            order = [4, 6, 3, 2, 1, 5, 0]
            for tstep in range(NT + 6):
                for s_ in order:
                    i = tstep - lag[s_]
                    if 0 <= i < NT:
                        stages[s_](i)
            S.flush()

        last = (l == NL - 1)
        with ExitStack() as st:
            Wu = sb(st, "E_Wu", [128, 8, 2 * DFF], BF16)
            Wd = sb(st, "E_Wd", [128, NFC, D], BF16)
            stg_r = ring(st, "E_stg", 2, [128, 512], F32)
            cw = sb(st, "E_cw", [128, 4 * NFC], F32)
            nT_r = ring(st, "E_nT", 1, [128, 8, 514], BF16)
            hT = sb(st, "E_hT", [128, NFC, 512], BF16)
            pg_r = ring(st, "E_pg", 2, [128, 512], F32, psum=True)
            pvv_r = ring(st, "E_pvv", 2, [128, 512], F32, psum=True)
            ph_r = ring(st, "E_ph", 2, [128, 2], F32, psum=True)
            pd_r = ring(st, "E_pd", 2, [128, 512], F32, psum=True)
            gx_r = ring(st, "E_gx", 2, [128, 514], F32)
            a_r = ring(st, "E_a", 2, [128, 512], F32)
            gl_r = ring(st, "E_gl", 2, [128, 512], F32)
            xm_r = ring(st, "E_xm", 2, [128, D], F32)
            xo_r = ring(st, "E_xo", 1, [128, D], F32)
            S.dma("sp", cw.t[:], convw[l], writes=[cw], key="E_cw")
            if last:
                g_t = sb(st, "E_g", [128, D], F32)
                jk_r = ring(st, "E_jk", 1, [128, D], BF16)
                ss_r = ring(st, "E_ss", 2, [128, 4], F32)
                yo_r = ring(st, "E_yo", 1, [128, D], F32)
                S.dma("sp", g_t.t[:], g_final[:, :], writes=[g_t], key="E_g")
            Wup = load_w(stg_r, Wu, 8, 2 * DFF, w_up[l], CH=512, order=[0, 2560, 3072, 512, 3584, 1024, 4096, 1536, 4608, 2048, 5120])
            load_w(stg_r, Wd, NFC, D, w_down[l], CH=512)
            for blk in range(NB):
                t0 = blk * 512
                nT = nT_r.next()
                S.dma("sp", nT.t[:], n2T_v[:, :, t0:t0 + 514], writes=[nT], key=nT.name)
                if NB >= 2 and blk == NB // 2:
                    S.op("dve", lambda e, nT=nT: e.tensor_scalar(out=nT.t[:, :, 0:1], in0=nT.t[:, :, 0:1], scalar1=cmask_t.t[:, 0:1], scalar2=None, op0=ALU.mult), reads=[nT, cmask_t], writes=[nT])
                if NB >= 2 and blk == NB // 2 - 1:
                    S.op("dve", lambda e, nT=nT: e.tensor_scalar(out=nT.t[:, :, 513:514], in0=nT.t[:, :, 513:514], scalar1=cmask_t.t[:, 1:2], scalar2=None, op0=ALU.mult), reads=[nT, cmask_t], writes=[nT])
                for fc in range(NFC):
                    pg = pg_r.next()
                    ph = ph_r.next()
                    pvv = pvv_r.next()
                    for k in range(8):
                        S.op("pe", lambda e, k=k, pg=pg, fc=fc, nT=nT: e.matmul(pg.t[:], lhsT=Wu.t[:, k, fc * 128:(fc + 1) * 128], rhs=nT.t[:, k, 1:513], start=(k == 0), stop=(k == 7)),
                             reads=[Wup[((fc * 128) // 512) * 512], nT], writes=[pg])
                    for k in range(8):
                        S.op("pe", lambda e, k=k, ph=ph, fc=fc, nT=nT: e.matmul(ph.t[:], lhsT=Wu.t[:, k, fc * 128:(fc + 1) * 128], rhs=nT.t[:, k, 0:514:513], start=(k == 0), stop=(k == 7)),
                             reads=[Wup[((fc * 128) // 512) * 512], nT], writes=[ph])
                    for k in range(8):
                        S.op("pe", lambda e, k=k, pvv=pvv, fc=fc, nT=nT: e.matmul(pvv.t[:], lhsT=Wu.t[:, k, DFF + fc * 128:DFF + (fc + 1) * 128], rhs=nT.t[:, k, 1:513], start=(k == 0), stop=(k == 7)),
                             reads=[Wup[((DFF + fc * 128) // 512) * 512], nT], writes=[pvv])
                    gx = gx_r.next()
                    S.op("act", lambda e, gx=gx, pg=pg: e.activation(out=gx.t[:, 1:513], in_=pg.t[:], func=AF.Copy), reads=[pg], writes=[gx])
                    S.op("act", lambda e, gx=gx, ph=ph: e.activation(out=gx.t[:, 0:514:513], in_=ph.t[:], func=AF.Copy), reads=[ph], writes=[gx])
                    a = a_r.next()
                    S.op("dve", lambda e, a=a, gx=gx, fc=fc: e.tensor_scalar(out=a.t[:], in0=gx.t[:, 1:513], scalar1=cw.t[:, NFC + fc:NFC + fc + 1], scalar2=cw.t[:, 3 * NFC + fc:3 * NFC + fc + 1],
                                                                          op0=ALU.mult, op1=ALU.add), reads=[gx, cw], writes=[a])
                    S.op("dve", lambda e, a=a, gx=gx, fc=fc: e.scalar_tensor_tensor(out=a.t[:], in0=gx.t[:, 0:512], scalar=cw.t[:, fc:fc + 1], in1=a.t[:], op0=ALU.mult, op1=ALU.add),
                         reads=[gx, cw, a], writes=[a])
                    S.op("dve", lambda e, a=a, gx=gx, fc=fc: e.scalar_tensor_tensor(out=a.t[:], in0=gx.t[:, 2:514], scalar=cw.t[:, 2 * NFC + fc:2 * NFC + fc + 1], in1=a.t[:], op0=ALU.mult, op1=ALU.add),
                         reads=[gx, cw, a], writes=[a])
                    gl = gl_r.next()
                    S.op("act", lambda e, a=a, gl=gl: e.activation(out=gl.t[:], in_=a.t[:], func=AF.Gelu), reads=[a], writes=[gl])
                    S.op("dve", lambda e, gl=gl, pvv=pvv, fc=fc: e.tensor_tensor(out=hT.t[:, fc, :], in0=pvv.t[:], in1=gl.t[:], op=ALU.mult), reads=[gl, pvv], writes=[hT])
                for tt in range(4):
                    r0 = t0 + tt * 128
                    xm = xm_r.next()
                    S.dma("sp", xm.t[:], xmid[r0:r0 + 128, :], writes=[xm], key=xm.name)
                    xo = xo_r.next()
                    for half in range(2):
                        pd = pd_r.next()
                        for fc in range(NFC):
                            S.op("pe", lambda e, fc=fc, pd=pd, tt=tt, half=half: e.matmul(pd.t[:], lhsT=hT.t[:, fc, tt * 128:(tt + 1) * 128], rhs=Wd.t[:, fc, half * 512:(half + 1) * 512],
                                                                                       start=(fc == 0), stop=(fc == NFC - 1)), reads=[hT, Wd], writes=[pd])
                        S.op("dve", lambda e, pd=pd, xm=xm, xo=xo, half=half: e.tensor_tensor(out=xo.t[:, half * 512:(half + 1) * 512], in0=pd.t[:], in1=xm.t[:, half * 512:(half + 1) * 512], op=ALU.add),
                             reads=[pd, xm], writes=[xo])
                    if not last:
                        S.dma("pool", xres[r0:r0 + 128, :], xo.t[:], reads=[xo], key=xo.name)
                    else:
                        jk = jk_r.next()
                        ss = ss_r.next()
                        yo = yo_r.next()
                        S.op("dve", lambda e, jk=jk, ss=ss, xo=xo: e.scalar_tensor_tensor(out=jk.t[:], in0=xo.t[:], scalar=1.0, in1=xo.t[:], op0=ALU.mult, op1=ALU.mult, accum_out=ss.t[:, 0:1]),
                             reads=[xo], writes=[jk, ss])
                        S.op("dve", lambda e, ss=ss: e.tensor_scalar(out=ss.t[:, 1:2], in0=ss.t[:, 0:1], scalar1=1.0 / D, scalar2=EPS, op0=ALU.mult, op1=ALU.add), reads=[ss], writes=[ss])
                        S.op("act", lambda e, ss=ss: e.activation(out=ss.t[:, 3:4], in_=ss.t[:, 1:2], func=AF.Ln), reads=[ss], writes=[ss])
                        S.op("act", lambda e, ss=ss: e.activation(out=ss.t[:, 2:3], in_=ss.t[:, 3:4], func=AF.Exp, scale=-0.5), reads=[ss], writes=[ss])
                        S.op("dve", lambda e, ss=ss, xo=xo, yo=yo: e.scalar_tensor_tensor(out=yo.t[:], in0=xo.t[:], scalar=ss.t[:, 2:3], in1=g_t.t[:], op0=ALU.mult, op1=ALU.mult),
                             reads=[xo, ss, g_t], writes=[yo])
                        S.dma("pool", y_out[r0:r0 + 128, :], yo.t[:], reads=[yo], key=yo.name)
            S.flush()
    es.close()
    return nc


def _rowmask_tables(T, seq_len):
    ROWS = T // 64
    NG = T // 512
    SG = sorted(set([0, NG // 2 - 1, NG // 2, NG - 1]))
    R = seq_len // 64

    def table(g):
        m = np.zeros((8, 128, 512), np.float32)
        for j in range(8):
            for a in range(2):
                kr = 8 * g - 4 + 2 * j + a
                if kr < 0 or kr >= ROWS:
                    continue
                for b in range(8):
                    qr = 8 * g + b
                    if kr // R != qr // R:
                        continue
                    qs = qr % R
                    start = min(max(qs - 4, 0), R - 8)
                    ks = kr % R
                    if start <= ks < start + 8:
                        m[j, a * 64:(a + 1) * 64, b * 64:(b + 1) * 64] = 1.0
        return m
    mi = np.zeros((8, 128, 512), np.float32)
    for j in range(8):
        for a in range(2):
            for b in range(8):
                dr = 2 * j + a - b - 4
                if -4 <= dr <= 3:
                    mi[j, a * 64:(a + 1) * 64, b * 64:(b + 1) * 64] = 1.0
    tabs = [mi] + [table(g) for g in SG]
    return np.stack(tabs).astype(ml_dtypes.bfloat16)


def _rot_tables(T, seq_len):
    pos = (np.arange(T) % seq_len).astype(np.float32)
    inv = (1.0 / (10000.0 ** (np.arange(0, 64, 2, dtype=np.float32) / 64.0))).astype(np.float32)
    ang = pos[None, :] * inv[:, None]
    cos = np.cos(ang).astype(np.float32)
    sin = np.sin(ang).astype(np.float32)
    cos64 = np.concatenate([cos, cos], 0)
    sin64 = np.concatenate([-sin, sin], 0)
    return np.stack([np.concatenate([cos64, cos64], 0), np.concatenate([sin64, sin64], 0)]).astype(np.float32)


def _prep_shared(inp, NL):
    w_in = np.asarray(inp["w_in"], np.float32)
    perm = np.arange(1024).reshape(8, 2, 64)
    perm = np.concatenate([perm[..., 32:], perm[..., :32]], -1).reshape(-1)
    qb = w_in[:, :, 1536:2560]
    w_ext = np.concatenate([w_in, qb[:, :, perm]], axis=2)
    rep = lambda a: np.ascontiguousarray(np.broadcast_to(np.asarray(a, np.float32)[:, None, :], (a.shape[0], 128, a.shape[1])))
    lamv = np.concatenate([np.asarray(inp[k], np.float32) for k in ("lam_q1", "lam_k1", "lam_q2", "lam_k2")], axis=1)
    cw = np.concatenate([np.asarray(inp["conv_w"], np.float32), np.asarray(inp["conv_b"], np.float32)[:, None, :]], axis=1)
    cw = cw.reshape(NL, 4, NFC, 128).transpose(0, 3, 1, 2).reshape(NL, 128, 4 * NFC)
    rpb = np.asarray(inp["rpb"], np.float32)
    kc = np.arange(64)[:, None]
    qc = np.arange(64)[None, :]
    dc = kc - qc + 15
    cs = np.clip(qc - 8, 0, 48)
    ok = (kc >= cs) & (kc < cs + 16)
    dcc = np.clip(dc, 0, 30)
    rp = rpb[:, :, ::-1, :][:, :, :, dcc]
    rp = np.where(ok[None, None, None], rp, np.float32(NEG)).astype(np.float32)
    return dict(
        w_in=np.ascontiguousarray(w_ext), w_out=np.asarray(inp["w_out"], np.float32), w_up=np.asarray(inp["w_up"], np.float32),
        w_down=np.asarray(inp["w_down"], np.float32), g_attn=rep(inp["g_attn"]), g_ffn=rep(inp["g_ffn"]),
        g_final=np.ascontiguousarray(np.broadcast_to(np.asarray(inp["g_final"], np.float32)[None, :], (128, D))),
        subln=rep(inp["subln_g"]), lamv=rep(lamv), convw=np.ascontiguousarray(cw), rpbt=np.ascontiguousarray(rp),
        ident=np.eye(128, dtype=np.float32).astype(ml_dtypes.bfloat16),
    )


def _prep_core(T, seq_len):
    NT = T // 128
    dm = np.ones((2, NT), np.float32)
    if seq_len < T:
        dm[0, NT // 2:] = 0.0
        dm[1, :NT // 2] = 0.0
    dmask = np.ascontiguousarray(np.broadcast_to(dm.reshape(1, -1), (128, 2 * NT)))
    cm = np.full((128, 2), 1.0 if seq_len == T else 0.0, np.float32)
    return dict(rot=_rot_tables(T, seq_len), rowmask=_rowmask_tables(T, seq_len), dmask=dmask, cmask=cm)


_PROG_CACHE = {}


def run_cores(inp, xs, seqlens, T, NL, n_cores):
    key = (T, NL)
    if key not in _PROG_CACHE:
        _PROG_CACHE[key] = build_program(T, NL)
    nc = _PROG_CACHE[key]
    shared = _prep_shared(inp, NL)
    per_type = {}
    in_maps = []
    for x, sl in zip(xs, seqlens):
        if sl not in per_type:
            per_type[sl] = _prep_core(T, sl)
        m = dict(shared)
        m.update(per_type[sl])
        m["x"] = np.ascontiguousarray(x, dtype=np.float32)
        in_maps.append(m)
    res = run_bass_kernel_spmd(nc, in_maps, core_ids=list(range(n_cores)))
    return [r["y_out"] for r in res.results]


def kernel(x_prompt, x_sample, g_attn, w_in, rpb, lam_q1, lam_k1, lam_q2, lam_k2, subln_g, w_out,
           g_ffn, w_up, conv_w, conv_b, w_down, g_final):
    inp = dict(g_attn=g_attn, w_in=w_in, rpb=rpb, lam_q1=lam_q1, lam_k1=lam_k1, lam_q2=lam_q2, lam_k2=lam_k2,
               subln_g=subln_g, w_out=w_out, g_ffn=g_ffn, w_up=w_up, conv_w=conv_w, conv_b=conv_b, w_down=w_down, g_final=g_final)
    inp = {k: np.asarray(v) for k, v in inp.items()}
    xp = np.asarray(x_prompt, np.float32)
    xs_ = np.asarray(x_sample, np.float32)
    T = 8192
    xs = [xs_[0], xs_[1], xp[0:2].reshape(T, D), xp[2:4].reshape(T, D)]
    sl = [8192, 8192, 4096, 4096]
    outs = run_cores(inp, xs + xs, sl + sl, T, 4, 8)
    y_sample = np.stack([outs[0], outs[1]]).astype(np.float32)
    y_prompt = np.concatenate([outs[2].reshape(2, 4096, D), outs[3].reshape(2, 4096, D)], 0).astype(np.float32)
    return (y_prompt, y_sample)
```

```python
import math
from contextlib import ExitStack

import numpy as np
import ml_dtypes

import concourse.bass as bass
import concourse.mybir as mybir
from concourse.bass_utils import run_bass_kernel_spmd

F32 = mybir.dt.float32
BF16 = mybir.dt.bfloat16
AF = mybir.ActivationFunctionType
ALU = mybir.AluOpType

D = 1024
NAH = 8
DFF = 2816
NFC = DFF // 128
EPS = 1e-6
NEG = -30000.0
W_IN_EXT = 4096
SAME_ENGINE_SYNC = True


class Buf:
    def __init__(self, name, t):
        self.name = name
        self.t = t


class Ring:
    def __init__(self, bufs):
        self.bufs = bufs
        self.i = 0

    def next(self):
        b = self.bufs[self.i % len(self.bufs)]
        self.i += 1
        return b


class _Rec:
    def __getattr__(self, name):
        def f(*a, **k):
            self.call = (name, a, k)
            return self
        return f


class Sched:
    ENGS = ("pe", "act", "dve", "pool", "sp")

    def __init__(self, nc, es):
        self.nc = nc
        self.es = es
        self.ops = []
        self.state = {}
        self.dma_sems = {}
        self.dma_cnt = {}
        self.eng_sems = None
        self.eng_cnt = None

    def new_eng_sems(self, tag):
        self.eng_sems = {e: self.es.enter_context(self.nc.semaphore(f"s_{tag}_{e}")) for e in self.ENGS}
        self.eng_cnt = {e: 0 for e in self.ENGS}

    def I(self, eng, meth, reads=(), writes=(), dma_key=None, **kw):
        return self.op(eng, (meth, (), kw), reads, writes, dma_key)

    def op(self, eng, fn, reads=(), writes=(), dma_key=None):
        if callable(fn):
            rec = _Rec()
            fn(rec)
            fn = rec.call
        oid = len(self.ops)
        deps = set()
        evkey = ("dma", dma_key) if dma_key is not None else eng
        for r in reads:
            st = self.state.setdefault(r.name, {"w": {}, "r": {}})
            deps.update(st["w"].values())
        for w in writes:
            st = self.state.setdefault(w.name, {"w": {}, "r": {}})
            for k_, v_ in st["w"].items():
                if not (k_ == evkey and dma_key is None):
                    deps.add(v_)
            deps.update(st["r"].values())
        for r in reads:
            self.state[r.name]["r"][evkey] = oid
        for w in writes:
            st = self.state[w.name]
            st["r"] = {}
            st["w"] = {evkey: oid}
        self.ops.append(dict(eng=eng, fn=fn, deps=deps, dma_key=dma_key, sig=False, val=None))
        return oid

    def dma(self, eng, out, in_, reads=(), writes=(), key=None, **kw):
        assert key is not None
        return self.op(eng, ("dma_start", (), dict(out=out, in_=in_, **kw)), reads, writes, dma_key=key.split("|")[-1])

    def flush(self):
        nc = self.nc
        ops = self.ops
        for o in ops:
            need = set()
            for d in o["deps"]:
                p = ops[d]
                if p["dma_key"] is not None:
                    need.add(d)
                elif p["eng"] != o["eng"]:
                    need.add(d)
                    p["sig"] = True
                elif SAME_ENGINE_SYNC and p["eng"] in ("act", "dve", "pool"):
                    need.add(d)
                    p["sig"] = True
            o["need"] = need
        for o in ops:
            if o["dma_key"] is not None:
                k = o["dma_key"]
                if k not in self.dma_sems:
                    self.dma_sems[k] = self.es.enter_context(nc.semaphore(f"d_{k}"))
                    self.dma_cnt[k] = 0
                self.dma_cnt[k] += 16
                o["val"] = (self.dma_sems[k], self.dma_cnt[k])
            elif o["sig"]:
                self.eng_cnt[o["eng"]] += 1
                o["val"] = (self.eng_sems[o["eng"]], self.eng_cnt[o["eng"]])
        per = {e: [] for e in self.ENGS}
        for o in ops:
            per[o["eng"]].append(o)

        def emit(eng_name, lst):
            def body(e):
                seen = {}
                issued = {}
                for o in lst:
                    waits = {}
                    for d in o["need"]:
                        sem, v = ops[d]["val"]
                        kk = id(sem)
                        if seen.get(kk, 0) >= v:
                            continue
                        if kk not in waits or waits[kk][1] < v:
                            waits[kk] = (sem, v)
                    for kk, (sem, v) in waits.items():
                        e.wait_ge(sem, v)
                        seen[kk] = v
                    meth, ar, kw = o["fn"]
                    ins = getattr(e, meth)(*ar, **kw)
                    if o["dma_key"] is not None:
                        sem, v = o["val"]
                        ins.then_inc(sem, 16)
                        issued[id(sem)] = (sem, v)
                    elif o["sig"]:
                        ins.then_inc(o["val"][0], 1)
                for kk, (sem, v) in issued.items():
                    e.wait_ge(sem, v)
            return body

        with nc.Block() as block:
            if per["sp"]:
                block.sync(emit("sp", per["sp"]))
            if per["pe"]:
                block.tensor(emit("pe", per["pe"]))
            if per["act"]:
                block.scalar(emit("act", per["act"]))
            if per["dve"]:
                block.vector(emit("dve", per["dve"]))
            if per["pool"]:
                block.gpsimd(emit("pool", per["pool"]))
        self.ops = []
        self.state = {}


def build_program(T, NL, n_types_dummy=None):
    nc = bass.Bass("TRN2", target_bir_lowering=False)
    NT = T // 128
    NB = T // 512
    ROWS = T // 64
    NG = NB
    SG = sorted(set([0, NG // 2 - 1, NG // 2, NG - 1]))

    def din(name, shape, dt=F32):
        return nc.dram_tensor(name, list(shape), dt, kind="ExternalInput").ap()

    x_in = din("x", [T, D])
    w_in = din("w_in", [NL, D, W_IN_EXT])
    w_out = din("w_out", [NL, D, D])
    w_up = din("w_up", [NL, D, 2 * DFF])
    w_down = din("w_down", [NL, DFF, D])
    g_attn = din("g_attn", [NL, 128, D])
    g_ffn = din("g_ffn", [NL, 128, D])
    g_final = din("g_final", [128, D])
    subln = din("subln", [NL, 128, 128])
    lamv = din("lamv", [NL, 128, 256])
    convw = din("convw", [NL, 128, 4 * NFC])
    rpbt = din("rpbt", [NL, NAH, 15, 64, 64])
    rot = din("rot", [2, 128, T])
    rowmask = din("rowmask", [len(SG) + 1, 8, 128, 512], BF16)
    dmask = din("dmask", [128, 2 * (T // 128)])
    cmask = din("cmask", [128, 2])
    ident_in = din("ident", [128, 128], BF16)
    y_out = nc.dram_tensor("y_out", [T, D], F32, kind="ExternalOutput").ap()

    def dscr(name, shape, dt):
        return nc.dram_tensor(name, list(shape), dt).ap()

    qaT = dscr("qaT", [512, T], BF16)
    kaT = dscr("kaT", [512, T], BF16)
    va = dscr("va", [T, 8 * 65], BF16)
    qbT = dscr("qbT", [512, T], BF16)
    kbT = dscr("kbT", [512, T], BF16)
    vb = dscr("vb", [T, 4 * 129], BF16)
    ycat = dscr("ycat", [T, D], BF16)
    xmid = dscr("xmid", [T, D], F32)
    xres = dscr("xres", [T, D], F32)
    n2T = dscr("n2T", [D, T + 2], BF16)

    es = ExitStack()
    S = Sched(nc, es)

    cur = {"tag": "pre"}

    def sb(st, name, shape, dt):
        full = f"{cur['tag']}|{name}"
        return Buf(full, st.enter_context(nc.sbuf_tensor(full.replace("|", "_"), list(shape), dt)))

    def ps(st, name, shape, dt):
        full = f"{cur['tag']}|{name}"
        return Buf(full, st.enter_context(nc.psum_tensor(full.replace("|", "_"), list(shape), dt)))

    def ring(st, name, n, shape, dt, psum=False):
        f = ps if psum else sb
        return Ring([f(st, f"{name}{i}", shape, dt) for i in range(n)])

    ident = sb(es, "ident", [128, 128], BF16)
    lam_t = sb(es, "lam_t", [128, 8], F32)
    dmask_t = sb(es, "dmask_t", [128, 2 * NT], F32)
    cmask_t = sb(es, "cmask_t", [128, 2], F32)

    def load_w(st_ring, dstbuf, K, N, src, c0=0, CH=1024, order=None):
        parts = {}
        for n0 in (order if order is not None else range(0, N, CH)):
            w = min(CH, N - n0)
            part = Buf(f"{dstbuf.name}#c{n0}", dstbuf.t)
            parts[n0] = part
            for k in range(K):
                stg = st_ring.next()
                S.dma("sp", stg.t[:, 0:w], src[k * 128:(k + 1) * 128, c0 + n0:c0 + n0 + w], writes=[stg], key=stg.name)
                S.op("pool", lambda e: e.tensor_copy(out=dstbuf.t[:, k, n0:n0 + w], in_=stg.t[:, 0:w]), reads=[stg], writes=[part])
        return parts

    def rmsnorm_to_T(st, xt, g_t, nT_dst, col0, rings):
        junk, ssr, nbr, tpr = rings
        jk = junk.next()
        ss = ssr.next()
        S.op("dve", lambda e: e.scalar_tensor_tensor(out=jk.t[:], in0=xt.t[:], scalar=1.0, in1=xt.t[:], op0=ALU.mult, op1=ALU.mult,
                                                     accum_out=ss.t[:, 0:1]), reads=[xt], writes=[jk, ss])
        S.op("dve", lambda e: e.tensor_scalar(out=ss.t[:, 1:2], in0=ss.t[:, 0:1], scalar1=1.0 / D, scalar2=EPS, op0=ALU.mult, op1=ALU.add),
             reads=[ss], writes=[ss])
        S.op("act", lambda e: e.activation(out=ss.t[:, 3:4], in_=ss.t[:, 1:2], func=AF.Ln), reads=[ss], writes=[ss])
        S.op("act", lambda e: e.activation(out=ss.t[:, 2:3], in_=ss.t[:, 3:4], func=AF.Exp, scale=-0.5), reads=[ss], writes=[ss])
        nb = nbr.next()
        S.op("dve", lambda e: e.scalar_tensor_tensor(out=nb.t[:], in0=xt.t[:], scalar=ss.t[:, 2:3], in1=g_t.t[:], op0=ALU.mult, op1=ALU.mult),
             reads=[xt, ss, g_t], writes=[nb])
        tp = tpr.next()
        for c in range(8):
            S.op("pe", lambda e, c=c: e.transpose(out=tp.t[:, c, :], in_=nb.t[:, c * 128:(c + 1) * 128], identity=ident.t[:]),
                 reads=[nb, ident], writes=[tp])
        S.op("act", lambda e: e.activation(out=nT_dst.t[:, 0:8, col0:col0 + 128], in_=tp.t[:], func=AF.Copy),
             reads=[tp], writes=[nT_dst])

    S.new_eng_sems("pre")
    zt = sb(es, "zt", [128, 8], BF16)
    S.dma("sp", ident.t[:], ident_in[:, :], writes=[ident], key="c_ident")
    S.dma("sp", dmask_t.t[:], dmask[:, :], writes=[dmask_t], key="c_dmask")
    S.dma("sp", cmask_t.t[:], cmask[:, :], writes=[cmask_t], key="c_cmask")
    S.op("dve", lambda e: e.memset(zt.t[:], 0.0), writes=[zt])
    n2T_v = n2T.rearrange("(c p) t -> p c t", p=128)
    S.dma("sp", n2T_v[:, :, 0:1], zt.t[:, 0:8].rearrange("p (c o) -> p c o", o=1), reads=[zt], key="c_z0", allow_slow_non_contiguous=True)
    S.dma("sp", n2T_v[:, :, T + 1:T + 2], zt.t[:, 0:8].rearrange("p (c o) -> p c o", o=1), reads=[zt], key="c_z1", allow_slow_non_contiguous=True)
    S.flush()

    for l in range(NL):
        S.new_eng_sems(f"L{l}")
        cur["tag"] = f"L{l}"
        x_src = x_in if l == 0 else xres
        lam_init = 0.8 - 0.6 * math.exp(-0.3 * l)
        with ExitStack() as st:
            W = sb(st, "A_W", [128, 8, W_IN_EXT], BF16)
            stg_r = ring(st, "A_stg", 2, [128, 1024], F32)
            g_t = sb(st, "A_g", [128, D], F32)
            x_r = ring(st, "A_x", 2, [128, D], F32)
            junk_r = ring(st, "A_jk", 1, [128, D], BF16)
            ss_r = ring(st, "A_ss", 2, [128, 4], F32)
            nb_r = ring(st, "A_nb", 2, [128, D], BF16)
            tp_r = ring(st, "A_tp", 2, [128, 8, 128], BF16, psum=True)
            nT_r = ring(st, "A_nT", 2, [128, 8, 512], BF16)
            pj_r = ring(st, "A_pj", 4, [128, 512], F32, psum=True)
            pv_r = ring(st, "A_pv", 2, [128, 512], F32, psum=True)
            fo_r = ring(st, "A_fo", 3, [128, 512], BF16)
            rot_r = ring(st, "A_rot", 2, [128, 2, 512], F32)
            t1_r = ring(st, "A_t1", 2, [128, 512], F32)
            t2_r = ring(st, "A_t2", 2, [128, 512], F32)
            va_r = ring(st, "A_va", 2, [128, 8, 65], BF16)
            vb_r = ring(st, "A_vb", 2, [128, 4, 129], BF16)
            lv = sb(st, "A_lv", [128, 256], F32)
            lj = sb(st, "A_lj", [128, 64], F32)

            S.dma("sp", g_t.t[:], g_attn[l], writes=[g_t], key="A_g")
            S.dma("sp", lv.t[:], lamv[l], writes=[lv], key="A_lv")
            S.op("dve", lambda e: e.scalar_tensor_tensor(out=lj.t[:], in0=lv.t[:, 0:64], scalar=1.0, in1=lv.t[:, 64:128], op0=ALU.mult, op1=ALU.mult,
                                                         accum_out=lam_t.t[:, 1:2]), reads=[lv], writes=[lj, lam_t])
            S.op("dve", lambda e: e.scalar_tensor_tensor(out=lj.t[:], in0=lv.t[:, 128:192], scalar=1.0, in1=lv.t[:, 192:256], op0=ALU.mult, op1=ALU.mult,
                                                         accum_out=lam_t.t[:, 2:3]), reads=[lv], writes=[lj, lam_t])
            S.op("act", lambda e: e.activation(out=lam_t.t[:, 3:5], in_=lam_t.t[:, 1:3], func=AF.Exp), reads=[lam_t], writes=[lam_t])
            S.op("dve", lambda e: e.tensor_tensor(out=lam_t.t[:, 5:6], in0=lam_t.t[:, 4:5], in1=lam_t.t[:, 3:4], op=ALU.subtract), reads=[lam_t], writes=[lam_t])
            S.op("dve", lambda e: e.tensor_scalar(out=lam_t.t[:, 0:1], in0=lam_t.t[:, 5:6], scalar1=-lam_init, scalar2=None, op0=ALU.add), reads=[lam_t], writes=[lam_t])
            for bi in range(len(va_r.bufs)):
                b = va_r.bufs[bi]
                S.op("pool", lambda e, b=b: e.memset(b.t[:], 1.0), writes=[b])
                b = vb_r.bufs[bi]
                S.op("pool", lambda e, b=b: e.memset(b.t[:], 1.0), writes=[b])
            Wp = load_w(stg_r, W, 8, W_IN_EXT, w_in[l], CH=512, order=[0, 512, 1536, 3072, 2048, 3584, 1024, 2560])

            for blk in range(NB):
                t0 = blk * 512
                nT = nT_r.next()
                for tt in range(4):
                    xt = x_r.next()
                    r0 = t0 + tt * 128
                    S.dma("sp", xt.t[:], x_src[r0:r0 + 128, :], writes=[xt], key=xt.name)
                    rmsnorm_to_T(st, xt, g_t, nT, tt * 128, (junk_r, ss_r, nb_r, tp_r))
                rt = rot_r.next()
                S.dma("sp", rt.t[:], rot[:, :, t0:t0 + 512].rearrange("c p t -> p c t"), writes=[rt], key=rt.name)

                def proj(ocol):
                    p = pj_r.next()
                    for k in range(8):
                        S.op("pe", lambda e, k=k, p=p: e.matmul(p.t[:], lhsT=W.t[:, k, ocol:ocol + 128], rhs=nT.t[:, k, :], start=(k == 0), stop=(k == 7)),
                             reads=[Wp[(ocol // 512) * 512], nT], writes=[p])
                    return p
                for which, dst in ((0, qaT), (1, kaT)):
                    for c in range(4):
                        p = proj(which * 512 + c * 128)
                        fo = fo_r.next()
                        S.op("act", lambda e, p=p, fo=fo: e.activation(out=fo.t[:], in_=p.t[:], func=AF.Copy), reads=[p], writes=[fo])
                        S.dma("pool", dst[c * 128:(c + 1) * 128, t0:t0 + 512], fo.t[:], reads=[fo], key=fo.name)
                for which, dst in ((0, qbT), (1, kbT)):
                    for h in range(4):
                        p1 = proj(1536 + which * 512 + h * 128)
                        p2 = proj(3072 + which * 512 + h * 128)
                        t1 = t1_r.next()
                        t2 = t2_r.next()
                        fo = fo_r.next()
                        S.op("dve", lambda e, p1=p1, t1=t1: e.tensor_tensor(out=t1.t[:], in0=p1.t[:], in1=rt.t[:, 0, :], op=ALU.mult), reads=[p1, rt], writes=[t1])
                        S.op("dve", lambda e, p2=p2, t2=t2: e.tensor_tensor(out=t2.t[:], in0=p2.t[:], in1=rt.t[:, 1, :], op=ALU.mult), reads=[p2, rt], writes=[t2])
                        S.op("pool", lambda e, t1=t1, t2=t2, fo=fo: e.tensor_tensor(out=fo.t[:], in0=t1.t[:], in1=t2.t[:], op=ALU.add), reads=[t1, t2], writes=[fo])
                        S.dma("pool", dst[h * 128:(h + 1) * 128, t0:t0 + 512], fo.t[:], reads=[fo], key=fo.name)
                for tt in range(4):
                    r0 = t0 + tt * 128
                    for which in range(2):
                        pv = pv_r.next()
                        oc = 1024 if which == 0 else 2560
                        for k in range(8):
                            S.op("pe", lambda e, k=k, pv=pv, oc=oc: e.matmul(pv.t[:], lhsT=nT.t[:, k, tt * 128:(tt + 1) * 128], rhs=W.t[:, k, oc:oc + 512],
                                                                              start=(k == 0), stop=(k == 7)), reads=[Wp[oc], nT], writes=[pv])
                        if which == 0:
                            vt = va_r.next()
                            S.op("act", lambda e, pv=pv, vt=vt: e.activation(out=vt.t[:, :, 0:64], in_=pv.t[:].rearrange("p (h d) -> p h d", d=64), func=AF.Copy),
                                 reads=[pv], writes=[vt])
                            S.dma("pool", va[r0:r0 + 128, :], vt.t[:].rearrange("p h d -> p (h d)"), reads=[vt], key=vt.name)
                        else:
                            vt = vb_r.next()
                            S.op("act", lambda e, pv=pv, vt=vt: e.activation(out=vt.t[:, :, 0:128], in_=pv.t[:].rearrange("p (h d) -> p h d", d=128), func=AF.Copy),
                                 reads=[pv], writes=[vt])
                            S.dma("pool", vb[r0:r0 + 128, :], vt.t[:].rearrange("p h d -> p (h d)"), reads=[vt], key=vt.name)
            S.flush()

        with ExitStack() as st:
            Tf = sb(st, "N_Tf", [128, NAH, 24, 64], BF16)
            rstg = ring(st, "N_rs", 2, [128, 15, 64], F32)
            rm_int = sb(st, "N_rmi", [128, 8, 512], BF16)
            rm_r = ring(st, "N_rm", 2, [128, 8, 512], BF16)
            kT_r = ring(st, "N_kT", 2, [128, 4, 1024], BF16)
            qT_r = ring(st, "N_qT", 2, [128, 4, 512], BF16)
            v_r = ring(st, "N_v", 2, [128, 8, 520], BF16)
            s_r = ring(st, "N_s", 6, [128, 512], F32, psum=True)
            e_r = ring(st, "N_e", 3, [128, 1024], BF16)
            p_r = ring(st, "N_p", 3, [128, 1024], BF16)
            TMi = sb(st, "N_TMi", [128, 8, NAH, 512], BF16)
            acc_r = ring(st, "N_acc", 2, [128, 512], F32, psum=True)
            av = lambda b_: b_.t[:, 0:260].rearrange("p (q c) -> p q c", c=65)
            rc_r = ring(st, "N_rc", 2, [128, 4], F32)
            ya_r = ring(st, "N_ya", 2, [128, 4, 512], BF16)

            S.op("dve", lambda e: e.memset(Tf.t[:], 0.0), writes=[Tf])
            for h in range(NAH):
                for a in range(2):
                    rs = rstg.next()
                    S.dma("sp", rs.t[a * 64:(a + 1) * 64], rpbt[l, h].rearrange("r k q -> k r q"), writes=[rs], key=rs.name)
                    S.op("act", lambda e, rs=rs, h=h, a=a: e.activation(out=Tf.t[a * 64:(a + 1) * 64, h, 4 + a:19 + a, :], in_=rs.t[a * 64:(a + 1) * 64],
                                                                         func=AF.Exp), reads=[rs], writes=[Tf])
            S.dma("sp", rm_int.t[:], rowmask[0].rearrange("j p q -> p j q"), writes=[rm_int], key="N_rmi")
            for j in range(8):
                for h in range(NAH):
                    s0 = 15 - 2 * j
                    S.op("dve", lambda e: e.tensor_tensor(out=TMi.t[:, j, h, :].rearrange("p (b q) -> p b q", q=64), in0=Tf.t[:, h, s0:s0 + 8, :],
                                                          in1=rm_int.t[:, j, :].rearrange("p (b q) -> p b q", q=64), op=ALU.mult), reads=[Tf, rm_int], writes=[TMi])

            for g in range(NG):
                jl = [j for j in range(8) if 0 <= 8 * g - 4 + 2 * j and 8 * g - 4 + 2 * j + 1 < ROWS]
                jlo, jhi = jl[0], jl[-1] + 1
                k0 = (8 * g - 4 + 2 * jlo) * 64
                nk = (jhi - jlo) * 128
                kT = kT_r.next()
                qT = qT_r.next()
                vg = v_r.next()
                S.dma("sp", kT.t[:, :, jlo * 128:jhi * 128], kaT.rearrange("(hp p) t -> p hp t", p=128)[:, :, k0:k0 + nk], writes=[kT], key=kT.name)
                S.dma("sp", qT.t[:], qaT.rearrange("(hp p) t -> p hp t", p=128)[:, :, g * 512:(g + 1) * 512], writes=[qT], key=qT.name)
                S.dma("sp", vg.t[:, jlo:jhi, :], va[k0:k0 + nk, :].rearrange("(j p) c -> p j c", p=128), writes=[vg], key=vg.name)
                if g in SG:
                    rm = rm_r.next()
                    S.dma("sp", rm.t[:], rowmask[1 + SG.index(g)].rearrange("j p q -> p j q"), writes=[rm], key=rm.name)
                else:
                    rm = rm_int
                ya = ya_r.next()
                def n_scores(hp, j):
                    sps = []
                    for a2 in range(2):
                        sp_ = s_r.next()
                        lo = a2 * 64
                        S.op("pe", lambda e: e.matmul(sp_.t[:], lhsT=kT.t[lo:lo + 64, hp, j * 128:(j + 1) * 128], rhs=qT.t[lo:lo + 64, hp, :],
                                                      start=True, stop=True), reads=[kT, qT], writes=[sp_])
                        sps.append(sp_)
                    return sps

                units = [(hp, ji, j) for hp in range(4) for ji, j in enumerate(jl)]
                spq = [n_scores(units[0][0], units[0][2])]
                if len(units) > 1:
                    spq.append(n_scores(units[1][0], units[1][2]))
                accs = None
                for ui, (hp, ji, j) in enumerate(units):
                    if ji == 0:
                        accs = [acc_r.next(), acc_r.next()]
                    sps = spq.pop(0)
                    if ui + 2 < len(units):
                        spq.append(n_scores(units[ui + 2][0], units[ui + 2][2]))
                    et = e_r.next()
                    pt = p_r.next()
                    s0 = 15 - 2 * j
                    for a2 in range(2):
                        sp_ = sps[a2]
                        S.op("act", lambda e: e.activation(out=et.t[:, a2 * 512:(a2 + 1) * 512], in_=sp_.t[:], func=AF.Exp, scale=0.125), reads=[sp_], writes=[et])
                    if rm is rm_int:
                        S.op("dve", lambda e: e.tensor_tensor(out=pt.t[:], in0=et.t[:], in1=TMi.t[:, j, 2 * hp:2 * hp + 2, :].rearrange("p h q -> p (h q)"), op=ALU.mult),
                             reads=[et, TMi], writes=[pt])
                    else:
                        for a2 in range(2):
                            hh = 2 * hp + a2
                            ev = et.t[:, a2 * 512:(a2 + 1) * 512]
                            S.op("dve", lambda e: e.tensor_tensor(out=ev.rearrange("p (b q) -> p b q", q=64), in0=ev.rearrange("p (b q) -> p b q", q=64),
                                                                  in1=Tf.t[:, hh, s0:s0 + 8, :], op=ALU.mult), reads=[et, Tf], writes=[et])
                            S.op("dve", lambda e: e.tensor_tensor(out=pt.t[:, a2 * 512:(a2 + 1) * 512], in0=ev, in1=rm.t[:, j, :], op=ALU.mult), reads=[et, rm], writes=[pt])
                    for a2 in range(2):
                        hh = 2 * hp + a2
                        for qt in range(4):
                            S.op("pe", lambda e: e.matmul(av(accs[a2])[:, qt, :], lhsT=pt.t[:, a2 * 512 + qt * 128:a2 * 512 + (qt + 1) * 128],
                                                          rhs=vg.t[:, j, hh * 65:(hh + 1) * 65], start=(ji == 0 and qt == 0), stop=(ji == len(jl) - 1)),
                                 reads=[pt, vg], writes=[accs[a2]])
                    if ji == len(jl) - 1:
                        for a2 in range(2):
                            hh = 2 * hp + a2
                            rc = rc_r.next()
                            S.op("dve", lambda e: e.reciprocal(out=rc.t[:, 0:4].rearrange("p (q o) -> p q o", o=1), in_=av(accs[a2])[:, :, 64:65]), reads=[accs[a2]], writes=[rc])
                            for qt in range(4):
                                S.op("dve", lambda e: e.tensor_scalar(out=ya.t[:, qt, hh * 64:(hh + 1) * 64], in0=av(accs[a2])[:, qt, 0:64], scalar1=rc.t[:, qt:qt + 1],
                                                                      scalar2=None, op0=ALU.mult), reads=[accs[a2], rc], writes=[ya])
                S.dma("pool", ycat[g * 512:(g + 1) * 512, 0:512].rearrange("(qt p) c -> p qt c", p=128), ya.t[:], reads=[ya], key=ya.name)
            S.flush()

        with ExitStack() as st:
            kT_r = ring(st, "F_kT", 2, [128, T], BF16)
            v_r = ring(st, "F_v", 1, [128, NT, 129], BF16)
            vm_r = ring(st, "F_vm", 2, [128, 2, NT, 129], BF16)
            qT_r = ring(st, "F_qT", 2, [128, 512], BF16)
            s_r = ring(st, "F_s", 4, [128, 512], F32, psum=True)
            p_r = ring(st, "F_p", 6, [128, 512], BF16)
            acc_r = ring(st, "F_acc", 4, [128, 512], F32, psum=True)
            fv = lambda b_: b_.t[:, 0:258].rearrange("p (a c) -> p a c", c=129)
            rc_r = ring(st, "F_rc", 2, [128, 8], F32)
            t_r = ring(st, "F_t", 2, [128, 128], F32)
            yv_r = ring(st, "F_yv", 2, [128, 128], F32)
            jk_r = ring(st, "F_jk", 1, [128, 128], F32)
            yb_r = ring(st, "F_yb", 2, [128, 4, 128], BF16)
            gs = sb(st, "F_gs", [128, 128], F32)
            S.dma("sp", gs.t[:], subln[l], writes=[gs], key="F_gs")
            S.op("dve", lambda e: e.tensor_scalar(out=gs.t[:], in0=gs.t[:], scalar1=(1.0 - lam_init), scalar2=None, op0=ALU.mult), reads=[gs], writes=[gs])
            for h in range(4):
                kT = kT_r.next()
                vh = v_r.next()
                S.dma("sp", kT.t[:], kbT[h * 128:(h + 1) * 128, :], writes=[kT], key=kT.name)
                S.dma("sp", vh.t[:], vb[:, h * 129:(h + 1) * 129].rearrange("(j p) c -> p j c", p=128), writes=[vh], key=vh.name)
                vm = vm_r.next()
                for q2 in range(2):
                    S.op("dve", lambda e: e.tensor_tensor(out=vm.t[:, q2], in0=vh.t[:], in1=dmask_t.t[:, q2 * NT:(q2 + 1) * NT].unsqueeze(2).to_broadcast([128, NT, 129]),
                                                          op=ALU.mult), reads=[vh, dmask_t], writes=[vm])

                def f_scores(qT, kt):
                    sps = []
                    for a2 in range(2):
                        lo = a2 * 64
                        sp_ = s_r.next()
                        S.op("pe", lambda e: e.matmul(sp_.t[:], lhsT=kT.t[lo:lo + 64, kt * 128:(kt + 1) * 128], rhs=qT.t[lo:lo + 64, :], start=True, stop=True),
                             reads=[kT, qT], writes=[sp_])
                        sps.append(sp_)
                    return sps

                for qb in range(NB):
                    qT = qT_r.next()
                    S.dma("sp", qT.t[:], qbT[h * 128:(h + 1) * 128, qb * 512:(qb + 1) * 512], writes=[qT], key=qT.name)
                    accs = [acc_r.next() for _ in range(4)]
                    qhalf = 0 if qb < NB // 2 else 1
                    sps_next = f_scores(qT, 0)
                    for kt in range(NT):
                        sps = sps_next
                        if kt + 1 < NT:
                            sps_next = f_scores(qT, kt + 1)
                        pts = []
                        for a2 in range(2):
                            sp_ = sps[a2]
                            pt = p_r.next()
                            S.op("act", lambda e: e.activation(out=pt.t[:], in_=sp_.t[:], func=AF.Exp, scale=0.125), reads=sps, writes=[pt])
                            pts.append(pt)
                        for a2 in range(2):
                            pt = pts[a2]
                            for qt in range(4):
                                S.op("pe", lambda e: e.matmul(fv(accs[qt])[:, a2, :], lhsT=pt.t[:, qt * 128:(qt + 1) * 128], rhs=vm.t[:, qhalf, kt, :],
                                                              start=(kt == 0 and a2 == 0), stop=(kt == NT - 1)), reads=[pt, vm], writes=[accs[qt]])
                    yb = yb_r.next()
                    for qt in range(4):
                        ac = accs[qt]
                        rc = rc_r.next()
                        tt_ = t_r.next()
                        yv = yv_r.next()
                        jk = jk_r.next()
                        S.op("dve", lambda e, ac=ac, rc=rc: e.reciprocal(out=rc.t[:, 0:2].rearrange("p (q o) -> p q o", o=1), in_=fv(ac)[:, :, 128:129]), reads=[ac], writes=[rc])
                        S.op("dve", lambda e, rc=rc: e.tensor_tensor(out=rc.t[:, 2:3], in0=rc.t[:, 1:2], in1=lam_t.t[:, 0:1], op=ALU.mult), reads=[rc, lam_t], writes=[rc])
                        S.op("dve", lambda e, ac=ac, rc=rc, tt_=tt_: e.tensor_scalar(out=tt_.t[:], in0=fv(ac)[:, 1, 0:128], scalar1=rc.t[:, 2:3], scalar2=None, op0=ALU.mult),
                             reads=[ac, rc], writes=[tt_])
                        S.op("dve", lambda e, ac=ac, rc=rc, tt_=tt_, yv=yv: e.scalar_tensor_tensor(out=yv.t[:], in0=fv(ac)[:, 0, 0:128], scalar=rc.t[:, 0:1], in1=tt_.t[:],
                                                                                                    op0=ALU.mult, op1=ALU.add), reads=[ac, rc, tt_], writes=[yv])
                        S.op("dve", lambda e, yv=yv, jk=jk, rc=rc: e.scalar_tensor_tensor(out=jk.t[:], in0=yv.t[:], scalar=1.0, in1=yv.t[:], op0=ALU.mult, op1=ALU.mult,
                                                                                           accum_out=rc.t[:, 3:4]), reads=[yv], writes=[jk, rc])
                        S.op("dve", lambda e, rc=rc: e.tensor_scalar(out=rc.t[:, 4:5], in0=rc.t[:, 3:4], scalar1=1.0 / 128, scalar2=EPS, op0=ALU.mult, op1=ALU.add), reads=[rc], writes=[rc])
                        S.op("act", lambda e, rc=rc: e.activation(out=rc.t[:, 6:7], in_=rc.t[:, 4:5], func=AF.Ln), reads=[rc], writes=[rc])
                        S.op("act", lambda e, rc=rc: e.activation(out=rc.t[:, 5:6], in_=rc.t[:, 6:7], func=AF.Exp, scale=-0.5), reads=[rc], writes=[rc])
                        S.op("dve", lambda e, yv=yv, rc=rc, qt=qt: e.scalar_tensor_tensor(out=yb.t[:, qt, :], in0=yv.t[:], scalar=rc.t[:, 5:6], in1=gs.t[:], op0=ALU.mult, op1=ALU.mult),
                             reads=[yv, rc, gs], writes=[yb])
                    S.dma("pool", ycat[qb * 512:(qb + 1) * 512, 512 + h * 128:512 + (h + 1) * 128].rearrange("(qt p) c -> p qt c", p=128), yb.t[:], reads=[yb], key=yb.name)
            S.flush()

        with ExitStack() as st:
            W = sb(st, "D_W", [128, 8, D], BF16)
            stg_r = ring(st, "D_stg", 2, [128, 1024], F32)
            g_t = sb(st, "D_g", [128, D], F32)
            y_r = ring(st, "D_y", 3, [128, D], BF16)
            x_r = ring(st, "D_x", 4, [128, D], F32)
            xm_r = ring(st, "D_xm", 4, [128, D], F32)
            yT_r = ring(st, "D_yT", 3, [128, 8, 128], BF16)
            tpa_r = ring(st, "D_tpa", 2, [128, 8, 128], BF16, psum=True)
            tpb_r = ring(st, "D_tpb", 2, [128, 8, 128], BF16, psum=True)
            po_r = ring(st, "D_po", 4, [128, 512], F32, psum=True)
            junk_r = ring(st, "D_jk", 2, [128, D], BF16)
            ss_r = ring(st, "D_ss", 4, [128, 4], F32)
            nb_r = ring(st, "D_nb", 3, [128, D], BF16)
            nT_r = ring(st, "D_nT", 3, [128, 8, 128], BF16)
            S.dma("sp", g_t.t[:], g_ffn[l], writes=[g_t], key="D_g")
            load_w(stg_r, W, 8, D, w_out[l])
            ctx = {}

            def d0(i):
                c = ctx[i] = {}
                r0 = i * 128
                c["yt"] = yt = y_r.next()
                c["xt"] = xt = x_r.next()
                S.dma("sp", yt.t[:], ycat[r0:r0 + 128, :], writes=[yt], key=yt.name)
                S.dma("sp", xt.t[:], x_src[r0:r0 + 128, :], writes=[xt], key=xt.name)

            def d1(i):
                c = ctx[i]
                yt = c["yt"]
                tp = tpa_r.next()
                for cc in range(8):
                    S.op("pe", lambda e: e.transpose(out=tp.t[:, cc, :], in_=yt.t[:, cc * 128:(cc + 1) * 128], identity=ident.t[:]), reads=[yt, ident], writes=[tp])
                c["yT"] = yT = yT_r.next()
                S.op("act", lambda e: e.activation(out=yT.t[:], in_=tp.t[:], func=AF.Copy), reads=[tp], writes=[yT])

            def d2(i):
                c = ctx[i]
                yT, xt = c["yT"], c["xt"]
                r0 = i * 128
                c["xm"] = xm = xm_r.next()
                for half in range(2):
                    po = po_r.next()
                    for k in range(8):
                        S.op("pe", lambda e: e.matmul(po.t[:], lhsT=yT.t[:, k, :], rhs=W.t[:, k, half * 512:(half + 1) * 512], start=(k == 0), stop=(k == 7)),
                             reads=[yT, W], writes=[po])
                    S.op("dve", lambda e: e.tensor_tensor(out=xm.t[:, half * 512:(half + 1) * 512], in0=po.t[:], in1=xt.t[:, half * 512:(half + 1) * 512], op=ALU.add),
                         reads=[po, xt], writes=[xm])
                S.dma("pool", xmid[r0:r0 + 128, :], xm.t[:], reads=[xm], key=xm.name)
                jk = junk_r.next()
                c["ss"] = ss = ss_r.next()
                S.op("dve", lambda e: e.scalar_tensor_tensor(out=jk.t[:], in0=xm.t[:], scalar=1.0, in1=xm.t[:], op0=ALU.mult, op1=ALU.mult,
                                                             accum_out=ss.t[:, 0:1]), reads=[xm], writes=[jk, ss])
                S.op("dve", lambda e: e.tensor_scalar(out=ss.t[:, 1:2], in0=ss.t[:, 0:1], scalar1=1.0 / D, scalar2=EPS, op0=ALU.mult, op1=ALU.add),
                     reads=[ss], writes=[ss])

            def d3(i):
                ss = ctx[i]["ss"]
                S.op("act", lambda e: e.activation(out=ss.t[:, 3:4], in_=ss.t[:, 1:2], func=AF.Ln), reads=[ss], writes=[ss])
                S.op("act", lambda e: e.activation(out=ss.t[:, 2:3], in_=ss.t[:, 3:4], func=AF.Exp, scale=-0.5), reads=[ss], writes=[ss])

            def d4a(i):
                c = ctx[i]
                xm, ss = c["xm"], c["ss"]
                c["nb"] = nb = nb_r.next()
                S.op("dve", lambda e: e.scalar_tensor_tensor(out=nb.t[:], in0=xm.t[:], scalar=ss.t[:, 2:3], in1=g_t.t[:], op0=ALU.mult, op1=ALU.mult),
                     reads=[xm, ss, g_t], writes=[nb])

            def d4b(i):
                c = ctx[i]
                nb = c["nb"]
                c["tp2"] = tp = tpb_r.next()
                for cc in range(8):
                    S.op("pe", lambda e: e.transpose(out=tp.t[:, cc, :], in_=nb.t[:, cc * 128:(cc + 1) * 128], identity=ident.t[:]), reads=[nb, ident], writes=[tp])

            def d5(i):
                c = ctx[i]
                tp = c["tp2"]
                r0 = i * 128
                nT = nT_r.next()
                S.op("act", lambda e: e.activation(out=nT.t[:], in_=tp.t[:], func=AF.Copy), reads=[tp], writes=[nT])
                S.dma("pool", n2T_v[:, :, 1 + r0:1 + r0 + 128], nT.t[:], reads=[nT], key=nT.name)
                del ctx[i]

            stages = [d0, d1, d2, d3, d4a, d4b, d5]
            lag = [0, 1, 2, 3, 4, 4, 5]
            order = [4, 6, 3, 2, 1, 5, 0]
            for tstep in range(NT + 6):
                for s_ in order:
                    i = tstep - lag[s_]
                    if 0 <= i < NT:
                        stages[s_](i)
            S.flush()

        last = (l == NL - 1)
        with ExitStack() as st:
            Wu = sb(st, "E_Wu", [128, 8, 2 * DFF], BF16)
            Wd = sb(st, "E_Wd", [128, NFC, D], BF16)
            stg_r = ring(st, "E_stg", 2, [128, 512], F32)
            cw = sb(st, "E_cw", [128, 4 * NFC], F32)
            nT_r = ring(st, "E_nT", 1, [128, 8, 514], BF16)
            hT = sb(st, "E_hT", [128, NFC, 512], BF16)
            pg_r = ring(st, "E_pg", 2, [128, 512], F32, psum=True)
            pvv_r = ring(st, "E_pvv", 2, [128, 512], F32, psum=True)
            ph_r = ring(st, "E_ph", 2, [128, 2], F32, psum=True)
            pd_r = ring(st, "E_pd", 2, [128, 512], F32, psum=True)
            gx_r = ring(st, "E_gx", 2, [128, 514], F32)
            a_r = ring(st, "E_a", 2, [128, 512], F32)
            gl_r = ring(st, "E_gl", 2, [128, 512], F32)
            xm_r = ring(st, "E_xm", 2, [128, D], F32)
            xo_r = ring(st, "E_xo", 1, [128, D], F32)
            S.dma("sp", cw.t[:], convw[l], writes=[cw], key="E_cw")
            if last:
                g_t = sb(st, "E_g", [128, D], F32)
                jk_r = ring(st, "E_jk", 1, [128, D], BF16)
                ss_r = ring(st, "E_ss", 2, [128, 4], F32)
                yo_r = ring(st, "E_yo", 1, [128, D], F32)
                S.dma("sp", g_t.t[:], g_final[:, :], writes=[g_t], key="E_g")
            Wup = load_w(stg_r, Wu, 8, 2 * DFF, w_up[l], CH=512, order=[0, 2560, 3072, 512, 3584, 1024, 4096, 1536, 4608, 2048, 5120])
            load_w(stg_r, Wd, NFC, D, w_down[l], CH=512)
            for blk in range(NB):
                t0 = blk * 512
                nT = nT_r.next()
                S.dma("sp", nT.t[:], n2T_v[:, :, t0:t0 + 514], writes=[nT], key=nT.name)
                if NB >= 2 and blk == NB // 2:
                    S.op("dve", lambda e, nT=nT: e.tensor_scalar(out=nT.t[:, :, 0:1], in0=nT.t[:, :, 0:1], scalar1=cmask_t.t[:, 0:1], scalar2=None, op0=ALU.mult), reads=[nT, cmask_t], writes=[nT])
                if NB >= 2 and blk == NB // 2 - 1:
                    S.op("dve", lambda e, nT=nT: e.tensor_scalar(out=nT.t[:, :, 513:514], in0=nT.t[:, :, 513:514], scalar1=cmask_t.t[:, 1:2], scalar2=None, op0=ALU.mult), reads=[nT, cmask_t], writes=[nT])
                for fc in range(NFC):
                    pg = pg_r.next()
                    ph = ph_r.next()
                    pvv = pvv_r.next()
                    for k in range(8):
                        S.op("pe", lambda e, k=k, pg=pg, fc=fc, nT=nT: e.matmul(pg.t[:], lhsT=Wu.t[:, k, fc * 128:(fc + 1) * 128], rhs=nT.t[:, k, 1:513], start=(k == 0), stop=(k == 7)),
                             reads=[Wup[((fc * 128) // 512) * 512], nT], writes=[pg])
                    for k in range(8):
                        S.op("pe", lambda e, k=k, ph=ph, fc=fc, nT=nT: e.matmul(ph.t[:], lhsT=Wu.t[:, k, fc * 128:(fc + 1) * 128], rhs=nT.t[:, k, 0:514:513], start=(k == 0), stop=(k == 7)),
                             reads=[Wup[((fc * 128) // 512) * 512], nT], writes=[ph])
                    for k in range(8):
                        S.op("pe", lambda e, k=k, pvv=pvv, fc=fc, nT=nT: e.matmul(pvv.t[:], lhsT=Wu.t[:, k, DFF + fc * 128:DFF + (fc + 1) * 128], rhs=nT.t[:, k, 1:513], start=(k == 0), stop=(k == 7)),
                             reads=[Wup[((DFF + fc * 128) // 512) * 512], nT], writes=[pvv])
                    gx = gx_r.next()
                    S.op("act", lambda e, gx=gx, pg=pg: e.activation(out=gx.t[:, 1:513], in_=pg.t[:], func=AF.Copy), reads=[pg], writes=[gx])
                    S.op("act", lambda e, gx=gx, ph=ph: e.activation(out=gx.t[:, 0:514:513], in_=ph.t[:], func=AF.Copy), reads=[ph], writes=[gx])
                    a = a_r.next()
                    S.op("dve", lambda e, a=a, gx=gx, fc=fc: e.tensor_scalar(out=a.t[:], in0=gx.t[:, 1:513], scalar1=cw.t[:, NFC + fc:NFC + fc + 1], scalar2=cw.t[:, 3 * NFC + fc:3 * NFC + fc + 1],
                                                                          op0=ALU.mult, op1=ALU.add), reads=[gx, cw], writes=[a])
                    S.op("dve", lambda e, a=a, gx=gx, fc=fc: e.scalar_tensor_tensor(out=a.t[:], in0=gx.t[:, 0:512], scalar=cw.t[:, fc:fc + 1], in1=a.t[:], op0=ALU.mult, op1=ALU.add),
                         reads=[gx, cw, a], writes=[a])
                    S.op("dve", lambda e, a=a, gx=gx, fc=fc: e.scalar_tensor_tensor(out=a.t[:], in0=gx.t[:, 2:514], scalar=cw.t[:, 2 * NFC + fc:2 * NFC + fc + 1], in1=a.t[:], op0=ALU.mult, op1=ALU.add),
                         reads=[gx, cw, a], writes=[a])
                    gl = gl_r.next()
                    S.op("act", lambda e, a=a, gl=gl: e.activation(out=gl.t[:], in_=a.t[:], func=AF.Gelu), reads=[a], writes=[gl])
                    S.op("dve", lambda e, gl=gl, pvv=pvv, fc=fc: e.tensor_tensor(out=hT.t[:, fc, :], in0=pvv.t[:], in1=gl.t[:], op=ALU.mult), reads=[gl, pvv], writes=[hT])
                for tt in range(4):
                    r0 = t0 + tt * 128
                    xm = xm_r.next()
                    S.dma("sp", xm.t[:], xmid[r0:r0 + 128, :], writes=[xm], key=xm.name)
                    xo = xo_r.next()
                    for half in range(2):
                        pd = pd_r.next()
                        for fc in range(NFC):
                            S.op("pe", lambda e, fc=fc, pd=pd, tt=tt, half=half: e.matmul(pd.t[:], lhsT=hT.t[:, fc, tt * 128:(tt + 1) * 128], rhs=Wd.t[:, fc, half * 512:(half + 1) * 512],
                                                                                       start=(fc == 0), stop=(fc == NFC - 1)), reads=[hT, Wd], writes=[pd])
                        S.op("dve", lambda e, pd=pd, xm=xm, xo=xo, half=half: e.tensor_tensor(out=xo.t[:, half * 512:(half + 1) * 512], in0=pd.t[:], in1=xm.t[:, half * 512:(half + 1) * 512], op=ALU.add),
                             reads=[pd, xm], writes=[xo])
                    if not last:
                        S.dma("pool", xres[r0:r0 + 128, :], xo.t[:], reads=[xo], key=xo.name)
                    else:
                        jk = jk_r.next()
                        ss = ss_r.next()
                        yo = yo_r.next()
                        S.op("dve", lambda e, jk=jk, ss=ss, xo=xo: e.scalar_tensor_tensor(out=jk.t[:], in0=xo.t[:], scalar=1.0, in1=xo.t[:], op0=ALU.mult, op1=ALU.mult, accum_out=ss.t[:, 0:1]),
                             reads=[xo], writes=[jk, ss])
                        S.op("dve", lambda e, ss=ss: e.tensor_scalar(out=ss.t[:, 1:2], in0=ss.t[:, 0:1], scalar1=1.0 / D, scalar2=EPS, op0=ALU.mult, op1=ALU.add), reads=[ss], writes=[ss])
                        S.op("act", lambda e, ss=ss: e.activation(out=ss.t[:, 3:4], in_=ss.t[:, 1:2], func=AF.Ln), reads=[ss], writes=[ss])
                        S.op("act", lambda e, ss=ss: e.activation(out=ss.t[:, 2:3], in_=ss.t[:, 3:4], func=AF.Exp, scale=-0.5), reads=[ss], writes=[ss])
                        S.op("dve", lambda e, ss=ss, xo=xo, yo=yo: e.scalar_tensor_tensor(out=yo.t[:], in0=xo.t[:], scalar=ss.t[:, 2:3], in1=g_t.t[:], op0=ALU.mult, op1=ALU.mult),
                             reads=[xo, ss, g_t], writes=[yo])
                        S.dma("pool", y_out[r0:r0 + 128, :], yo.t[:], reads=[yo], key=yo.name)
            S.flush()
    es.close()
    return nc


def _rowmask_tables(T, seq_len):
    ROWS = T // 64
    NG = T // 512
    SG = sorted(set([0, NG // 2 - 1, NG // 2, NG - 1]))
    R = seq_len // 64

    def table(g):
        m = np.zeros((8, 128, 512), np.float32)
        for j in range(8):
            for a in range(2):
                kr = 8 * g - 4 + 2 * j + a
                if kr < 0 or kr >= ROWS:
                    continue
                for b in range(8):
                    qr = 8 * g + b
                    if kr // R != qr // R:
                        continue
                    qs = qr % R
                    start = min(max(qs - 4, 0), R - 8)
                    ks = kr % R
                    if start <= ks < start + 8:
                        m[j, a * 64:(a + 1) * 64, b * 64:(b + 1) * 64] = 1.0
        return m
    mi = np.zeros((8, 128, 512), np.float32)
    for j in range(8):
        for a in range(2):
            for b in range(8):
                dr = 2 * j + a - b - 4
                if -4 <= dr <= 3:
                    mi[j, a * 64:(a + 1) * 64, b * 64:(b + 1) * 64] = 1.0
    tabs = [mi] + [table(g) for g in SG]
    return np.stack(tabs).astype(ml_dtypes.bfloat16)


def _rot_tables(T, seq_len):
    pos = (np.arange(T) % seq_len).astype(np.float32)
    inv = (1.0 / (10000.0 ** (np.arange(0, 64, 2, dtype=np.float32) / 64.0))).astype(np.float32)
    ang = pos[None, :] * inv[:, None]
    cos = np.cos(ang).astype(np.float32)
    sin = np.sin(ang).astype(np.float32)
    cos64 = np.concatenate([cos, cos], 0)
    sin64 = np.concatenate([-sin, sin], 0)
    return np.stack([np.concatenate([cos64, cos64], 0), np.concatenate([sin64, sin64], 0)]).astype(np.float32)


def _prep_shared(inp, NL):
    w_in = np.asarray(inp["w_in"], np.float32)
    perm = np.arange(1024).reshape(8, 2, 64)
    perm = np.concatenate([perm[..., 32:], perm[..., :32]], -1).reshape(-1)
    qb = w_in[:, :, 1536:2560]
    w_ext = np.concatenate([w_in, qb[:, :, perm]], axis=2)
    rep = lambda a: np.ascontiguousarray(np.broadcast_to(np.asarray(a, np.float32)[:, None, :], (a.shape[0], 128, a.shape[1])))
    lamv = np.concatenate([np.asarray(inp[k], np.float32) for k in ("lam_q1", "lam_k1", "lam_q2", "lam_k2")], axis=1)
    cw = np.concatenate([np.asarray(inp["conv_w"], np.float32), np.asarray(inp["conv_b"], np.float32)[:, None, :]], axis=1)
    cw = cw.reshape(NL, 4, NFC, 128).transpose(0, 3, 1, 2).reshape(NL, 128, 4 * NFC)
    rpb = np.asarray(inp["rpb"], np.float32)
    kc = np.arange(64)[:, None]
    qc = np.arange(64)[None, :]
    dc = kc - qc + 15
    cs = np.clip(qc - 8, 0, 48)
    ok = (kc >= cs) & (kc < cs + 16)
    dcc = np.clip(dc, 0, 30)
    rp = rpb[:, :, ::-1, :][:, :, :, dcc]
    rp = np.where(ok[None, None, None], rp, np.float32(NEG)).astype(np.float32)
    return dict(
        w_in=np.ascontiguousarray(w_ext), w_out=np.asarray(inp["w_out"], np.float32), w_up=np.asarray(inp["w_up"], np.float32),
        w_down=np.asarray(inp["w_down"], np.float32), g_attn=rep(inp["g_attn"]), g_ffn=rep(inp["g_ffn"]),
        g_final=np.ascontiguousarray(np.broadcast_to(np.asarray(inp["g_final"], np.float32)[None, :], (128, D))),
        subln=rep(inp["subln_g"]), lamv=rep(lamv), convw=np.ascontiguousarray(cw), rpbt=np.ascontiguousarray(rp),
        ident=np.eye(128, dtype=np.float32).astype(ml_dtypes.bfloat16),
    )


def _prep_core(T, seq_len):
    NT = T // 128
    dm = np.ones((2, NT), np.float32)
    if seq_len < T:
        dm[0, NT // 2:] = 0.0
        dm[1, :NT // 2] = 0.0
    dmask = np.ascontiguousarray(np.broadcast_to(dm.reshape(1, -1), (128, 2 * NT)))
    cm = np.full((128, 2), 1.0 if seq_len == T else 0.0, np.float32)
    return dict(rot=_rot_tables(T, seq_len), rowmask=_rowmask_tables(T, seq_len), dmask=dmask, cmask=cm)


_PROG_CACHE = {}


def run_cores(inp, xs, seqlens, T, NL, n_cores):
    key = (T, NL)
    if key not in _PROG_CACHE:
        _PROG_CACHE[key] = build_program(T, NL)
    nc = _PROG_CACHE[key]
    shared = _prep_shared(inp, NL)
    per_type = {}
    in_maps = []
    for x, sl in zip(xs, seqlens):
        if sl not in per_type:
            per_type[sl] = _prep_core(T, sl)
        m = dict(shared)
        m.update(per_type[sl])
        m["x"] = np.ascontiguousarray(x, dtype=np.float32)
        in_maps.append(m)
    res = run_bass_kernel_spmd(nc, in_maps, core_ids=list(range(n_cores)))
    return [r["y_out"] for r in res.results]


def kernel(x_prompt, x_sample, g_attn, w_in, rpb, lam_q1, lam_k1, lam_q2, lam_k2, subln_g, w_out,
           g_ffn, w_up, conv_w, conv_b, w_down, g_final):
    inp = dict(g_attn=g_attn, w_in=w_in, rpb=rpb, lam_q1=lam_q1, lam_k1=lam_k1, lam_q2=lam_q2, lam_k2=lam_k2,
               subln_g=subln_g, w_out=w_out, g_ffn=g_ffn, w_up=w_up, conv_w=conv_w, conv_b=conv_b, w_down=w_down, g_final=g_final)
    inp = {k: np.asarray(v) for k, v in inp.items()}
    xp = np.asarray(x_prompt, np.float32)
    xs_ = np.asarray(x_sample, np.float32)
    T = 8192
    xs = [xs_[0], xs_[1], xp[0:2].reshape(T, D), xp[2:4].reshape(T, D)]
    sl = [8192, 8192, 4096, 4096]
    outs = run_cores(inp, xs + xs, sl + sl, T, 4, 8)
    y_sample = np.stack([outs[0], outs[1]]).astype(np.float32)
    y_prompt = np.concatenate([outs[2].reshape(2, 4096, D), outs[3].reshape(2, 4096, D)], 0).astype(np.float32)
    return (y_prompt, y_sample)
```
